# Optimizing a Trainium2 kernel written in Bass

```python
import math
import jax
import jax.numpy as jnp
from jax import lax
import numpy as np

D_MODEL = 1024
BATCH = 4
SEQ = 4096
DEPTH = 1

D_MIX = D_MODEL
D_NSA = D_MIX // 2
D_CONV = D_MIX - D_NSA
HEAD_DIM = 64
N_HEADS = D_NSA // HEAD_DIM
N_KV = 2
REP = N_HEADS // N_KV
KV_W = N_KV * HEAD_DIM
N_GATES = 3 * N_HEADS
D_IN = D_NSA + 6 * KV_W + N_GATES + 3 * D_CONV
L_CMP = 32
S_CMP = 16
L_SLC = 64
N_SEL = 16
WINDOW = 512
Q_BLOCK = 128
CMP_HIDDEN = 256
CONV_W = 3
D_FF = 2816
N_BUCKETS = 32
MAX_DIST = 128
RMS_EPS = 1e-6
NEG = -1e30
FORCE_BONUS = 1e4

kernel_name = "hybrid_nsa_shortconv_convffn_block"


def rms_norm(x, g):
    xf = x.astype(jnp.float32)
    y = xf * lax.rsqrt(jnp.mean(xf * xf, axis=-1, keepdims=True) + RMS_EPS)
    return (y * g.astype(jnp.float32)).astype(x.dtype)


def causal_dwconv(x, w):
    return lax.conv_general_dilated(
        x, w[:, None, :].astype(x.dtype), window_strides=(1,),
        padding=[(CONV_W - 1, 0)], dimension_numbers=('NWC', 'WIO', 'NWC'),
        feature_group_count=x.shape[-1])


def t5_bucket(dist):
    max_exact = N_BUCKETS // 2
    d = jnp.maximum(dist, 0)
    df = jnp.maximum(d, 1).astype(jnp.float32)
    large = max_exact + (jnp.log(df / max_exact) / math.log(MAX_DIST / max_exact)
                         * (N_BUCKETS - max_exact)).astype(jnp.int32)
    return jnp.where(d < max_exact, d, jnp.minimum(large, N_BUCKETS - 1))


def masked_softmax(logits, mask):
    z = jnp.where(mask, logits.astype(jnp.float32), NEG)
    p = jax.nn.softmax(z, axis=-1)
    return p * jnp.any(mask, axis=-1, keepdims=True)


def compress_kv(kv, pe, w1, w2):
    B, S, G, dh = kv.shape
    n_cmp = (S - L_CMP) // S_CMP + 1
    idx = jnp.arange(n_cmp)[:, None] * S_CMP + jnp.arange(L_CMP)[None, :]
    blk = kv[:, idx] + pe[None, None, :, None, :]
    blk = blk.transpose(0, 3, 1, 2, 4).reshape(B, G, n_cmp, L_CMP * dh)
    return jax.nn.gelu(blk @ w1, approximate=True) @ w2


def nsa_attention(q, k_c, v_c, k_s, v_s, k_w, v_w, gate_logits,
                  pe_k, pe_v, wk1, wk2, wv1, wv2, rel_bias):
    B, S = q.shape[0], q.shape[1]
    n_cmp = (S - L_CMP) // S_CMP + 1
    n_slc = S // L_SLC
    n_sel = min(N_SEL, n_slc)
    nqb = S // Q_BLOCK
    scale = HEAD_DIM ** -0.5
    t = jnp.arange(S)
    tab = rel_bias.reshape(N_KV, REP, N_BUCKETS)
    qg = q.reshape(B, S, N_KV, REP, HEAD_DIM).transpose(0, 2, 3, 1, 4)

    kc = compress_kv(k_c, pe_k, wk1, wk2)
    vc = compress_kv(v_c, pe_v, wv1, wv2)
    c_start = jnp.arange(n_cmp) * S_CMP
    dist_c = t[:, None] - (c_start + L_CMP - 1)[None, :]
    logits_c = (jnp.einsum('bgrsd,bgnd->bgrsn', qg, kc) * scale
                + tab[:, :, t5_bucket(dist_c)])
    p_c = masked_softmax(logits_c, dist_c >= 0)
    o_c = jnp.einsum('bgrsn,bgnd->bgrsd', p_c.astype(vc.dtype), vc)

    s_start = jnp.arange(n_slc) * L_SLC
    overlap = jnp.clip(jnp.minimum(c_start[:, None] + L_CMP, s_start[None, :] + L_SLC)
                       - jnp.maximum(c_start[:, None], s_start[None, :]), 0, None)
    overlap = overlap.astype(jnp.float32) / L_CMP
    imp = jnp.einsum('bgrsn,nj->bgsj', p_c, overlap)
    j = jnp.arange(n_slc)[None, :]
    cur = (t // L_SLC)[:, None]
    forced = (j == 0) | (j == cur) | (j == cur - 1)
    score = jnp.where(j <= cur, imp + jnp.where(forced, FORCE_BONUS, 0.0), NEG)
    top_val, top_idx = lax.top_k(score, n_sel)
    top_ok = top_val > 0.5 * NEG

    k_sb = k_s.reshape(B, n_slc, L_SLC, N_KV, HEAD_DIM).transpose(0, 3, 1, 2, 4)
    v_sb = v_s.reshape(B, n_slc, L_SLC, N_KV, HEAD_DIM).transpose(0, 3, 1, 2, 4)
    k_wp = jnp.pad(k_w, ((0, 0), (WINDOW, 0), (0, 0), (0, 0)))
    v_wp = jnp.pad(v_w, ((0, 0), (WINDOW, 0), (0, 0), (0, 0)))
    bi = jnp.arange(B)[:, None, None, None]
    gi = jnp.arange(N_KV)[None, :, None, None]

    def block_fn(args):
        n, q_b, idx_b, ok_b = args
        t_b = n * Q_BLOCK + jnp.arange(Q_BLOCK)
        ks = k_sb[bi, gi, idx_b].reshape(B, N_KV, Q_BLOCK, n_sel * L_SLC, HEAD_DIM)
        vs = v_sb[bi, gi, idx_b].reshape(B, N_KV, Q_BLOCK, n_sel * L_SLC, HEAD_DIM)
        pos = (idx_b[..., None] * L_SLC + jnp.arange(L_SLC)).reshape(
            B, N_KV, Q_BLOCK, n_sel * L_SLC)
        dist_s = t_b[None, None, :, None] - pos
        m_s = (dist_s >= 0) & jnp.repeat(ok_b, L_SLC, axis=-1)
        bias_s = jnp.moveaxis(tab[gi, :, t5_bucket(dist_s)], -1, 2)
        logits_s = jnp.einsum('bgrqd,bgqkd->bgrqk', q_b, ks) * scale + bias_s
        p_s = masked_softmax(logits_s, m_s[:, :, None])
        o_s = jnp.einsum('bgrqk,bgqkd->bgrqd', p_s.astype(vs.dtype), vs)
        kw = lax.dynamic_slice_in_dim(k_wp, n * Q_BLOCK, Q_BLOCK + WINDOW, axis=1)
        vw = lax.dynamic_slice_in_dim(v_wp, n * Q_BLOCK, Q_BLOCK + WINDOW, axis=1)
        s_w = n * Q_BLOCK - WINDOW + jnp.arange(Q_BLOCK + WINDOW)
        dist_w = t_b[:, None] - s_w[None, :]
        m_w = (dist_w >= 0) & (dist_w < WINDOW) & (s_w[None, :] >= 0)
        logits_w = (jnp.einsum('bgrqd,bkgd->bgrqk', q_b, kw) * scale
                    + tab[:, :, t5_bucket(dist_w)])
        p_w = masked_softmax(logits_w, m_w)
        o_w = jnp.einsum('bgrqk,bkgd->bgrqd', p_w.astype(vw.dtype), vw)
        return o_s, o_w

    q_blocks = jnp.moveaxis(qg.reshape(B, N_KV, REP, nqb, Q_BLOCK, HEAD_DIM), 3, 0)
    idx_blocks = jnp.moveaxis(top_idx.reshape(B, N_KV, nqb, Q_BLOCK, n_sel), 2, 0)
    ok_blocks = jnp.moveaxis(top_ok.reshape(B, N_KV, nqb, Q_BLOCK, n_sel), 2, 0)
    o_s, o_w = lax.map(block_fn, (jnp.arange(nqb), q_blocks, idx_blocks, ok_blocks))
    o_s = jnp.moveaxis(o_s, 0, 3).reshape(B, N_KV, REP, S, HEAD_DIM)
    o_w = jnp.moveaxis(o_w, 0, 3).reshape(B, N_KV, REP, S, HEAD_DIM)

    g = jax.nn.sigmoid(gate_logits.astype(jnp.float32)).reshape(B, S, N_KV, REP, 3)
    g = g.transpose(0, 2, 3, 1, 4).astype(q.dtype)
    o = g[..., 0:1] * o_c + g[..., 1:2] * o_s + g[..., 2:3] * o_w
    return o.transpose(0, 3, 1, 2, 4).reshape(B, S, D_NSA)


def short_conv_mixer(b_gate, c_gate, x_t, conv_w):
    return b_gate * causal_dwconv(c_gate * x_t, conv_w)


def split_points():
    sizes = [D_NSA] + [KV_W] * 6 + [N_GATES, D_CONV, D_CONV]
    pts, acc = [], 0
    for s in sizes:
        acc += s
        pts.append(acc)
    return pts


def setup_inputs(seed: int = 0) -> dict:
    key = jax.random.key(seed)
    ks = jax.random.split(key, 20)

    def nrm(k, shape, scale):
        return jax.random.normal(k, shape, jnp.float32) * scale

    return {
        'x': nrm(ks[0], (BATCH, SEQ, D_MODEL), 1.0),
        'norm_mix_pre': 1.0 + nrm(ks[1], (DEPTH, D_MODEL), 0.05),
        'norm_mix_post': 1.0 + nrm(ks[2], (DEPTH, D_MODEL), 0.05),
        'norm_ffn_pre': 1.0 + nrm(ks[3], (DEPTH, D_MODEL), 0.05),
        'norm_ffn_post': 1.0 + nrm(ks[4], (DEPTH, D_MODEL), 0.05),
        'w_in': nrm(ks[5], (DEPTH, D_MODEL, D_IN), D_MODEL ** -0.5),
        'pe_cmp_k': nrm(ks[6], (DEPTH, L_CMP, HEAD_DIM), 0.5),
        'pe_cmp_v': nrm(ks[7], (DEPTH, L_CMP, HEAD_DIM), 0.5),
        'w_cmp_k1': nrm(ks[8], (DEPTH, L_CMP * HEAD_DIM, CMP_HIDDEN), (L_CMP * HEAD_DIM) ** -0.5),
        'w_cmp_k2': nrm(ks[9], (DEPTH, CMP_HIDDEN, HEAD_DIM), CMP_HIDDEN ** -0.5),
        'w_cmp_v1': nrm(ks[10], (DEPTH, L_CMP * HEAD_DIM, CMP_HIDDEN), (L_CMP * HEAD_DIM) ** -0.5),
        'w_cmp_v2': nrm(ks[11], (DEPTH, CMP_HIDDEN, HEAD_DIM), CMP_HIDDEN ** -0.5),
        'rel_bias': nrm(ks[12], (N_HEADS, N_BUCKETS), 0.5),
        'conv_mix_w': nrm(ks[13], (DEPTH, CONV_W, D_CONV), CONV_W ** -0.5),
        'w_out': nrm(ks[14], (DEPTH, D_MIX, D_MODEL), D_MIX ** -0.5),
        'w_ffn_up': nrm(ks[15], (DEPTH, D_MODEL, 2 * D_FF), D_MODEL ** -0.5),
        'ffn_conv_w': nrm(ks[16], (DEPTH, CONV_W, 2 * D_FF), CONV_W ** -0.5),
        'w_ffn_down': nrm(ks[17], (DEPTH, D_FF, D_MODEL), D_FF ** -0.5),
    }


def reference(x, norm_mix_pre, norm_mix_post, norm_ffn_pre, norm_ffn_post, w_in,
              pe_cmp_k, pe_cmp_v, w_cmp_k1, w_cmp_k2, w_cmp_v1, w_cmp_v2, rel_bias,
              conv_mix_w, w_out, w_ffn_up, ffn_conv_w, w_ffn_down):
    B, S, _ = x.shape
    for l in range(DEPTH):
        h = rms_norm(x, norm_mix_pre[l])
        u = h @ w_in[l]
        q, k_c, v_c, k_s, v_s, k_w, v_w, gl, b_g, c_g, x_t = jnp.split(
            u, split_points(), axis=-1)
        kv = lambda a: a.reshape(B, S, N_KV, HEAD_DIM)
        o_nsa = nsa_attention(q.reshape(B, S, N_HEADS, HEAD_DIM),
                              kv(k_c), kv(v_c), kv(k_s), kv(v_s), kv(k_w), kv(v_w), gl,
                              pe_cmp_k[l], pe_cmp_v[l], w_cmp_k1[l], w_cmp_k2[l],
                              w_cmp_v1[l], w_cmp_v2[l], rel_bias)
        o_conv = short_conv_mixer(b_g, c_g, x_t, conv_mix_w[l])
        y = jnp.concatenate([o_nsa, o_conv], axis=-1) @ w_out[l]
        x = x + rms_norm(y, norm_mix_post[l])
        h = rms_norm(x, norm_ffn_pre[l])
        gu = causal_dwconv(h @ w_ffn_up[l], ffn_conv_w[l])
        g, up = jnp.split(gu, 2, axis=-1)
        y = (jax.nn.gelu(g, approximate=True) * up) @ w_ffn_down[l]
        x = x + rms_norm(y, norm_ffn_post[l])
    return x
```

```python
import contextlib
import math
import numpy as np
import concourse.bass as bass
import concourse.mybir as mybir
from concourse.bass_utils import run_bass_kernel_spmd

F32 = mybir.dt.float32
BF16 = mybir.dt.bfloat16
AF = mybir.ActivationFunctionType
ALU = mybir.AluOpType
AX = mybir.AxisListType

ENGS = ("pe", "act", "dve", "pool", "sp")

S_LEN = 4096
D = 1024
NT = 32
NTG = 8
NEG = -30000.0
NEGB = -1.0e9
D_FF = 2816
NFC = 22
OWN_T = 17
SCR_W = S_LEN
WOUT_OFF = 212608
QGROUPS = ((3, 3), (4, 0), (5, 0), (6, 0), (7, 0))


class Op:
    __slots__ = ("eng", "fn", "deps", "dma", "idx", "need_inc", "semval", "sem", "strong", "eidx")


class Sched:
    def __init__(self):
        self.ops = []
        self.lw = {}
        self.rd = {}
        self.last_eng = {}
        self.dma_since_barrier = []
        self.ecount = {}

    def add(self, eng, fn, r=(), w=(), dma=False):
        op = Op()
        op.eng = eng
        op.fn = fn
        op.dma = dma
        op.idx = len(self.ops)
        op.need_inc = dma
        op.sem = None
        op.semval = 0
        deps = set()
        for k in r:
            if k in self.lw:
                deps.add(self.lw[k])
        op.strong = set(deps)
        op.eidx = self.ecount.get(eng, 0)
        self.ecount[eng] = op.eidx + 1
        for k in w:
            if k in self.lw:
                deps.add(self.lw[k])
            deps.update(self.rd.get(k, ()))
        for k in r:
            self.rd.setdefault(k, set()).add(op.idx)
        for k in w:
            self.lw[k] = op.idx
            self.rd[k] = set()
        op.deps = deps
        self.ops.append(op)
        if dma:
            self.dma_since_barrier.append(op.idx)
        else:
            self.last_eng[eng] = op.idx
        return op.idx

    def barrier(self):
        deps = set(self.last_eng.values()) | set(self.dma_since_barrier)
        self.dma_since_barrier = []
        for eng in ENGS:
            op = Op()
            op.eng = eng
            op.fn = None
            op.dma = False
            op.idx = len(self.ops)
            op.need_inc = False
            op.sem = None
            op.semval = 0
            op.deps = set(deps)
            op.strong = set(deps)
            op.eidx = -1
            self.ops.append(op)
        self.lw = {}
        self.rd = {}

    @staticmethod
    def _weak_skip(op, Dp, d):
        return ((not op.dma) and (not Dp.dma) and op.fn is not None and Dp.eng == op.eng
                and d not in op.strong and op.eidx - Dp.eidx >= 2)

    def emit(self, nc, n_dma_sems=64):
        ops = self.ops
        es = contextlib.ExitStack()
        with es:
            dma_sems = [es.enter_context(nc.semaphore(f"dq{i}")) for i in range(n_dma_sems)]
            eng_sems = {e: es.enter_context(nc.semaphore(f"es_{e}")) for e in ENGS}
            dma_last = [None] * n_dma_sems
            dma_cnt = [0] * n_dma_sems
            di = 0
            for op in ops:
                if op.dma:
                    j = di % n_dma_sems
                    di += 1
                    if dma_last[j] is not None:
                        op.deps.add(dma_last[j])
                    dma_last[j] = op.idx
                    dma_cnt[j] += 16
                    op.sem = ("d", j)
                    op.semval = dma_cnt[j]
            for op in ops:
                for d in op.deps:
                    Dp = ops[d]
                    if Dp.dma:
                        continue
                    if Dp.eng == op.eng and Dp.eng == "pe" and not op.dma:
                        continue
                    if self._weak_skip(op, Dp, d):
                        continue
                    Dp.need_inc = True
            cnt = {}
            for op in ops:
                if (not op.dma) and op.need_inc:
                    cnt[op.eng] = cnt.get(op.eng, 0) + 1
                    op.sem = ("e", op.eng)
                    op.semval = cnt[op.eng]

            def semh(s):
                return dma_sems[s[1]] if s[0] == "d" else eng_sems[s[1]]

            def make(eng_name):
                def body(e):
                    waited = {}
                    for op in ops:
                        if op.eng != eng_name:
                            continue
                        need = {}
                        for d in op.deps:
                            Dp = ops[d]
                            if (not Dp.dma) and Dp.eng == eng_name and eng_name == "pe" and not op.dma:
                                continue
                            if self._weak_skip(op, Dp, d):
                                continue
                            if Dp.sem is None:
                                continue
                            if need.get(Dp.sem, 0) < Dp.semval:
                                need[Dp.sem] = Dp.semval
                        for s, v in need.items():
                            if waited.get(s, 0) < v:
                                e.wait_ge(semh(s), v)
                                waited[s] = v
                        if op.fn is None:
                            continue
                        ins = op.fn(e)
                        if op.need_inc:
                            ins.then_inc(semh(op.sem), 16 if op.dma else 1)
                return body

            with nc.Block() as block:
                block.tensor(make("pe"))
                block.scalar(make("act"))
                block.vector(make("dve"))
                block.gpsimd(make("pool"))
                block.sync(make("sp"))
        return len(ops)


class SbAlloc:
    def __init__(self, nc, limit=229000):
        self.nc = nc
        self.off = 16384
        self.limit = limit
        self.n = 0

    def __call__(self, name, shape, dtype):
        nb = 4 if dtype == F32 else 2
        sz = int(np.prod(shape[1:])) * nb
        sz = (sz + 63) // 64 * 64
        if self.off + sz > self.limit:
            raise RuntimeError(f"SBUF overflow allocating {name}: {self.off}+{sz}")
        self.n += 1
        t = self.nc.alloc_sbuf_tensor_at(f"{name}_{self.n}", list(shape), dtype, offset=self.off)
        self.off += sz
        return t

    def mark(self):
        return self.off

    def release(self, m):
        self.off = m


def build_nc(debug=False):
    nc = bass.Bass("TRN2", target_bir_lowering=False)
    okind = "ExternalOutput" if debug else "Internal"

    def din(name, shape, dt=F32):
        return nc.dram_tensor(name, list(shape), dt, kind="ExternalInput").ap()

    x_full = din("x_full", [S_LEN, D])
    valid_d = din("valid", [128, NT])
    gam = din("gam", [4, 128, D])
    w_kvc = din("w_kvc", [D, 256])
    w_g = din("w_g", [2, D, 1292])
    w1kv_d = din("w1kv", [128, 32, 256])
    pekv_d = din("pekv", [128, 32])
    w2kv_d = din("w2kv", [128, 4, 64])
    btab_d = din("btab", [128, 8, 1280])
    c31_d = din("c31", [128, 8])
    ctab_d = din("ctab", [8, 2, 128, S_LEN])
    ovm_d = din("ovm1", [128, 2, 65])
    bsel_d = din("bsel", [NTG, 128, 4, 64])
    emat_d = din("emat", [64, S_LEN])
    ident_d = din("ident", [128, 128])
    cwm_d = din("cwm", [128, 4, 3])
    w_out_d = din("w_out", [D, D])
    w_up_d = din("w_up", [NFC, 128, 8, 256])
    cwf_d = din("cwf", [128, 2 * NFC, 3])
    w_dn_d = din("w_dn", [D_FF, D])
    out_d = nc.dram_tensor("out_own", [2048, D], F32, kind="ExternalOutput").ap()

    hT_scr = nc.dram_tensor("hT_scr", [D, S_LEN], BF16, kind=okind).ap()
    oT_scr = nc.dram_tensor("oT_scr", [D, SCR_W], BF16, kind=okind).ap()
    x1_scr = nc.dram_tensor("x1_scr", [OWN_T * 128, D], F32, kind=okind).ap()

    A = SbAlloc(nc)
    S = Sched()

    def op(eng, meth, r, w, *args, **kw):
        S.add(eng, lambda e: getattr(e, meth)(*args, **kw), r=r, w=w)

    def dma(eng, out, in_, r, w):
        S.add(eng, lambda e: e.dma_start(out=out, in_=in_), r=r, w=w, dma=True)

    def mm(out, lhsT, rhs, start, stop, r, w, skip=False):
        S.add("pe", lambda e: e.matmul(out, lhsT, rhs, start=start, stop=stop, skip_group_check=skip), r=r, w=w)

    def tr(out, in_, ident, r, w):
        S.add("pe", lambda e: e.transpose(out=out, in_=in_, identity=ident), r=r, w=w)

    def run_stages(n, stages):
        ns = len(stages)
        for step in range(n + ns - 1):
            for si in range(ns - 1, -1, -1):
                i = step - si
                if 0 <= i < n:
                    stages[si](i)

    psall = nc.alloc_psum_tensor("psall", [128, 7 * 512], F32)
    psf = [psall[:, i * 512:(i + 1) * 512] for i in range(7)]
    psb = nc.alloc_psum_tensor("psb", [128, 1024], BF16)
    PSB = ("ps", 7)
    psrot = [0]

    def next_ps(lo, hi):
        i = lo + psrot[0] % (hi - lo)
        psrot[0] += 1
        return i

    ident_b = A("ident_b", [128, 128], BF16)
    dma("pool", ident_b[:, :], ident_d, [], ["ident_b"])
    gbuf = A("gbuf", [128, D], F32)
    small = A("small", [128, 64], F32)
    eps_t = A("eps_t", [128, 1], F32)
    op("dve", "memset", [], ["eps_t"], eps_t[:, :], 1e-6)

    def rms_rstd(src, key_src, dst, key_dst, sq, key_sq):
        op("act", "activation", [key_src], [key_sq], out=sq, in_=src, func=AF.Square)
        op("dve", "reduce_sum", [key_sq], [key_dst], out=dst, in_=sq, axis=AX.X)
        op("act", "activation", [key_dst, "eps_t"], [key_dst], out=dst, in_=dst, func=AF.Ln, bias=eps_t[:, 0:1], scale=1.0 / D)
        op("act", "activation", [key_dst], [key_dst], out=dst, in_=dst, func=AF.Exp, scale=-0.5)

    mark_main = A.mark()

    hi = 150016
    Wc = nc.alloc_sbuf_tensor_at("Wc_hi", [128, 8, 256], BF16, offset=hi)
    w1kv = nc.alloc_sbuf_tensor_at("w1kv_hi", [128, 32, 256], BF16, offset=hi + 4096)
    pekv = nc.alloc_sbuf_tensor_at("pekv_hi", [128, 32], BF16, offset=hi + 4096 + 16384)
    w2kv = nc.alloc_sbuf_tensor_at("w2kv_hi", [128, 4, 64], BF16, offset=hi + 4096 + 16384 + 64)
    dma("pool", Wc[:, :, :], w_kvc.rearrange("(k p) n -> p k n", p=128), [], ["Wc"])
    for q4 in range(4):
        dma("pool", w1kv[:, q4 * 8:(q4 + 1) * 8, :], w1kv_d[:, q4 * 8:(q4 + 1) * 8, :], [], [("w1kv", q4)])
    dma("pool", pekv[:, :], pekv_d, [], ["pekv"])
    dma("pool", w2kv[:, :, :], w2kv_d, [], ["w2kv"])

    xin = [A(f"xin{i}", [128, D], F32) for i in range(8)]
    sqbs = [A(f"sqb{i}", [128, D], F32) for i in range(2)]
    hb = [A(f"hb{i}", [128, D], BF16) for i in range(2)]
    hstage = [A(f"hst{i}", [128, 8, 512], BF16) for i in range(2)]
    rs0 = A("rs0", [128, NT], F32)
    dma("sp", gbuf[:, :], gam[0], [], ["gbuf"])
    hT_v = hT_scr.rearrange("(k p) t -> p k t", p=128)

    def p0_a(i):
        dma("sp", xin[i % 8][:, :], x_full[i * 128:(i + 1) * 128, :], [], [("xin", i % 8)])

    def p0_b(i):
        op("act", "activation", [("xin", i % 8)], [("sqb", i % 2), ("rs0", i)], out=sqbs[i % 2][:, :], in_=xin[i % 8][:, :], func=AF.Square,
           accum_out=rs0[:, i:i + 1])

    def p0_c(i):
        op("dve", "reduce_sum", [("sqb", i % 2)], [("rs0", i)], out=rs0[:, i:i + 1], in_=sqbs[i % 2][:, :], axis=AX.X)

    def p0_d(i):
        kr = ("rs0", i)
        op("act", "activation", [kr, "eps_t"], [kr], out=rs0[:, i:i + 1], in_=rs0[:, i:i + 1], func=AF.Ln, bias=eps_t[:, 0:1], scale=1.0 / D)
        op("act", "activation", [kr], [kr], out=rs0[:, i:i + 1], in_=rs0[:, i:i + 1], func=AF.Exp, scale=-0.5)

    def p0_e(i):
        hbi = hb[i % 2]
        kh = ("hb", i % 2)
        op("dve", "scalar_tensor_tensor", [("xin", i % 8), ("rs0", i), "gbuf"], [kh], out=hbi[:, :], in0=xin[i % 8][:, :],
           scalar=rs0[:, i:i + 1], in1=gbuf[:, :], op0=ALU.mult, op1=ALU.mult)
        for k in range(8):
            tr(psb[:, k * 128:(k + 1) * 128], hbi[:, k * 128:(k + 1) * 128], ident_b[:, :], [kh, "ident_b"], [PSB])

    def p0_f(i):
        tg = i // 4
        hs = hstage[tg % 2]
        ks = ("hst", tg % 2, i % 4)
        hdst = hs[:, :, (i % 4) * 128:(i % 4 + 1) * 128]
        hsrc = psb[:, :].rearrange("p (k t) -> p k t", k=8)
        if i % 2 == 0:
            op("act", "activation", [PSB], [ks], out=hdst, in_=hsrc, func=AF.Copy)
        else:
            op("dve", "tensor_copy", [PSB], [ks], out=hdst, in_=hsrc)
        if i % 4 == 3:
            dma("sp", hT_v[:, :, tg * 512:(tg + 1) * 512], hs[:, :, :], [("hst", tg % 2, j) for j in range(4)], [("hT", tg)])

    run_stages(NT, [p0_a, lambda i: None, lambda i: None, p0_b, p0_d, p0_e, p0_f])
    A.peak = max(getattr(A, "peak", 0), A.off)
    if A.off > WOUT_OFF and len(S.ops) > 3000 and len(S.ops) < 16000:
        raise RuntimeError("attention scope overlaps wout")
    S.barrier()
    A.release(mark_main)

    hbuf = [A(f"hbuf{i}", [128, 8, 512], BF16) for i in range(2)]
    KCT = [A(f"KCT{g}", [64, 256], BF16) for g in range(2)]
    CV = [A(f"CV{g}", [128, 2, 128], BF16) for g in range(2)]
    cbias = A("cbias", [128, 4], F32)

    def load_hbuf(tg, it, split=False):
        hbq = hbuf[it % 2]
        if split:
            dma("sp", hbq[:, 0:4, :], hT_v[:, 0:4, tg * 512:(tg + 1) * 512], [("hT", tg)], [("hbuf", it % 2)])
            dma("pool", hbq[:, 4:8, :], hT_v[:, 4:8, tg * 512:(tg + 1) * 512], [("hT", tg)], [("hbuf", it % 2)])
        else:
            dma("sp", hbq[:, :, :], hT_v[:, :, tg * 512:(tg + 1) * 512], [("hT", tg)], [("hbuf", it % 2)])
        return hbq, ("hbuf", it % 2)

    hit = [0]
    mark_1a = A.mark()
    KVC = [A(f"KVC{g}", [128, 16, 256], BF16) for g in range(2)]
    HID = A("HID", [128, 4, 256], BF16)
    for g in range(2):
        op("pool", "memset", [], [("CV", g)], CV[g][:, :, :], 0.0)
        op("pool", "memset", [], [("KCT", g)], KCT[g][:, :], 0.0)
    for g in range(2):
        dma("pool", CV[g][:, :, 0:64], ovm_d[:, :, 0:64], [("CV", g)], [("CV", g)])
    for tg in range(NTG):
        hbq, khb = load_hbuf(tg, hit[0], split=True); hit[0] += 1
        for g in range(2):
            pi = next_ps(0, 6)
            for k in range(8):
                mm(psf[pi][:, :], Wc[:, k, g * 128:(g + 1) * 128], hbq[:, k, :], k == 0, k == 7, ["Wc", khb], [("ps", pi)])
            op("act", "activation", [("ps", pi)], [("KVC", g, tg)], out=KVC[g][:, :, tg * 32:(tg + 1) * 32], in_=psf[pi][:, :].rearrange("p (m f) -> p f m", f=16), func=AF.Copy)
    w1keys = [("w1kv", q4) for q4 in range(4)]
    for g in range(2):
        kvall = [("KVC", g, tg) for tg in range(NTG)]
        pi = next_ps(0, 6)
        for kvi in range(2):
            rows = slice(kvi * 64, (kvi + 1) * 64)
            for hc in range(2):
                col = kvi * 2 + hc
                for l in range(32):
                    mm(psf[pi][:, col:col + 1], w1kv[rows, l, hc * 128:(hc + 1) * 128], pekv[rows, l:l + 1],
                       l == 0 and col == 0, l == 31, w1keys + ["pekv"], [("ps", pi)], skip=True)
        op("dve", "tensor_copy", [("ps", pi)], ["cbias"], out=cbias[:, :], in_=psf[pi][:, 0:4])
        for kvi in range(2):
            rows = slice(kvi * 64, (kvi + 1) * 64)
            for hc in range(2):
                col = kvi * 2 + hc
                pi = next_ps(0, 6)
                for l in range(32):
                    mm(psf[pi][:, 0:255], w1kv[rows, l, hc * 128:(hc + 1) * 128], KVC[g][rows, l % 16, l // 16:l // 16 + 255],
                       l == 0, l == 31, w1keys + kvall, [("ps", pi)])
                op("act", "activation", [("ps", pi), "cbias"], [("HID", col)], out=HID[:, col, 0:255], in_=psf[pi][:, 0:255],
                   func=AF.Gelu_apprx_tanh, bias=cbias[:, col:col + 1], scale=1.0)
        pi = next_ps(0, 6)
        for hc in range(2):
            mm(psf[pi][0:64, 0:255], w2kv[:, hc, :], HID[:, hc, 0:255], hc == 0, hc == 1, ["w2kv", ("HID", hc)], [("ps", pi)])
        op("act", "activation", [("ps", pi)], [("KCT", g)], out=KCT[g][:, 0:255], in_=psf[pi][0:64, 0:255], func=AF.Copy)
        for nt in range(2):
            rows_n = 128 if nt == 0 else 127
            pi = next_ps(0, 6)
            for hc in range(2):
                mm(psf[pi][0:rows_n, 0:64], HID[:, 2 + hc, nt * 128:nt * 128 + rows_n], w2kv[:, 2 + hc, :], hc == 0, hc == 1,
                   ["w2kv", ("HID", 2 + hc)], [("ps", pi)])
            op("act", "activation", [("ps", pi)], [("CV", g)], out=CV[g][0:rows_n, nt, 64:128], in_=psf[pi][0:rows_n, 0:64], func=AF.Copy)
    A.peak = max(getattr(A, "peak", 0), A.off)
    S.barrier()
    A.release(mark_1a)

    Wg = A("Wg", [128, 8, 1292], BF16)
    QN = A("QN", [128, 4, S_LEN], BF16)
    KE = A("KE", [128, S_LEN], BF16)
    KW = A("KW", [64, S_LEN], BF16)
    V1s = A("V1s", [128, NT, 65], BF16)
    V1w = A("V1w", [128, NT, 65], BF16)
    btab2 = [A(f"btab{i}", [128, 4, 1280], BF16) for i in range(2)]
    wout = nc.alloc_sbuf_tensor_at("wout_hi", [128, 8, D], BF16, offset=WOUT_OFF)
    c31 = A("c31", [128, 8], F32)
    gsig = A("gsig", [128, NT, 12], F32)
    bsel = [A(f"bsel{i}", [128, 4, 64], F32) for i in range(2)]
    ctb = [A(f"ctb{i}", [128, 512], BF16) for i in range(16)]
    EC = [A(f"EC{i}", [128, 2, 512], BF16) for i in range(3)]
    PT = [A(f"PT{i}", [128, 512], BF16) for i in range(4)]
    PT2 = [A(f"PTW{i}", [128, 1024], BF16) for i in range(3)]
    p2it = [0]
    pbit = [0]
    onsa2 = [A(f"onsa{i}", [128, 4, 256], F32) for i in range(3)]
    onsb = A("onsb", [128, 4, 256], BF16)
    ostage = A("ostage", [128, 2, 512], BF16)
    impacc = A("impacc", [128, 4, 64], F32)
    sc1 = A("sc1", [128, 64], F32)
    sc2 = A("sc2", [128, 64], F32)
    m8a = A("m8a", [128, 8], F32)
    m8b = A("m8b", [128, 8], F32)
    nsb4 = A("nsb4", [128, 4, 128], BF16)
    zbuf = A("zbuf", [128, 2, 514], F32)
    tmpc = A("tmpc", [128, 512], F32)
    cacc = A("cacc", [128, 512], F32)
    convo = [A(f"convo{i}", [128, 512], BF16) for i in range(2)]
    cwm = A("cwm", [128, 4, 3], F32)
    zero_b = A("zero_b", [128, 128], BF16)
    nsT = A("nsT", [128, 512], BF16)

    dma("sp", c31[:, :], c31_d, [], ["c31"])
    dma("sp", cwm[:, :, :], cwm_d, [], ["cwm"])
    dma("pool", KE[64:128, :], emat_d, [], ["KE_E"])
    vld = A("vld", [128, NT], F32)
    dma("sp", vld[:, :], valid_d, [], ["vld"])
    op("pool", "tensor_copy", ["vld"], ["V1s1"], out=V1s[:, :, 64], in_=vld[:, :])
    op("pool", "tensor_copy", ["vld"], ["V1w1"], out=V1w[:, :, 64], in_=vld[:, :])
    for i2 in range(3):
        op("pool", "memset", [], [("onsa", i2, qb, r) for qb in range(4) for r in range(4)], onsa2[i2][:, :, :], 0.0)
    op("pool", "memset", [], ["nsb0"], nsb4[:, :, 0:64], 0.0)
    op("pool", "memset", [], ["zero_b"], zero_b[:, :], 0.0)
    oT_v = oT_scr.rearrange("(k p) t -> p k t", p=128)

    cit = [0]
    pit = [0]
    sm = [0]
    eit = [0]
    obr = [0]

    def smcol():
        c = sm[0] % 64
        sm[0] += 1
        return c

    for g in range(2):
        btab = btab2[g]
        if g == 0:
            dma("pool", Wg[:, :, :], w_g[0].rearrange("(k p) n -> p k n", p=128), [], ["Wg"])
            dma("pool", btab2[0][:, :, :], btab_d[:, 0:4, :], [], [("btab", 0)])
            dma("pool", btab2[1][:, :, :], btab_d[:, 4:8, :], [], [("btab", 1)])
        op("dve", "memset", [], [("zbuf", 0), ("zbuf", 1)], zbuf[:, :, 0:2], 0.0)
        def sm4():
            c = (sm[0] % 16) * 4
            sm[0] += 1
            return c, ("sm4", c)

        oslot = {}

        def run_pipeline(tiles, hooks=None, SK=2):
            n = len(tiles)
            for i in range(n + SK):
                if i < n:
                    tiles[i][0]()
                    tiles[i][1]()
                if i - SK >= 0 and tiles[i - SK][2] is not None:
                    tiles[i - SK][2]()
                if hooks and i in hooks:
                    for hk in hooks[i]:
                        hk()

        gidx = {t_: i_ for i_, (t_, _q2) in enumerate(QGROUPS)}

        def prefetch_ctab(tg, qmin):
            cq_ = qmin * 128
            cols_ = slice(tg * 512 + cq_, (tg + 1) * 512)
            for r_ in range(4):
                for nt_ in ([0] if tg < 4 else [0, 1]):
                    ci_ = (gidx[tg] % 2) * 8 + r_ * 2 + nt_
                    dma("pool", ctb[ci_][:, cq_:512], ctab_d[4 * g + r_, nt_, :, cols_], [], [("ctb", ci_)])

        def comp_tiles(tg, qmin):
            cols = slice(tg * 512 + qmin * 128, (tg + 1) * 512)
            cq = qmin * 128
            onsa = onsa2[oslot[tg]]
            par = oslot[tg]
            bs = bsel[tg % 2]
            kbs = ("bsel", tg % 2)
            dma("sp", bs[:, :, :], bsel_d[tg], [], [kbs])
            nts = [0] if tg < 4 else [0, 1]
            tiles = []
            for r in range(4):
                h = 4 * g + r
                ei = eit[0] % 3
                eit[0] += 1
                ec = EC[ei]
                for nt in nts:
                    st = {}

                    def A_(r=r, h=h, nt=nt, st=st):
                        ci = (gidx[tg] % 2) * 8 + r * 2 + nt
                        pi = next_ps(0, 4)
                        st["pi"] = pi
                        mm(psf[pi][:, cq:512], KCT[g][0:64, nt * 128:(nt + 1) * 128], QN[0:64, r, cols], True, False,
                           [("KCT", g), ("QNq", r, tg)], [("ps", pi)])
                        mm(psf[pi][:, cq:512], ident_b[:, :], ctb[ci][:, cq:512], False, True, ["ident_b", ("ctb", ci)], [("ps", pi)])

                    def B_(nt=nt, st=st, ec=ec, ei=ei):
                        pi = st["pi"]
                        op("act", "activation", [("ps", pi)], [("EC", ei, nt)], out=ec[:, nt, cq:512], in_=psf[pi][:, cq:512], func=AF.Exp)

                    C_ = None
                    if nt == nts[-1]:
                        def C_(r=r, ec=ec, ei=ei):
                            ob = 4 + obr[0] % 3
                            obr[0] += 1
                            kob = ("ps", ob)
                            firstmm = True
                            for qb in range(qmin, 4):
                                for nt2 in nts:
                                    mm(psf[ob][:, qb * 128:(qb + 1) * 128], ec[:, nt2, qb * 128:(qb + 1) * 128], CV[g][:, nt2, :],
                                       firstmm, False, [("EC", ei, nt2), ("CV", g)], [kob], skip=True)
                                    firstmm = False
                            pv4 = psf[ob][:, :].rearrange("p (q c) -> p q c", q=4)
                            c0, k0 = sm4()
                            c1, k1 = sm4()
                            op("dve", "reduce_sum", [kob], [k0], out=small[:, c0:c0 + 4], in_=pv4[:, :, 0:64], axis=AX.X)
                            op("dve", "tensor_scalar", [k0], [k0], out=small[:, c0:c0 + 4], in0=small[:, c0:c0 + 4], scalar1=1e-30, scalar2=None, op0=ALU.max)
                            op("dve", "reciprocal", [k0], [k0], out=small[:, c0:c0 + 4], in_=small[:, c0:c0 + 4])
                            op("dve", "tensor_tensor", [k0] + [("gsig", 4 * tg + j) for j in range(4)], [k1], out=small[:, c1:c1 + 4],
                               in0=small[:, c0:c0 + 4], in1=gsig[:, 4 * tg:4 * tg + 4, r * 3], op=ALU.mult)
                            for qb in range(qmin, 4):
                                if r == 0:
                                    op("dve", "tensor_scalar", [kob, k0], [("imp", qb)], out=impacc[:, qb, :], in0=pv4[:, qb, 0:64],
                                       scalar1=small[:, c0 + qb:c0 + qb + 1], scalar2=None, op0=ALU.mult)
                                else:
                                    op("dve", "scalar_tensor_tensor", [kob, k0, ("imp", qb)], [("imp", qb)], out=impacc[:, qb, :],
                                       in0=pv4[:, qb, 0:64], scalar=small[:, c0 + qb:c0 + qb + 1], in1=impacc[:, qb, :], op0=ALU.mult, op1=ALU.add)
                                op("dve", "tensor_scalar", [kob, k1], [("onsa", par, qb, r)], out=onsa[:, qb, r * 64:(r + 1) * 64],
                                   in0=pv4[:, qb, 64:128], scalar1=small[:, c1 + qb:c1 + qb + 1], scalar2=None, op0=ALU.mult)
                            if r == 3:
                                topk_dve(tg, qmin)
                    tiles.append((A_, B_, C_))
            return tiles

        def topk_dve(tg, qmin):
            bs = bsel[tg % 2]
            kbs = ("bsel", tg % 2)
            for qb in range(qmin, 4):
                op("dve", "tensor_tensor", [("imp", qb), kbs], ["sc1"], out=sc1[:, :], in0=impacc[:, qb, :], in1=bs[:, qb, :], op=ALU.add)
                op("dve", "max", ["sc1"], ["m8a"], out=m8a[:, :], in_=sc1[:, :])
                op("dve", "match_replace", ["sc1", "m8a"], ["sc2"], out=sc2[:, :], in_to_replace=m8a[:, :], in_values=sc1[:, :], imm_value=-1e30)
                op("dve", "max", ["sc2"], ["m8b"], out=m8b[:, :], in_=sc2[:, :])
                op("dve", "tensor_scalar", ["sc1", "m8b", "nsb0"], [("nsb", qb)], out=nsb4[:, qb, 64:128], in0=sc1[:, :], scalar1=m8b[:, 7:8], scalar2=NEG,
                   op0=ALU.is_lt, op1=ALU.mult)

        def topk_pe(tg, qmin):
            for qb in range(qmin, 4):
                tr(psb[:, qb * 128:(qb + 1) * 128], nsb4[:, qb, :], ident_b[:, :], [("nsb", qb), "nsb0", "ident_b"], [PSB])
            cq = qmin * 128
            op("dve", "tensor_copy", [PSB], ["nsT"], out=nsT[64:128, cq:512], in_=psb[64:128, cq:512])
            for r in range(4):
                op("pool", "tensor_copy", ["nsT"], [("QNm", r, tg * 4 + qb) for qb in range(qmin, 4)],
                   out=QN[64:128, r, tg * 512 + cq:(tg + 1) * 512], in_=nsT[64:128, cq:512])

        def evac_o(tg, r, br, ob, qmin=0):
            onsa = onsa2[oslot[tg]]
            par = oslot[tg]
            kob = ("ps", ob)
            c0, k0 = sm4()
            op("dve", "tensor_scalar", [kob], [k0], out=small[:, c0:c0 + 4], in0=psf[ob][:, 64:260:65], scalar1=1e-30, scalar2=None, op0=ALU.max)
            op("dve", "reciprocal", [k0], [k0], out=small[:, c0:c0 + 4], in_=small[:, c0:c0 + 4])
            op("dve", "tensor_tensor", [k0] + [("gsig", 4 * tg + j) for j in range(4)], [k0], out=small[:, c0:c0 + 4],
               in0=small[:, c0:c0 + 4], in1=gsig[:, 4 * tg:4 * tg + 4, r * 3 + br], op=ALU.mult)
            for qb in range(qmin, 4):
                op("dve", "scalar_tensor_tensor", [kob, k0, ("onsa", par, qb, r)], [("onsa", par, qb, r)], out=onsa[:, qb, r * 64:(r + 1) * 64],
                   in0=psf[ob][:, qb * 65:qb * 65 + 64], scalar=small[:, c0 + qb:c0 + qb + 1], in1=onsa[:, qb, r * 64:(r + 1) * 64],
                   op0=ALU.mult, op1=ALU.add)

        def sel_tiles(tg, qmin):
            tiles = []
            for r in range(4):
                h = 4 * g + r
                grp = {"ob": None}
                nkt = 4 * tg + 4
                npair = (4 * tg - 1) // 2 if qmin == 0 else 0
                for pj in range(npair):
                    st = {}
                    kt0 = 2 * pj

                    def A_(r=r, kt0=kt0, st=st):
                        pb = pbit[0] % 2
                        pbit[0] += 1
                        st["pb"] = pb
                        for u in range(2):
                            kt = kt0 + u
                            mm(psf[2 * pb + u][:, :], KE[:, kt * 128:(kt + 1) * 128], QN[:, r, tg * 512:(tg + 1) * 512], True, True,
                               [("KEk", kt // 4), "KE_E", ("QNq", r, tg)] + [("QNm", r, tg * 4 + j) for j in range(4)], [("ps", 2 * pb + u)])

                    def B_(h=h, st=st):
                        pb = st["pb"]
                        x_ = p2it[0] % 3
                        p2it[0] += 1
                        st["pt2"] = x_
                        op("act", "activation", [("ps", 2 * pb), ("ps", 2 * pb + 1), "c31"], [("PT2", x_)], out=PT2[x_][:, :],
                           in_=psall[:, 2 * pb * 512:2 * pb * 512 + 1024], func=AF.Exp, bias=c31[:, h:h + 1], scale=1.0)

                    def C_(r=r, kt0=kt0, st=st, grp=grp):
                        if grp["ob"] is None:
                            grp["ob"] = 4 + obr[0] % 3
                            obr[0] += 1
                            grp["first"] = True
                        ob = grp["ob"]
                        x_ = st["pt2"]
                        for u in range(2):
                            kt = kt0 + u
                            for qb in range(4):
                                mm(psf[ob][:, qb * 65:(qb + 1) * 65], PT2[x_][:, u * 512 + qb * 128:u * 512 + (qb + 1) * 128], V1s[:, kt, :],
                                   grp["first"], False, [("PT2", x_), ("V1s", kt), "V1s1"], [("ps", ob)], skip=True)
                                grp["first"] = False
                    tiles.append((A_, B_, C_))
                for kt in range(2 * npair, nkt):
                    st = {}
                    d = kt - 4 * tg
                    q0 = max(qmin, max(0, d))
                    if q0 > 3:
                        continue
                    c0 = 128 * q0
                    near = d >= -1

                    def A_(r=r, kt=kt, d=d, q0=q0, c0=c0, near=near, st=st):
                        pi = next_ps(0, 4)
                        st["pi"] = pi
                        mm(psf[pi][:, c0:512], KE[:, kt * 128:(kt + 1) * 128], QN[:, r, tg * 512 + c0:(tg + 1) * 512], True, not near,
                           [("KEk", kt // 4), "KE_E", ("QNq", r, tg)] + [("QNm", r, tg * 4 + j) for j in range(q0, 4)], [("ps", pi)])
                        if near:
                            mm(psf[pi][:, c0:512], ident_b[:, :], btab[:, r, (q0 - d) * 128:(4 - d) * 128], False, True,
                               ["ident_b", ("btab", g)], [("ps", pi)])

                    def B_(h=h, c0=c0, near=near, st=st):
                        pi = st["pi"]
                        p_ = pit[0] % 4
                        pit[0] += 1
                        st["pt"] = p_
                        if near:
                            op("act", "activation", [("ps", pi)], [("PT", p_)], out=PT[p_][:, c0:512], in_=psf[pi][:, c0:512], func=AF.Exp)
                        else:
                            op("act", "activation", [("ps", pi), "c31"], [("PT", p_)], out=PT[p_][:, c0:512], in_=psf[pi][:, c0:512], func=AF.Exp,
                               bias=c31[:, h:h + 1], scale=1.0)

                    def C_(r=r, kt=kt, q0=q0, st=st, grp=grp, last=(kt == nkt - 1)):
                        if grp["ob"] is None:
                            grp["ob"] = 4 + obr[0] % 3
                            obr[0] += 1
                            grp["first"] = True
                        ob = grp["ob"]
                        p_ = st["pt"]
                        for qb in range(q0, 4):
                            mm(psf[ob][:, qb * 65:(qb + 1) * 65], PT[p_][:, qb * 128:(qb + 1) * 128], V1s[:, kt, :], grp["first"], False,
                               [("PT", p_), ("V1s", kt), "V1s1"], [("ps", ob)], skip=True)
                            grp["first"] = False
                        if last:
                            evac_o(tg, r, 1, ob, qmin)
                    tiles.append((A_, B_, C_))
            return tiles

        def win_tiles(tg, qmin):
            tiles = []
            for r in range(4):
                grp = {"ob": None}
                ds_ = [d for d in range(-4, 4) if 4 * tg + d >= 0 and max(qmin, max(0, d)) <= min(3, d + 4)]
                for d in ds_:
                    st = {}
                    kt = 4 * tg + d
                    qlo = max(qmin, max(0, d))
                    qhi = min(3, d + 4)
                    c0 = 128 * qlo
                    c1 = 128 * (qhi + 1)

                    def A_(r=r, kt=kt, qlo=qlo, qhi=qhi, c0=c0, c1=c1, st=st):
                        pi = next_ps(0, 4)
                        st["pi"] = pi
                        mm(psf[pi][:, c0:c1], KW[0:64, kt * 128:(kt + 1) * 128], QN[0:64, r, tg * 512 + c0:tg * 512 + c1], True, False,
                           [("KW", kt // 4), ("QNq", r, tg)], [("ps", pi)])
                        dd = kt - 4 * tg
                        mm(psf[pi][:, c0:c1], ident_b[:, :], btab[:, r, 640 + (qlo - dd) * 128:640 + (qhi - dd + 1) * 128], False, True,
                           ["ident_b", ("btab", g)], [("ps", pi)])

                    def B_(c0=c0, c1=c1, st=st):
                        pi = st["pi"]
                        p_ = pit[0] % 4
                        pit[0] += 1
                        st["pt"] = p_
                        op("act", "activation", [("ps", pi)], [("PT", p_)], out=PT[p_][:, c0:c1], in_=psf[pi][:, c0:c1], func=AF.Exp)

                    def C_(r=r, kt=kt, qlo=qlo, qhi=qhi, st=st, grp=grp, last=(d == ds_[-1])):
                        if grp["ob"] is None:
                            grp["ob"] = 4 + obr[0] % 3
                            obr[0] += 1
                            grp["first"] = True
                        ob = grp["ob"]
                        p_ = st["pt"]
                        for qb in range(qlo, qhi + 1):
                            mm(psf[ob][:, qb * 65:(qb + 1) * 65], PT[p_][:, qb * 128:(qb + 1) * 128], V1w[:, kt, :], grp["first"], False,
                               [("PT", p_), ("V1w", kt), "V1w1"], [("ps", ob)], skip=True)
                            grp["first"] = False
                        if last:
                            evac_o(tg, r, 2, ob, qmin)
                    tiles.append((A_, B_, C_))
            return tiles

        def finalize(tg):
            onsa = onsa2[oslot[tg]]
            par = oslot[tg]
            allo = [("onsa", par, qb, r) for qb in range(4) for r in range(4)]
            op("pool", "tensor_copy", allo, ["onsb"], out=onsb[:, :, :], in_=onsa[:, :, :])
            for qb in range(4):
                for cc in range(2):
                    tr(psb[:, (qb * 2 + cc) * 128:(qb * 2 + cc + 1) * 128], onsb[:, qb, cc * 128:(cc + 1) * 128], ident_b[:, :],
                       ["onsb", "ident_b"], [PSB])
            pv = psb[:, :].rearrange("p (q c t) -> p q c t", q=4, c=2)
            for cc in range(2):
                op("act", "activation", [PSB], [("ostage", cc)], out=ostage[:, cc, :].rearrange("p (q t) -> p q t", q=4), in_=pv[:, :, cc, :], func=AF.Copy)
                dma("sp", oT_v[:, 2 * g + cc, tg * 512:(tg + 1) * 512], ostage[:, cc, :], [("ostage", cc)],
                    [("oscr", 2 * g + cc, tg * 4 + j) for j in range(4)])

        for gi_, (tg_, _q) in enumerate(QGROUPS):
            oslot[tg_] = gi_ % 3
        prefetch_ctab(*QGROUPS[0])
        hnext = load_hbuf(0, hit[0]); hit[0] += 1
        for tg in range(NTG):
            hbq, khb = hnext
            if tg + 1 < NTG:
                hnext = load_hbuf(tg + 1, hit[0]); hit[0] += 1
            cols = slice(tg * 512, (tg + 1) * 512)
            for r in range(4 if tg >= QGROUPS[0][0] else 0):
                pi = next_ps(0, 6)
                for k in range(8):
                    mm(psf[pi][0:64, :], Wg[:, k, r * 64:(r + 1) * 64], hbq[:, k, :], k == 0, k == 7, ["Wg", khb], [("ps", pi)])
                op("act", "activation", [("ps", pi)], [("QNq", r, tg)], out=QN[0:64, r, cols], in_=psf[pi][0:64, :], func=AF.Copy, scale=0.125)
            for which, dst, key in ((0, KE, "KEk"), (1, KW, "KW")):
                pi = next_ps(0, 6)
                for k in range(8):
                    mm(psf[pi][0:64, :], Wg[:, k, 256 + which * 64:320 + which * 64], hbq[:, k, :], k == 0, k == 7, ["Wg", khb], [("ps", pi)])
                op("dve", "tensor_copy", [("ps", pi)], [(key, tg)], out=dst[0:64, cols], in_=psf[pi][0:64, :])
            for b in range(4):
                tile_i = tg * 4 + b
                pi = next_ps(0, 6)
                for k in range(8):
                    mm(psf[pi][:, 0:140], hbq[:, k, b * 128:(b + 1) * 128], Wg[:, k, 384:524], k == 0, k == 7, ["Wg", khb], [("ps", pi)])
                op("dve", "tensor_copy", [("ps", pi)], [("V1s", tile_i), ("psx", pi)], out=V1s[:, tile_i, 0:64], in_=psf[pi][:, 0:64])
                op("dve", "tensor_copy", [("ps", pi)], [("V1w", tile_i), ("psx", pi)], out=V1w[:, tile_i, 0:64], in_=psf[pi][:, 64:128])
                op("act", "activation", [("ps", pi)], [("gsig", tile_i), ("psx", pi)], out=gsig[:, tile_i, :], in_=psf[pi][:, 128:140], func=AF.Sigmoid)
            for cc in range(2 if tg >= QGROUPS[0][0] else 0):
                ch = 2 * g + cc
                kz = ("zbuf", cc)
                pc = next_ps(0, 6)
                for k in range(8):
                    mm(psf[pc][:, :], Wg[:, k, 524 + cc * 128:652 + cc * 128], hbq[:, k, :], k == 0, k == 7, ["Wg", khb], [("ps", pc)])
                op("act", "activation", [("ps", pc)], ["tmpc"], out=tmpc[:, :], in_=psf[pc][:, :], func=AF.Copy)
                px = next_ps(0, 6)
                for k in range(8):
                    mm(psf[px][:, :], Wg[:, k, 780 + cc * 128:908 + cc * 128], hbq[:, k, :], k == 0, k == 7, ["Wg", khb], [("ps", px)])
                op("dve", "tensor_tensor", [("ps", px), "tmpc", kz], [kz], out=zbuf[:, cc, 2:514], in0=tmpc[:, :], in1=psf[px][:, :], op=ALU.mult)
                op("dve", "tensor_scalar", [kz, "cwm"], ["cacc"], out=cacc[:, :], in0=zbuf[:, cc, 2:514], scalar1=cwm[:, ch, 2:3], scalar2=None, op0=ALU.mult)
                op("dve", "scalar_tensor_tensor", [kz, "cwm", "cacc"], ["cacc"], out=cacc[:, :], in0=zbuf[:, cc, 1:513], scalar=cwm[:, ch, 1:2],
                   in1=cacc[:, :], op0=ALU.mult, op1=ALU.add)
                op("dve", "scalar_tensor_tensor", [kz, "cwm", "cacc"], ["cacc"], out=cacc[:, :], in0=zbuf[:, cc, 0:512], scalar=cwm[:, ch, 0:1],
                   in1=cacc[:, :], op0=ALU.mult, op1=ALU.add)
                op("dve", "tensor_copy", [kz], [kz], out=zbuf[:, cc, 0:2], in_=zbuf[:, cc, 512:514])
                pb = next_ps(0, 6)
                for k in range(8):
                    mm(psf[pb][:, :], Wg[:, k, 1036 + cc * 128:1164 + cc * 128], hbq[:, k, :], k == 0, k == 7, ["Wg", khb], [("ps", pb)])
                cvo = convo[cit[0] % 2]
                kcv = ("convo", cit[0] % 2)
                cit[0] += 1
                op("dve", "tensor_tensor", [("ps", pb), "cacc"], [kcv], out=cvo[:, :], in0=cacc[:, :], in1=psf[pb][:, :], op=ALU.mult)
                dma("sp", oT_v[:, 4 + ch, tg * 512:(tg + 1) * 512], cvo[:, :], [kcv],
                    [("oscr", 4 + ch, tg * 4 + j) for j in range(4)])
            if tg == QGROUPS[0][0]:
                run_pipeline(comp_tiles(*QGROUPS[0]))
                prefetch_ctab(*QGROUPS[1])

        def late_weights():
            dma("pool", Wg[:, :, :], w_g[1].rearrange("(k p) n -> p k n", p=128), [], ["Wg"])
            for k2 in range(2):
                dma("pool", wout[:, 4 * k2:4 * k2 + 4, :], w_out_d.rearrange("(k p) n -> p k n", p=128)[:, 4 * k2:4 * k2 + 4, :], [], [("wout", k2)])
        topk_pe(*QGROUPS[0])
        for gi_, (tg, qmin) in enumerate(QGROUPS):
            tl = []
            hooks = {}
            if gi_ + 1 < len(QGROUPS):
                tl += comp_tiles(*QGROUPS[gi_ + 1])
            ncomp = len(tl)
            tl += sel_tiles(tg, qmin)
            tl += win_tiles(tg, qmin)
            if gi_ > 0:
                ptg = QGROUPS[gi_ - 1][0]
                hooks.setdefault(min(ncomp + 10, len(tl) - 1), []).append(lambda ptg=ptg: finalize(ptg))
            if gi_ + 1 < len(QGROUPS):
                nxt = QGROUPS[gi_ + 1]
                hooks.setdefault(ncomp + (len(tl) - ncomp) // 2, []).append(lambda nxt=nxt: topk_pe(*nxt))
            if g == 0 and gi_ == len(QGROUPS) - 2:
                hooks.setdefault(ncomp + (len(tl) - ncomp) // 3, []).append(late_weights)
            if gi_ + 2 < len(QGROUPS):
                nn2 = QGROUPS[gi_ + 2]
                hooks.setdefault(ncomp + (len(tl) - ncomp) // 3, []).append(lambda nn2=nn2: prefetch_ctab(*nn2))
            run_pipeline(tl, hooks)
        finalize(QGROUPS[-1][0])
    A.peak = max(getattr(A, "peak", 0), A.off)
    if A.off > WOUT_OFF and len(S.ops) > 3000 and len(S.ops) < 16000:
        raise RuntimeError("attention scope overlaps wout")
    S.barrier()
    A.release(mark_main)

    h2T = A("h2T", [128, 8, OWN_T * 128], BF16)
    mark_post = A.mark()
    osel = [A(f"osel{i}", [128, 8, 128], BF16) for i in range(4)]
    ybufs = [A(f"ybuf{i}", [128, D], F32) for i in range(4)]
    xo = [A(f"xo{i}", [128, D], F32) for i in range(7)]
    x1b = [A(f"x1b{i}", [128, D], F32) for i in range(5)]
    sq2s = [A(f"sq2{i}", [128, D], F32) for i in range(2)]
    hb2 = [A(f"hb2{i}", [128, D], BF16) for i in range(2)]
    g1 = A("g1", [128, D], F32)
    g2 = A("g2", [128, D], F32)
    rs1 = A("rs1", [128, 2 * OWN_T], F32)
    dma("sp", g1[:, :], gam[1], [], ["g1"])
    dma("sp", g2[:, :], gam[2], [], ["g2"])
    mp_ps = {}

    def mp_ld(i):
        dma("sp", osel[i % 4][:, :, :], oT_v[:, :, (15 + i) * 128:(16 + i) * 128], [], [("osel", i % 4)])
        dma("sp", xo[i % 7][:, :], x_full[(15 + i) * 128:(16 + i) * 128, :], [], [("xo", i % 7)])

    def mp0(i):
        s_ = osel[i % 4]
        for hf in range(2):
            pi = next_ps(0, 6)
            mp_ps[(i, hf)] = pi
            for k in range(8):
                mm(psf[pi][:, :], s_[:, k, :], wout[:, k, hf * 512:(hf + 1) * 512], k == 0, k == 7, [("osel", i % 4), ("wout", k // 4)], [("ps", pi)])

    def mp1(i):
        ybuf = ybufs[i % 4]
        for hf in range(2):
            pi = mp_ps[(i, hf)]
            op("act", "activation", [("ps", pi)], [("ybuf", i % 4, hf)], out=ybuf[:, hf * 512:(hf + 1) * 512], in_=psf[pi][:, :], func=AF.Copy)
        ky = [("ybuf", i % 4, 0), ("ybuf", i % 4, 1)]
        op("act", "activation", ky, [("sq2", 0), ("rs1", 2 * i)], out=sq2s[0][:, :], in_=ybuf[:, :], func=AF.Square, accum_out=rs1[:, 2 * i:2 * i + 1])

    def mp2(i):
        op("dve", "reduce_sum", [("sq2", 0)], [("rs1", 2 * i)], out=rs1[:, 2 * i:2 * i + 1], in_=sq2s[0][:, :], axis=AX.X)

    def mp3(i):
        ka = ("rs1", 2 * i)
        op("act", "activation", [ka, "eps_t"], [ka], out=rs1[:, 2 * i:2 * i + 1], in_=rs1[:, 2 * i:2 * i + 1], func=AF.Ln, bias=eps_t[:, 0:1], scale=1.0 / D)
        op("act", "activation", [ka], [ka], out=rs1[:, 2 * i:2 * i + 1], in_=rs1[:, 2 * i:2 * i + 1], func=AF.Exp, scale=-0.5)

    def mp4(i):
        ybuf = ybufs[i % 4]
        ky = [("ybuf", i % 4, 0), ("ybuf", i % 4, 1)]
        ka = ("rs1", 2 * i)
        x1 = x1b[i % 5]
        kx1 = ("x1b", i % 5)
        op("dve", "scalar_tensor_tensor", ky + [ka, "g1"], ky, out=ybuf[:, :], in0=ybuf[:, :], scalar=rs1[:, 2 * i:2 * i + 1], in1=g1[:, :], op0=ALU.mult, op1=ALU.mult)
        op("dve", "tensor_tensor", ky + [("xo", i % 7)], [kx1], out=x1[:, :], in0=ybuf[:, :], in1=xo[i % 7][:, :], op=ALU.add)
        dma("sp", x1_scr[i * 128:(i + 1) * 128, :], x1[:, :], [kx1], [("x1s", i)])

    def mp5(i):
        op("act", "activation", [("x1b", i % 5)], [("sq2", 1), ("rs1", 2 * i + 1)], out=sq2s[1][:, :], in_=x1b[i % 5][:, :], func=AF.Square,
           accum_out=rs1[:, 2 * i + 1:2 * i + 2])

    def mp6(i):
        op("dve", "reduce_sum", [("sq2", 1)], [("rs1", 2 * i + 1)], out=rs1[:, 2 * i + 1:2 * i + 2], in_=sq2s[1][:, :], axis=AX.X)

    def mp7(i):
        kb_ = ("rs1", 2 * i + 1)
        op("act", "activation", [kb_, "eps_t"], [kb_], out=rs1[:, 2 * i + 1:2 * i + 2], in_=rs1[:, 2 * i + 1:2 * i + 2], func=AF.Ln, bias=eps_t[:, 0:1], scale=1.0 / D)
        op("act", "activation", [kb_], [kb_], out=rs1[:, 2 * i + 1:2 * i + 2], in_=rs1[:, 2 * i + 1:2 * i + 2], func=AF.Exp, scale=-0.5)

    def mp8(i):
        h2 = hb2[i % 2]
        kh2 = ("hb2", i % 2)
        op("dve", "scalar_tensor_tensor", [("x1b", i % 5), ("rs1", 2 * i + 1), "g2"], [kh2], out=h2[:, :], in0=x1b[i % 5][:, :],
           scalar=rs1[:, 2 * i + 1:2 * i + 2], in1=g2[:, :], op0=ALU.mult, op1=ALU.mult)
        for k in range(8):
            tr(psb[:, k * 128:(k + 1) * 128], h2[:, k * 128:(k + 1) * 128], ident_b[:, :], [kh2, "ident_b"], [PSB])

    def mp9(i):
        op("act", "activation", [PSB], [("h2T", i)], out=h2T[:, :, i * 128:(i + 1) * 128], in_=psb[:, :].rearrange("p (k t) -> p k t", k=8), func=AF.Copy)

    run_stages(OWN_T, [mp_ld, lambda i: None, mp0, mp1, mp3, mp4, mp5, mp7, mp8, mp9])
    A.peak = max(getattr(A, "peak", 0), A.off)
    if A.off > WOUT_OFF:
        raise RuntimeError("post scope overlaps wout")
    S.barrier()
    A.release(mark_post)

    aT = A("aT", [128, NFC, 1024], BF16)
    wdn = A("wdn", [128, NFC, D], BF16)
    wub = [A(f"wub{i}", [128, 8, 256], BF16) for i in range(3)]
    cvb = [A(f"cvb{i}", [128, 512], F32) for i in range(6)]
    Gb = [A(f"Gb{i}", [128, 514], F32) for i in range(4)]
    git = [0]
    ggbs = [A(f"ggb{i}", [128, 512], F32) for i in range(3)]
    carry = A("carry", [128, 2 * NFC, 2], F32)
    cwf = A("cwf", [128, 2 * NFC, 3], F32)
    y2bs = [A(f"y2b{i}", [128, D], F32) for i in range(4)]
    x1r = [A(f"x1r{i}", [128, D], F32) for i in range(2)]
    sq3 = A("sq3", [128, D], F32)
    outb = [A(f"outb{i}", [128, D], F32) for i in range(2)]
    g3 = A("g3", [128, D], F32)
    rs3 = A("rs3", [128, 16], F32)
    dma("sp", cwf[:, :, :], cwf_d, [], ["cwf"])
    dma("sp", g3[:, :], gam[3], [], ["g3"])
    h2all = [("h2T", i) for i in range(OWN_T)]
    pairs = [(half, j) for half in range(2) for j in range(NFC)]

    def load_wub(pidx):
        half, j = pairs[pidx]
        wb = wub[pidx % 3]
        kwb = ("wub", pidx % 3)
        dma("pool", wb[:, :, :], w_up_d[j], [], [kwb])

    load_wub(0)
    load_wub(1)
    wdn_v = w_dn_d.rearrange("(j p) n -> p j n", p=128)
    wdn_loads = [(lambda j0=j0: dma("pool", wdn[:, j0:j0 + 2, :], wdn_v[:, j0:j0 + 2, :], [], [("wdn", j0 // 2)]))
                 for j0 in range(0, NFC, 2)]
    fit = [0]
    dn_ps = {}

    def dn0(ot):
        tb = ot % 8
        for hf in range(2):
            pi = next_ps(0, 6)
            dn_ps[(ot, hf)] = pi
            for j in range(NFC):
                mm(psf[pi][:, :], aT[:, j, tb * 128:(tb + 1) * 128], wdn[:, j, hf * 512:(hf + 1) * 512], j == 0, j == NFC - 1,
                   [("aT", j, tb // 4), ("wdn", j // 2)], [("ps", pi)])

    def dn1(ot):
        y2b = y2bs[ot % 4]
        for hf in range(2):
            pi = dn_ps[(ot, hf)]
            op("act", "activation", [("ps", pi)], [("y2b", ot % 4, hf)], out=y2b[:, hf * 512:(hf + 1) * 512], in_=psf[pi][:, :], func=AF.Copy)
        ky = [("y2b", ot % 4, 0), ("y2b", ot % 4, 1)]
        op("act", "activation", ky, ["sq3", ("rs3", ot)], out=sq3[:, :], in_=y2b[:, :], func=AF.Square, accum_out=rs3[:, ot:ot + 1])

    def dn2(ot):
        op("dve", "reduce_sum", ["sq3"], [("rs3", ot)], out=rs3[:, ot:ot + 1], in_=sq3[:, :], axis=AX.X)

    def dn3(ot):
        kr = ("rs3", ot)
        dma("sp", x1r[ot % 2][:, :], x1_scr[(1 + ot) * 128:(2 + ot) * 128, :], [("x1s", 1 + ot)], [("x1r", ot % 2)])
        op("act", "activation", [kr, "eps_t"], [kr], out=rs3[:, ot:ot + 1], in_=rs3[:, ot:ot + 1], func=AF.Ln, bias=eps_t[:, 0:1], scale=1.0 / D)
        op("act", "activation", [kr], [kr], out=rs3[:, ot:ot + 1], in_=rs3[:, ot:ot + 1], func=AF.Exp, scale=-0.5)

    def dn4(ot):
        y2b = y2bs[ot % 4]
        ky = [("y2b", ot % 4, 0), ("y2b", ot % 4, 1)]
        kr = ("rs3", ot)
        op("dve", "scalar_tensor_tensor", ky + [kr, "g3"], ky, out=y2b[:, :], in0=y2b[:, :], scalar=rs3[:, ot:ot + 1], in1=g3[:, :], op0=ALU.mult, op1=ALU.mult)
        ob_ = outb[ot % 2]
        op("dve", "tensor_tensor", ky + [("x1r", ot % 2)], [("outb", ot % 2)], out=ob_[:, :], in0=y2b[:, :], in1=x1r[ot % 2][:, :], op=ALU.add)
        dma("sp", out_d[ot * 128:(ot + 1) * 128, :], ob_[:, :], [("outb", ot % 2)], [("out", ot)])

    for pidx, (half, j) in enumerate(pairs):
        wb = wub[pidx % 3]
        kwb = ("wub", pidx % 3)
        if pidx + 2 < len(pairs):
            load_wub(pidx + 2)
        if half == 0 and pidx % 2 == 0 and wdn_loads:
            wdn_loads.pop(0)()
        if half == 0:
            for which in range(2):
                chn = j + NFC * which
                pi = next_ps(0, 6)
                for k in range(8):
                    mm(psf[pi][:, 0:2], wb[:, k, which * 128:(which + 1) * 128], h2T[:, k, 126:128], k == 0, k == 7, [kwb] + h2all, [("ps", pi)])
                op("dve", "tensor_copy", [("ps", pi)], [("carry", chn)], out=carry[:, chn, :], in_=psf[pi][:, 0:2])
        for t in range(2):
            tg = 2 * half + t
            tcols = slice(128 + tg * 512, 128 + (tg + 1) * 512)
            par = fit[0] % 3
            fit[0] += 1
            for which in range(2):
                chn = j + NFC * which
                pi = next_ps(0, 6)
                kp = ("ps", pi)
                for k in range(8):
                    mm(psf[pi][:, :], wb[:, k, which * 128:(which + 1) * 128], h2T[:, k, tcols], k == 0, k == 7, [kwb] + h2all, [kp])
                cv = cvb[which * 3 + par]
                kcv = ("cvb", which * 3 + par)
                kc = ("carry", chn)
                gi = which * 2 + (git[0] % 2)
                G = Gb[gi]
                kG = ("Gb", gi)
                op("act", "activation", [kc], [(kG, "h")], out=G[:, 0:2], in_=carry[:, chn, :], func=AF.Copy)
                op("act", "activation", [kp], [(kG, "m")], out=G[:, 2:514], in_=psf[pi][:, :], func=AF.Copy)
                op("act", "activation", [kp, "cwf"], [kcv], out=cv[:, :], in_=psf[pi][:, :], func=AF.Copy, scale=cwf[:, chn, 2:3])
                op("dve", "tensor_copy", [(kG, "m")], [kc], out=carry[:, chn, :], in_=G[:, 512:514])
                op("dve", "scalar_tensor_tensor", [(kG, "h"), (kG, "m"), "cwf", kcv], [kcv], out=cv[:, :], in0=G[:, 1:513], scalar=cwf[:, chn, 1:2],
                   in1=cv[:, :], op0=ALU.mult, op1=ALU.add)
                op("dve", "scalar_tensor_tensor", [(kG, "h"), (kG, "m"), "cwf", kcv], [kcv], out=cv[:, :], in0=G[:, 0:512], scalar=cwf[:, chn, 0:1],
                   in1=cv[:, :], op0=ALU.mult, op1=ALU.add)
            git[0] += 1
            ggb = ggbs[par]
            op("act", "activation", [("cvb", 0 * 3 + par)], [("ggb", par)], out=ggb[:, :], in_=cvb[0 * 3 + par][:, :], func=AF.Gelu_apprx_tanh)
            op("dve", "tensor_tensor", [("ggb", par), ("cvb", 1 * 3 + par)], [("aT", j, t)], out=aT[:, j, t * 512:(t + 1) * 512], in0=ggb[:, :],
               in1=cvb[1 * 3 + par][:, :], op=ALU.mult)
        if j == NFC - 1:
            run_stages(8, [lambda tb, half=half: dn0(half * 8 + tb), lambda tb, half=half: dn1(half * 8 + tb),
                           lambda tb, half=half: dn3(half * 8 + tb),
                           lambda tb, half=half: dn4(half * 8 + tb)])
    A.peak = max(getattr(A, "peak", 0), A.off)
    S.barrier()
    n_ops = S.emit(nc)
    return nc, n_ops


def _t5_bucket(d):
    d = np.asarray(d)
    max_exact = 16
    df = np.maximum(d, 1).astype(np.float32)
    large = max_exact + (np.log(df / np.float32(max_exact)) / np.float32(math.log(128 / max_exact)) * np.float32(32 - max_exact)).astype(np.int32)
    return np.where(d < max_exact, d, np.minimum(large, 31)).astype(np.int64)


def _prep_common(inp):
    f = np.float32
    w_in = np.asarray(inp["w_in"][0], f)
    rel = np.asarray(inp["rel_bias"], f)
    c = {}
    c["gam"] = np.ascontiguousarray(np.stack([
        np.broadcast_to(np.asarray(inp[k][0], f)[None, :], (128, D))
        for k in ("norm_mix_pre", "norm_mix_post", "norm_ffn_pre", "norm_ffn_post")]))
    kc0, vc0 = 512, 640
    c["w_kvc"] = np.ascontiguousarray(np.concatenate(
        [w_in[:, kc0:kc0 + 64], w_in[:, vc0:vc0 + 64], w_in[:, kc0 + 64:kc0 + 128], w_in[:, vc0 + 64:vc0 + 128]], axis=1))
    wg = []
    for g in range(2):
        gcols = np.array([1280 + g * 12 + r * 3 + b for r in range(4) for b in range(3)])
        parts = [w_in[:, g * 256:(g + 1) * 256],
                 w_in[:, 768 + g * 64:768 + (g + 1) * 64],
                 w_in[:, 1024 + g * 64:1024 + (g + 1) * 64],
                 w_in[:, 896 + g * 64:896 + (g + 1) * 64],
                 w_in[:, 1152 + g * 64:1152 + (g + 1) * 64],
                 w_in[:, gcols],
                 w_in[:, 1816 + g * 256:1816 + (g + 1) * 256],
                 w_in[:, 2328 + g * 256:2328 + (g + 1) * 256],
                 w_in[:, 1304 + g * 256:1304 + (g + 1) * 256]]
        wg.append(np.concatenate(parts, axis=1))
    c["w_g"] = np.ascontiguousarray(np.stack(wg))
    w1k = np.asarray(inp["w_cmp_k1"][0], f).reshape(32, 64, 256).transpose(1, 0, 2)
    w1v = np.asarray(inp["w_cmp_v1"][0], f).reshape(32, 64, 256).transpose(1, 0, 2)
    c["w1kv"] = np.ascontiguousarray(np.concatenate([w1k, w1v], axis=0))
    c["pekv"] = np.ascontiguousarray(np.concatenate([np.asarray(inp["pe_cmp_k"][0], f).T, np.asarray(inp["pe_cmp_v"][0], f).T], axis=0))
    w2k = np.asarray(inp["w_cmp_k2"][0], f).reshape(2, 128, 64).transpose(1, 0, 2)
    w2v = np.asarray(inp["w_cmp_v2"][0], f).reshape(2, 128, 64).transpose(1, 0, 2)
    c["w2kv"] = np.ascontiguousarray(np.concatenate([w2k, w2v], axis=1))
    bias_d = rel[:, _t5_bucket(np.arange(0, 8192))]
    s_i = np.arange(128)[:, None]
    t_i = np.arange(128)[None, :]
    bt = np.empty((8, 128, 4, 128), f)
    d0 = t_i - s_i
    bt[:, :, 0, :] = np.where(d0 >= 0, bias_d[:, np.maximum(d0, 0)], f(NEG))
    bt[:, :, 1, :] = bias_d[:, 128 + d0]
    bt[:, :, 2, :] = rel[:, 31][:, None, None]
    bt[:, :, 3, :] = np.where((512 + d0) < 512, rel[:, 31][:, None, None], f(NEG))
    bt10 = bt[:, :, [0, 1, 2, 2, 2, 0, 1, 2, 2, 3], :]
    c["btab"] = np.ascontiguousarray(bt10.transpose(1, 0, 2, 3).reshape(128, 8, 1280))
    c["c31"] = np.ascontiguousarray(np.broadcast_to(rel[:, 31][None, :], (128, 8)))
    c["_bias_d"] = bias_d
    nn = np.arange(256)[:, None]
    jj = np.arange(64)[None, :]
    ov = np.clip(np.minimum(16 * nn + 32, 64 * jj + 64) - np.maximum(16 * nn, 64 * jj), 0, None).astype(f) / 32.0
    ov[255] = 0
    ovm = np.zeros((256, 65), f)
    ovm[:, :64] = ov
    ovm[:255, 64] = 1.0
    c["ovm1"] = np.ascontiguousarray(ovm.reshape(2, 128, 65).transpose(1, 0, 2))
    c["emat"] = np.ascontiguousarray((np.arange(S_LEN)[None, :] // 64 == np.arange(64)[:, None]).astype(f))
    c["ident"] = np.eye(128, dtype=f)
    c["cwm"] = np.ascontiguousarray(np.asarray(inp["conv_mix_w"][0], f).T.reshape(4, 128, 3).transpose(1, 0, 2))
    c["w_out"] = np.ascontiguousarray(np.asarray(inp["w_out"][0], f))
    wu = np.asarray(inp["w_ffn_up"][0], f)
    wu2 = np.stack([wu[:, :D_FF].reshape(8, 128, NFC, 128), wu[:, D_FF:].reshape(8, 128, NFC, 128)], axis=3)
    c["w_up"] = np.ascontiguousarray(wu2.transpose(2, 1, 0, 3, 4).reshape(NFC, 128, 8, 256))
    c["cwf"] = np.ascontiguousarray(np.asarray(inp["ffn_conv_w"][0], f).T.reshape(2 * NFC, 128, 3).transpose(1, 0, 2))
    c["w_dn"] = np.ascontiguousarray(np.asarray(inp["w_ffn_down"][0], f))
    return c


_NC_CACHE = {}


def _frame_tables(bias_d, shift):
    f = np.float32
    tl = np.arange(S_LEN)
    valid = (tl >= shift).astype(f)
    out = {"valid": np.ascontiguousarray(valid.reshape(NT, 128).T)}
    jj = np.arange(64)[None, :]
    cur = (tl // 64)[:, None]
    j0 = shift // 64
    okj = (jj >= j0) & (jj <= cur) & (tl[:, None] >= shift)
    forced = (jj == j0) | (jj == cur) | (jj == cur - 1)
    bsel = np.where(okj, np.where(forced, f(1e4), f(0)), f(NEGB)).astype(f)
    out["bsel"] = np.ascontiguousarray(bsel.reshape(NTG, 4, 128, 64).transpose(0, 2, 1, 3))
    n_i = np.arange(256)[:, None]
    dc = tl[None, :] - 16 * n_i - 31
    okn = (dc >= 0) & (n_i < 255) & (n_i >= shift // 16)
    ct = np.where(okn[None], bias_d[:, np.clip(dc, 0, 8191)], f(NEG)).astype(f)
    out["ctab"] = np.ascontiguousarray(ct.reshape(8, 2, 128, S_LEN))
    return out


def _in_maps(inputs):
    common = _prep_common(inputs)
    bias_d = common.pop("_bias_d")
    frames = [_frame_tables(bias_d, 2048), _frame_tables(bias_d, 0)]
    x = np.asarray(inputs["x"], np.float32)
    maps = []
    for core in range(8):
        b, cpos = core // 2, core % 2
        m = dict(common)
        m.update(frames[cpos])
        if cpos == 0:
            xf = np.zeros((S_LEN, D), np.float32)
            xf[2048:] = x[b, 0:2048]
        else:
            xf = np.ascontiguousarray(x[b])
        m["x_full"] = xf
        maps.append(m)
    return maps


def kernel(**inputs):
    if "nc" not in _NC_CACHE:
        _NC_CACHE["nc"] = build_nc(False)[0]
    nc = _NC_CACHE["nc"]
    maps = _in_maps(inputs)
    res = run_bass_kernel_spmd(nc, maps, core_ids=list(range(8)))
    out = np.empty((4, S_LEN, D), np.float32)
    for core in range(8):
        b, cpos = core // 2, core % 2
        out[b, 2048 * cpos:2048 * (cpos + 1)] = res.results[core]["out_own"]
    return out
```

```python
import contextlib
import math
import numpy as np
import concourse.bass as bass
import concourse.mybir as mybir
from concourse.bass_utils import run_bass_kernel_spmd

F32 = mybir.dt.float32
BF16 = mybir.dt.bfloat16
AF = mybir.ActivationFunctionType
ALU = mybir.AluOpType
AX = mybir.AxisListType

ENGS = ("pe", "act", "dve", "pool", "sp")

S_LEN = 4096
D = 1024
NT = 32
NTG = 8
NEG = -30000.0
NEGB = -1.0e9
D_FF = 2816
NFC = 22
OWN_T = 17
SCR_W = S_LEN
WOUT_OFF = 212608
QGROUPS = ((3, 3), (4, 0), (5, 0), (6, 0), (7, 0))


class Op:
    __slots__ = ("eng", "fn", "deps", "dma", "idx", "need_inc", "semval", "sem", "strong", "eidx")


class Sched:
    def __init__(self):
        self.ops = []
        self.lw = {}
        self.rd = {}
        self.last_eng = {}
        self.dma_since_barrier = []
        self.ecount = {}

    def add(self, eng, fn, r=(), w=(), dma=False):
        op = Op()
        op.eng = eng
        op.fn = fn
        op.dma = dma
        op.idx = len(self.ops)
        op.need_inc = dma
        op.sem = None
        op.semval = 0
        deps = set()
        for k in r:
            if k in self.lw:
                deps.add(self.lw[k])
        op.strong = set(deps)
        op.eidx = self.ecount.get(eng, 0)
        self.ecount[eng] = op.eidx + 1
        for k in w:
            if k in self.lw:
                deps.add(self.lw[k])
            deps.update(self.rd.get(k, ()))
        for k in r:
            self.rd.setdefault(k, set()).add(op.idx)
        for k in w:
            self.lw[k] = op.idx
            self.rd[k] = set()
        op.deps = deps
        self.ops.append(op)
        if dma:
            self.dma_since_barrier.append(op.idx)
        else:
            self.last_eng[eng] = op.idx
        return op.idx

    def barrier(self):
        deps = set(self.last_eng.values()) | set(self.dma_since_barrier)
        self.dma_since_barrier = []
        for eng in ENGS:
            op = Op()
            op.eng = eng
            op.fn = None
            op.dma = False
            op.idx = len(self.ops)
            op.need_inc = False
            op.sem = None
            op.semval = 0
            op.deps = set(deps)
            op.strong = set(deps)
            op.eidx = -1
            self.ops.append(op)
        self.lw = {}
        self.rd = {}

    @staticmethod
    def _weak_skip(op, Dp, d):
        return ((not op.dma) and (not Dp.dma) and op.fn is not None and Dp.eng == op.eng
                and d not in op.strong and op.eidx - Dp.eidx >= 2)

    def emit(self, nc, n_dma_sems=64):
        ops = self.ops
        es = contextlib.ExitStack()
        with es:
            dma_sems = [es.enter_context(nc.semaphore(f"dq{i}")) for i in range(n_dma_sems)]
            eng_sems = {e: es.enter_context(nc.semaphore(f"es_{e}")) for e in ENGS}
            dma_last = [None] * n_dma_sems
            dma_cnt = [0] * n_dma_sems
            di = 0
            for op in ops:
                if op.dma:
                    j = di % n_dma_sems
                    di += 1
                    if dma_last[j] is not None:
                        op.deps.add(dma_last[j])
                    dma_last[j] = op.idx
                    dma_cnt[j] += 16
                    op.sem = ("d", j)
                    op.semval = dma_cnt[j]
            for op in ops:
                for d in op.deps:
                    Dp = ops[d]
                    if Dp.dma:
                        continue
                    if Dp.eng == op.eng and Dp.eng == "pe" and not op.dma:
                        continue
                    if self._weak_skip(op, Dp, d):
                        continue
                    Dp.need_inc = True
            cnt = {}
            for op in ops:
                if (not op.dma) and op.need_inc:
                    cnt[op.eng] = cnt.get(op.eng, 0) + 1
                    op.sem = ("e", op.eng)
                    op.semval = cnt[op.eng]

            def semh(s):
                return dma_sems[s[1]] if s[0] == "d" else eng_sems[s[1]]

            def make(eng_name):
                def body(e):
                    waited = {}
                    for op in ops:
                        if op.eng != eng_name:
                            continue
                        need = {}
                        for d in op.deps:
                            Dp = ops[d]
                            if (not Dp.dma) and Dp.eng == eng_name and eng_name == "pe" and not op.dma:
                                continue
                            if self._weak_skip(op, Dp, d):
                                continue
                            if Dp.sem is None:
                                continue
                            if need.get(Dp.sem, 0) < Dp.semval:
                                need[Dp.sem] = Dp.semval
                        for s, v in need.items():
                            if waited.get(s, 0) < v:
                                e.wait_ge(semh(s), v)
                                waited[s] = v
                        if op.fn is None:
                            continue
                        ins = op.fn(e)
                        if op.need_inc:
                            ins.then_inc(semh(op.sem), 16 if op.dma else 1)
                return body

            with nc.Block() as block:
                block.tensor(make("pe"))
                block.scalar(make("act"))
                block.vector(make("dve"))
                block.gpsimd(make("pool"))
                block.sync(make("sp"))
        return len(ops)


class SbAlloc:
    def __init__(self, nc, limit=229000):
        self.nc = nc
        self.off = 16384
        self.limit = limit
        self.n = 0

    def __call__(self, name, shape, dtype):
        nb = 4 if dtype == F32 else 2
        sz = int(np.prod(shape[1:])) * nb
        sz = (sz + 63) // 64 * 64
        if self.off + sz > self.limit:
            raise RuntimeError(f"SBUF overflow allocating {name}: {self.off}+{sz}")
        self.n += 1
        t = self.nc.alloc_sbuf_tensor_at(f"{name}_{self.n}", list(shape), dtype, offset=self.off)
        self.off += sz
        return t

    def mark(self):
        return self.off

    def release(self, m):
        self.off = m


def build_nc(debug=False):
    nc = bass.Bass("TRN2", target_bir_lowering=False)
    okind = "ExternalOutput" if debug else "Internal"

    def din(name, shape, dt=F32):
        return nc.dram_tensor(name, list(shape), dt, kind="ExternalInput").ap()

    x_full = din("x_full", [S_LEN, D])
    valid_d = din("valid", [128, NT])
    gam = din("gam", [4, 128, D])
    w_kvc = din("w_kvc", [D, 256])
    w_g = din("w_g", [2, D, 1292])
    w1kv_d = din("w1kv", [128, 32, 256])
    pekv_d = din("pekv", [128, 32])
    w2kv_d = din("w2kv", [128, 4, 64])
    btab_d = din("btab", [128, 8, 1280])
    c31_d = din("c31", [128, 8])
    ctab_d = din("ctab", [8, 2, 128, S_LEN])
    ovm_d = din("ovm1", [128, 2, 65])
    bsel_d = din("bsel", [NTG, 128, 4, 64])
    emat_d = din("emat", [64, S_LEN])
    ident_d = din("ident", [128, 128])
    cwm_d = din("cwm", [128, 4, 3])
    w_out_d = din("w_out", [D, D])
    w_up_d = din("w_up", [NFC, 128, 8, 256])
    cwf_d = din("cwf", [128, 2 * NFC, 3])
    w_dn_d = din("w_dn", [D_FF, D])
    out_d = nc.dram_tensor("out_own", [2048, D], F32, kind="ExternalOutput").ap()

    hT_scr = nc.dram_tensor("hT_scr", [D, S_LEN], BF16, kind=okind).ap()
    oT_scr = nc.dram_tensor("oT_scr", [D, SCR_W], BF16, kind=okind).ap()
    x1_scr = nc.dram_tensor("x1_scr", [OWN_T * 128, D], F32, kind=okind).ap()

    A = SbAlloc(nc)
    S = Sched()

    def op(eng, meth, r, w, *args, **kw):
        S.add(eng, lambda e: getattr(e, meth)(*args, **kw), r=r, w=w)

    def dma(eng, out, in_, r, w):
        S.add(eng, lambda e: e.dma_start(out=out, in_=in_), r=r, w=w, dma=True)

    def mm(out, lhsT, rhs, start, stop, r, w, skip=False):
        S.add("pe", lambda e: e.matmul(out, lhsT, rhs, start=start, stop=stop, skip_group_check=skip), r=r, w=w)

    def tr(out, in_, ident, r, w):
        S.add("pe", lambda e: e.transpose(out=out, in_=in_, identity=ident), r=r, w=w)

    def run_stages(n, stages):
        ns = len(stages)
        for step in range(n + ns - 1):
            for si in range(ns - 1, -1, -1):
                i = step - si
                if 0 <= i < n:
                    stages[si](i)

    psall = nc.alloc_psum_tensor("psall", [128, 7 * 512], F32)
    psf = [psall[:, i * 512:(i + 1) * 512] for i in range(7)]
    psb = nc.alloc_psum_tensor("psb", [128, 1024], BF16)
    PSB = ("ps", 7)
    psb_alt = psall[:, 0:512].bitcast(BF16)
    psrot = [0]

    def next_ps(lo, hi):
        i = lo + psrot[0] % (hi - lo)
        psrot[0] += 1
        return i

    ident_b = A("ident_b", [128, 128], BF16)
    dma("pool", ident_b[:, :], ident_d, [], ["ident_b"])
    gbuf = A("gbuf", [128, D], F32)
    small = A("small", [128, 64], F32)
    eps_t = A("eps_t", [128, 1], F32)
    op("dve", "memset", [], ["eps_t"], eps_t[:, :], 1e-6)

    def rms_rstd(src, key_src, dst, key_dst, sq, key_sq):
        op("act", "activation", [key_src], [key_sq], out=sq, in_=src, func=AF.Square)
        op("dve", "reduce_sum", [key_sq], [key_dst], out=dst, in_=sq, axis=AX.X)
        op("act", "activation", [key_dst, "eps_t"], [key_dst], out=dst, in_=dst, func=AF.Ln, bias=eps_t[:, 0:1], scale=1.0 / D)
        op("act", "activation", [key_dst], [key_dst], out=dst, in_=dst, func=AF.Exp, scale=-0.5)

    mark_main = A.mark()

    hi = 150016
    Wc = nc.alloc_sbuf_tensor_at("Wc_hi", [128, 8, 256], BF16, offset=hi)
    w1kv = nc.alloc_sbuf_tensor_at("w1kv_hi", [128, 32, 256], BF16, offset=hi + 4096)
    pekv = nc.alloc_sbuf_tensor_at("pekv_hi", [128, 32], BF16, offset=hi + 4096 + 16384)
    w2kv = nc.alloc_sbuf_tensor_at("w2kv_hi", [128, 4, 64], BF16, offset=hi + 4096 + 16384 + 64)
    dma("pool", Wc[:, :, :], w_kvc.rearrange("(k p) n -> p k n", p=128), [], ["Wc"])
    for q4 in range(4):
        dma("pool", w1kv[:, q4 * 8:(q4 + 1) * 8, :], w1kv_d[:, q4 * 8:(q4 + 1) * 8, :], [], [("w1kv", q4)])
    dma("pool", pekv[:, :], pekv_d, [], ["pekv"])
    dma("pool", w2kv[:, :, :], w2kv_d, [], ["w2kv"])

    xin = [A(f"xin{i}", [128, D], F32) for i in range(8)]
    sqbs = [A(f"sqb{i}", [128, D], F32) for i in range(2)]
    hb = [A(f"hb{i}", [128, D], BF16) for i in range(2)]
    hstage = [A(f"hst{i}", [128, 8, 512], BF16) for i in range(2)]
    rs0 = A("rs0", [128, NT], F32)
    dma("sp", gbuf[:, :], gam[0], [], ["gbuf"])
    hT_v = hT_scr.rearrange("(k p) t -> p k t", p=128)

    def p0_a(i):
        dma("sp", xin[i % 8][:, :], x_full[i * 128:(i + 1) * 128, :], [], [("xin", i % 8)])

    def p0_b(i):
        op("act", "activation", [("xin", i % 8)], [("sqb", i % 2), ("rs0", i)], out=sqbs[i % 2][:, :], in_=xin[i % 8][:, :], func=AF.Square,
           accum_out=rs0[:, i:i + 1])

    def p0_c(i):
        op("dve", "reduce_sum", [("sqb", i % 2)], [("rs0", i)], out=rs0[:, i:i + 1], in_=sqbs[i % 2][:, :], axis=AX.X)

    def p0_d(i):
        kr = ("rs0", i)
        op("act", "activation", [kr, "eps_t"], [kr], out=rs0[:, i:i + 1], in_=rs0[:, i:i + 1], func=AF.Ln, bias=eps_t[:, 0:1], scale=1.0 / D)
        op("act", "activation", [kr], [kr], out=rs0[:, i:i + 1], in_=rs0[:, i:i + 1], func=AF.Exp, scale=-0.5)

    def p0_e(i):
        hbi = hb[i % 2]
        kh = ("hb", i % 2)
        op("dve", "scalar_tensor_tensor", [("xin", i % 8), ("rs0", i), "gbuf"], [kh], out=hbi[:, :], in0=xin[i % 8][:, :],
           scalar=rs0[:, i:i + 1], in1=gbuf[:, :], op0=ALU.mult, op1=ALU.mult)
        tb_, ktb = (psb, PSB) if i % 2 == 0 else (psb_alt, ("ps", 0))
        for k in range(8):
            tr(tb_[:, k * 128:(k + 1) * 128], hbi[:, k * 128:(k + 1) * 128], ident_b[:, :], [kh, "ident_b"], [ktb])

    def p0_f(i):
        tg = i // 4
        hs = hstage[tg % 2]
        ks = ("hst", tg % 2, i % 4)
        hdst = hs[:, :, (i % 4) * 128:(i % 4 + 1) * 128]
        tb_, ktb = (psb, PSB) if i % 2 == 0 else (psb_alt, ("ps", 0))
        hsrc = tb_[:, :].rearrange("p (k t) -> p k t", k=8)
        if i % 2 == 0:
            op("act", "activation", [ktb], [ks], out=hdst, in_=hsrc, func=AF.Copy)
        else:
            op("dve", "tensor_copy", [ktb], [ks], out=hdst, in_=hsrc)
        if i % 4 == 3:
            dma("sp", hT_v[:, :, tg * 512:(tg + 1) * 512], hs[:, :, :], [("hst", tg % 2, j) for j in range(4)], [("hT", tg)])

    run_stages(NT, [p0_a, lambda i: None, lambda i: None, p0_b, p0_d, p0_e, p0_f])
    A.peak = max(getattr(A, "peak", 0), A.off)
    if A.off > WOUT_OFF and len(S.ops) > 3000 and len(S.ops) < 16000:
        raise RuntimeError("attention scope overlaps wout")
    S.barrier()
    A.release(mark_main)

    hbuf = [A(f"hbuf{i}", [128, 8, 512], BF16) for i in range(2)]
    KCT = [A(f"KCT{g}", [64, 256], BF16) for g in range(2)]
    CV = [A(f"CV{g}", [128, 2, 128], BF16) for g in range(2)]
    cbias = A("cbias", [128, 4], F32)

    def load_hbuf(tg, it):
        hbq = hbuf[it % 2]
        dma("sp", hbq[:, :, :], hT_v[:, :, tg * 512:(tg + 1) * 512], [("hT", tg)], [("hbuf", it % 2)])
        return hbq, ("hbuf", it % 2)

    hit = [0]
    mark_1a = A.mark()
    KVC = [A(f"KVC{g}", [128, 16, 256], BF16) for g in range(2)]
    HID = A("HID", [128, 4, 256], BF16)
    for g in range(2):
        op("pool", "memset", [], [("CV", g)], CV[g][:, :, :], 0.0)
        op("pool", "memset", [], [("KCT", g)], KCT[g][:, :], 0.0)
    for g in range(2):
        dma("pool", CV[g][:, :, 0:64], ovm_d[:, :, 0:64], [("CV", g)], [("CV", g)])
    for tg in range(NTG):
        hbq, khb = load_hbuf(tg, hit[0]); hit[0] += 1
        for g in range(2):
            pi = next_ps(0, 6)
            for k in range(8):
                mm(psf[pi][:, :], Wc[:, k, g * 128:(g + 1) * 128], hbq[:, k, :], k == 0, k == 7, ["Wc", khb], [("ps", pi)])
            op("act", "activation", [("ps", pi)], [("KVC", g, tg)], out=KVC[g][:, :, tg * 32:(tg + 1) * 32], in_=psf[pi][:, :].rearrange("p (m f) -> p f m", f=16), func=AF.Copy)
    w1keys = [("w1kv", q4) for q4 in range(4)]
    for g in range(2):
        kvall = [("KVC", g, tg) for tg in range(NTG)]
        pi = next_ps(0, 6)
        for kvi in range(2):
            rows = slice(kvi * 64, (kvi + 1) * 64)
            for hc in range(2):
                col = kvi * 2 + hc
                for l in range(32):
                    mm(psf[pi][:, col:col + 1], w1kv[rows, l, hc * 128:(hc + 1) * 128], pekv[rows, l:l + 1],
                       l == 0 and col == 0, l == 31, w1keys + ["pekv"], [("ps", pi)], skip=True)
        op("dve", "tensor_copy", [("ps", pi)], ["cbias"], out=cbias[:, :], in_=psf[pi][:, 0:4])
        for kvi in range(2):
            rows = slice(kvi * 64, (kvi + 1) * 64)
            for hc in range(2):
                col = kvi * 2 + hc
                pi = next_ps(0, 6)
                for l in range(32):
                    mm(psf[pi][:, 0:255], w1kv[rows, l, hc * 128:(hc + 1) * 128], KVC[g][rows, l % 16, l // 16:l // 16 + 255],
                       l == 0, l == 31, w1keys + kvall, [("ps", pi)])
                op("act", "activation", [("ps", pi), "cbias"], [("HID", col)], out=HID[:, col, 0:255], in_=psf[pi][:, 0:255],
                   func=AF.Gelu_apprx_tanh, bias=cbias[:, col:col + 1], scale=1.0)
        pi = next_ps(0, 6)
        for hc in range(2):
            mm(psf[pi][0:64, 0:255], w2kv[:, hc, :], HID[:, hc, 0:255], hc == 0, hc == 1, ["w2kv", ("HID", hc)], [("ps", pi)])
        op("act", "activation", [("ps", pi)], [("KCT", g)], out=KCT[g][:, 0:255], in_=psf[pi][0:64, 0:255], func=AF.Copy)
        for nt in range(2):
            rows_n = 128 if nt == 0 else 127
            pi = next_ps(0, 6)
            for hc in range(2):
                mm(psf[pi][0:rows_n, 0:64], HID[:, 2 + hc, nt * 128:nt * 128 + rows_n], w2kv[:, 2 + hc, :], hc == 0, hc == 1,
                   ["w2kv", ("HID", 2 + hc)], [("ps", pi)])
            op("act", "activation", [("ps", pi)], [("CV", g)], out=CV[g][0:rows_n, nt, 64:128], in_=psf[pi][0:rows_n, 0:64], func=AF.Copy)
    A.peak = max(getattr(A, "peak", 0), A.off)
    S.barrier()
    A.release(mark_1a)

    Wg = A("Wg", [128, 8, 1292], BF16)
    QN = A("QN", [128, 4, S_LEN], BF16)
    KE = A("KE", [128, S_LEN], BF16)
    KW = A("KW", [64, S_LEN], BF16)
    V1s = A("V1s", [128, NT, 65], BF16)
    V1w = A("V1w", [128, NT, 65], BF16)
    btab2 = [A(f"btab{i}", [128, 4, 1280], BF16) for i in range(2)]
    wout = nc.alloc_sbuf_tensor_at("wout_hi", [128, 8, D], BF16, offset=WOUT_OFF)
    c31 = A("c31", [128, 8], F32)
    gsig = A("gsig", [128, NT, 12], F32)
    bsel = [A(f"bsel{i}", [128, 4, 64], F32) for i in range(2)]
    ctb = [A(f"ctb{i}", [128, 512], BF16) for i in range(16)]
    EC = [A(f"EC{i}", [128, 2, 512], BF16) for i in range(3)]
    PT = [A(f"PT{i}", [128, 512], BF16) for i in range(4)]
    PT2 = [A(f"PTW{i}", [128, 1024], BF16) for i in range(3)]
    p2it = [0]
    pbit = [0]
    onsa2 = [A(f"onsa{i}", [128, 4, 256], F32) for i in range(3)]
    onsb = A("onsb", [128, 4, 256], BF16)
    ostage = A("ostage", [128, 2, 512], BF16)
    impacc = A("impacc", [128, 4, 64], F32)
    sc1 = A("sc1", [128, 64], F32)
    sc2 = A("sc2", [128, 64], F32)
    m8a = A("m8a", [128, 8], F32)
    m8b = A("m8b", [128, 8], F32)
    nsb4 = A("nsb4", [128, 4, 128], BF16)
    zbuf = A("zbuf", [128, 2, 514], F32)
    tmpc = A("tmpc", [128, 512], F32)
    cacc = A("cacc", [128, 512], F32)
    convo = [A(f"convo{i}", [128, 512], BF16) for i in range(2)]
    cwm = A("cwm", [128, 4, 3], F32)
    zero_b = A("zero_b", [128, 128], BF16)
    nsT = A("nsT", [128, 512], BF16)

    dma("sp", c31[:, :], c31_d, [], ["c31"])
    dma("sp", cwm[:, :, :], cwm_d, [], ["cwm"])
    dma("pool", KE[64:128, :], emat_d, [], ["KE_E"])
    vld = A("vld", [128, NT], F32)
    dma("sp", vld[:, :], valid_d, [], ["vld"])
    op("pool", "tensor_copy", ["vld"], ["V1s1"], out=V1s[:, :, 64], in_=vld[:, :])
    op("pool", "tensor_copy", ["vld"], ["V1w1"], out=V1w[:, :, 64], in_=vld[:, :])
    for i2 in range(3):
        op("pool", "memset", [], [("onsa", i2, qb, r) for qb in range(4) for r in range(4)], onsa2[i2][:, :, :], 0.0)
    op("pool", "memset", [], ["nsb0"], nsb4[:, :, 0:64], 0.0)
    op("pool", "memset", [], ["zero_b"], zero_b[:, :], 0.0)
    oT_v = oT_scr.rearrange("(k p) t -> p k t", p=128)

    cit = [0]
    pit = [0]
    sm = [0]
    eit = [0]
    obr = [0]

    def smcol():
        c = sm[0] % 64
        sm[0] += 1
        return c

    for g in range(2):
        btab = btab2[g]
        if g == 0:
            dma("pool", Wg[:, :, :], w_g[0].rearrange("(k p) n -> p k n", p=128), [], ["Wg"])
            dma("pool", btab2[0][:, :, :], btab_d[:, 0:4, :], [], [("btab", 0)])
            dma("pool", btab2[1][:, :, :], btab_d[:, 4:8, :], [], [("btab", 1)])
        op("dve", "memset", [], [("zbuf", 0), ("zbuf", 1)], zbuf[:, :, 0:2], 0.0)
        def sm4():
            c = (sm[0] % 16) * 4
            sm[0] += 1
            return c, ("sm4", c)

        oslot = {}

        def run_pipeline(tiles, hooks=None, SK=2):
            n = len(tiles)
            for i in range(n + SK):
                if i < n:
                    tiles[i][0]()
                    tiles[i][1]()
                if i - SK >= 0 and tiles[i - SK][2] is not None:
                    tiles[i - SK][2]()
                if hooks and i in hooks:
                    for hk in hooks[i]:
                        hk()

        gidx = {t_: i_ for i_, (t_, _q2) in enumerate(QGROUPS)}

        def prefetch_ctab(tg, qmin):
            cq_ = qmin * 128
            cols_ = slice(tg * 512 + cq_, (tg + 1) * 512)
            for r_ in range(4):
                for nt_ in ([0] if tg < 4 else [0, 1]):
                    ci_ = (gidx[tg] % 2) * 8 + r_ * 2 + nt_
                    dma("pool", ctb[ci_][:, cq_:512], ctab_d[4 * g + r_, nt_, :, cols_], [], [("ctb", ci_)])

        def comp_tiles(tg, qmin):
            cols = slice(tg * 512 + qmin * 128, (tg + 1) * 512)
            cq = qmin * 128
            onsa = onsa2[oslot[tg]]
            par = oslot[tg]
            bs = bsel[tg % 2]
            kbs = ("bsel", tg % 2)
            dma("sp", bs[:, :, :], bsel_d[tg], [], [kbs])
            nts = [0] if tg < 4 else [0, 1]
            tiles = []
            for r in range(4):
                h = 4 * g + r
                ei = eit[0] % 3
                eit[0] += 1
                ec = EC[ei]
                for nt in nts:
                    st = {}

                    def A_(r=r, h=h, nt=nt, st=st):
                        ci = (gidx[tg] % 2) * 8 + r * 2 + nt
                        pi = next_ps(0, 4)
                        st["pi"] = pi
                        mm(psf[pi][:, cq:512], KCT[g][0:64, nt * 128:(nt + 1) * 128], QN[0:64, r, cols], True, False,
                           [("KCT", g), ("QNq", r, tg)], [("ps", pi)])
                        mm(psf[pi][:, cq:512], ident_b[:, :], ctb[ci][:, cq:512], False, True, ["ident_b", ("ctb", ci)], [("ps", pi)])

                    def B_(nt=nt, st=st, ec=ec, ei=ei):
                        pi = st["pi"]
                        op("act", "activation", [("ps", pi)], [("EC", ei, nt)], out=ec[:, nt, cq:512], in_=psf[pi][:, cq:512], func=AF.Exp)

                    C_ = None
                    if nt == nts[-1]:
                        def C_(r=r, ec=ec, ei=ei):
                            ob = 4 + obr[0] % 3
                            obr[0] += 1
                            kob = ("ps", ob)
                            firstmm = True
                            for qb in range(qmin, 4):
                                for nt2 in nts:
                                    mm(psf[ob][:, qb * 128:(qb + 1) * 128], ec[:, nt2, qb * 128:(qb + 1) * 128], CV[g][:, nt2, :],
                                       firstmm, False, [("EC", ei, nt2), ("CV", g)], [kob], skip=True)
                                    firstmm = False
                            pv4 = psf[ob][:, :].rearrange("p (q c) -> p q c", q=4)
                            c0, k0 = sm4()
                            c1, k1 = sm4()
                            op("dve", "reduce_sum", [kob], [k0], out=small[:, c0:c0 + 4], in_=pv4[:, :, 0:64], axis=AX.X)
                            op("dve", "tensor_scalar", [k0], [k0], out=small[:, c0:c0 + 4], in0=small[:, c0:c0 + 4], scalar1=1e-30, scalar2=None, op0=ALU.max)
                            op("dve", "reciprocal", [k0], [k0], out=small[:, c0:c0 + 4], in_=small[:, c0:c0 + 4])
                            op("dve", "tensor_tensor", [k0] + [("gsig", 4 * tg + j) for j in range(4)], [k1], out=small[:, c1:c1 + 4],
                               in0=small[:, c0:c0 + 4], in1=gsig[:, 4 * tg:4 * tg + 4, r * 3], op=ALU.mult)
                            for qb in range(qmin, 4):
                                if r == 0:
                                    op("dve", "tensor_scalar", [kob, k0], [("imp", qb)], out=impacc[:, qb, :], in0=pv4[:, qb, 0:64],
                                       scalar1=small[:, c0 + qb:c0 + qb + 1], scalar2=None, op0=ALU.mult)
                                else:
                                    op("dve", "scalar_tensor_tensor", [kob, k0, ("imp", qb)], [("imp", qb)], out=impacc[:, qb, :],
                                       in0=pv4[:, qb, 0:64], scalar=small[:, c0 + qb:c0 + qb + 1], in1=impacc[:, qb, :], op0=ALU.mult, op1=ALU.add)
                                op("dve", "tensor_scalar", [kob, k1], [("onsa", par, qb, r)], out=onsa[:, qb, r * 64:(r + 1) * 64],
                                   in0=pv4[:, qb, 64:128], scalar1=small[:, c1 + qb:c1 + qb + 1], scalar2=None, op0=ALU.mult)
                            if r == 3:
                                topk_dve(tg, qmin)
                    tiles.append((A_, B_, C_))
            return tiles

        def topk_dve(tg, qmin):
            bs = bsel[tg % 2]
            kbs = ("bsel", tg % 2)
            for qb in range(qmin, 4):
                op("dve", "tensor_tensor", [("imp", qb), kbs], ["sc1"], out=sc1[:, :], in0=impacc[:, qb, :], in1=bs[:, qb, :], op=ALU.add)
                op("dve", "max", ["sc1"], ["m8a"], out=m8a[:, :], in_=sc1[:, :])
                op("dve", "match_replace", ["sc1", "m8a"], ["sc2"], out=sc2[:, :], in_to_replace=m8a[:, :], in_values=sc1[:, :], imm_value=-1e30)
                op("dve", "max", ["sc2"], ["m8b"], out=m8b[:, :], in_=sc2[:, :])
                op("dve", "tensor_scalar", ["sc1", "m8b", "nsb0"], [("nsb", qb)], out=nsb4[:, qb, 64:128], in0=sc1[:, :], scalar1=m8b[:, 7:8], scalar2=NEG,
                   op0=ALU.is_lt, op1=ALU.mult)

        def topk_pe(tg, qmin):
            for qb in range(qmin, 4):
                tr(psb[:, qb * 128:(qb + 1) * 128], nsb4[:, qb, :], ident_b[:, :], [("nsb", qb), "nsb0", "ident_b"], [PSB])
            cq = qmin * 128
            op("dve", "tensor_copy", [PSB], ["nsT"], out=nsT[64:128, cq:512], in_=psb[64:128, cq:512])
            for r in range(4):
                op("pool", "tensor_copy", ["nsT"], [("QNm", r, tg * 4 + qb) for qb in range(qmin, 4)],
                   out=QN[64:128, r, tg * 512 + cq:(tg + 1) * 512], in_=nsT[64:128, cq:512])

        def evac_o(tg, r, br, ob, qmin=0):
            onsa = onsa2[oslot[tg]]
            par = oslot[tg]
            kob = ("ps", ob)
            c0, k0 = sm4()
            op("dve", "tensor_scalar", [kob], [k0], out=small[:, c0:c0 + 4], in0=psf[ob][:, 64:260:65], scalar1=1e-30, scalar2=None, op0=ALU.max)
            op("dve", "reciprocal", [k0], [k0], out=small[:, c0:c0 + 4], in_=small[:, c0:c0 + 4])
            op("dve", "tensor_tensor", [k0] + [("gsig", 4 * tg + j) for j in range(4)], [k0], out=small[:, c0:c0 + 4],
               in0=small[:, c0:c0 + 4], in1=gsig[:, 4 * tg:4 * tg + 4, r * 3 + br], op=ALU.mult)
            for qb in range(qmin, 4):
                op("dve", "scalar_tensor_tensor", [kob, k0, ("onsa", par, qb, r)], [("onsa", par, qb, r)], out=onsa[:, qb, r * 64:(r + 1) * 64],
                   in0=psf[ob][:, qb * 65:qb * 65 + 64], scalar=small[:, c0 + qb:c0 + qb + 1], in1=onsa[:, qb, r * 64:(r + 1) * 64],
                   op0=ALU.mult, op1=ALU.add)

        def sel_tiles(tg, qmin):
            tiles = []
            for r in range(4):
                h = 4 * g + r
                grp = {"ob": None}
                nkt = 4 * tg + 4
                npair = (4 * tg - 1) // 2 if qmin == 0 else 0
                for pj in range(npair):
                    st = {}
                    kt0 = 2 * pj

                    def A_(r=r, kt0=kt0, st=st):
                        pb = pbit[0] % 2
                        pbit[0] += 1
                        st["pb"] = pb
                        for u in range(2):
                            kt = kt0 + u
                            mm(psf[2 * pb + u][:, :], KE[:, kt * 128:(kt + 1) * 128], QN[:, r, tg * 512:(tg + 1) * 512], True, True,
                               [("KEk", kt // 4), "KE_E", ("QNq", r, tg)] + [("QNm", r, tg * 4 + j) for j in range(4)], [("ps", 2 * pb + u)])

                    def B_(h=h, st=st):
                        pb = st["pb"]
                        x_ = p2it[0] % 3
                        p2it[0] += 1
                        st["pt2"] = x_
                        op("act", "activation", [("ps", 2 * pb), ("ps", 2 * pb + 1), "c31"], [("PT2", x_)], out=PT2[x_][:, :],
                           in_=psall[:, 2 * pb * 512:2 * pb * 512 + 1024], func=AF.Exp, bias=c31[:, h:h + 1], scale=1.0)

                    def C_(r=r, kt0=kt0, st=st, grp=grp):
                        if grp["ob"] is None:
                            grp["ob"] = 4 + obr[0] % 3
                            obr[0] += 1
                            grp["first"] = True
                        ob = grp["ob"]
                        x_ = st["pt2"]
                        for u in range(2):
                            kt = kt0 + u
                            for qb in range(4):
                                mm(psf[ob][:, qb * 65:(qb + 1) * 65], PT2[x_][:, u * 512 + qb * 128:u * 512 + (qb + 1) * 128], V1s[:, kt, :],
                                   grp["first"], False, [("PT2", x_), ("V1s", kt), "V1s1"], [("ps", ob)], skip=True)
                                grp["first"] = False
                    tiles.append((A_, B_, C_))
                for kt in range(2 * npair, nkt):
                    st = {}
                    d = kt - 4 * tg
                    q0 = max(qmin, max(0, d))
                    if q0 > 3:
                        continue
                    c0 = 128 * q0
                    near = d >= -1

                    def A_(r=r, kt=kt, d=d, q0=q0, c0=c0, near=near, st=st):
                        pi = next_ps(0, 4)
                        st["pi"] = pi
                        mm(psf[pi][:, c0:512], KE[:, kt * 128:(kt + 1) * 128], QN[:, r, tg * 512 + c0:(tg + 1) * 512], True, not near,
                           [("KEk", kt // 4), "KE_E", ("QNq", r, tg)] + [("QNm", r, tg * 4 + j) for j in range(q0, 4)], [("ps", pi)])
                        if near:
                            mm(psf[pi][:, c0:512], ident_b[:, :], btab[:, r, (q0 - d) * 128:(4 - d) * 128], False, True,
                               ["ident_b", ("btab", g)], [("ps", pi)])

                    def B_(h=h, c0=c0, near=near, st=st):
                        pi = st["pi"]
                        p_ = pit[0] % 4
                        pit[0] += 1
                        st["pt"] = p_
                        if near:
                            op("act", "activation", [("ps", pi)], [("PT", p_)], out=PT[p_][:, c0:512], in_=psf[pi][:, c0:512], func=AF.Exp)
                        else:
                            op("act", "activation", [("ps", pi), "c31"], [("PT", p_)], out=PT[p_][:, c0:512], in_=psf[pi][:, c0:512], func=AF.Exp,
                               bias=c31[:, h:h + 1], scale=1.0)

                    def C_(r=r, kt=kt, q0=q0, st=st, grp=grp, last=(kt == nkt - 1)):
                        if grp["ob"] is None:
                            grp["ob"] = 4 + obr[0] % 3
                            obr[0] += 1
                            grp["first"] = True
                        ob = grp["ob"]
                        p_ = st["pt"]
                        for qb in range(q0, 4):
                            mm(psf[ob][:, qb * 65:(qb + 1) * 65], PT[p_][:, qb * 128:(qb + 1) * 128], V1s[:, kt, :], grp["first"], False,
                               [("PT", p_), ("V1s", kt), "V1s1"], [("ps", ob)], skip=True)
                            grp["first"] = False
                        if last:
                            evac_o(tg, r, 1, ob, qmin)
                    tiles.append((A_, B_, C_))
            return tiles

        def win_tiles(tg, qmin):
            tiles = []
            for r in range(4):
                grp = {"ob": None}
                ds_ = [d for d in range(-4, 4) if 4 * tg + d >= 0 and max(qmin, max(0, d)) <= min(3, d + 4)]
                for d in ds_:
                    st = {}
                    kt = 4 * tg + d
                    qlo = max(qmin, max(0, d))
                    qhi = min(3, d + 4)
                    c0 = 128 * qlo
                    c1 = 128 * (qhi + 1)

                    def A_(r=r, kt=kt, qlo=qlo, qhi=qhi, c0=c0, c1=c1, st=st):
                        pi = next_ps(0, 4)
                        st["pi"] = pi
                        mm(psf[pi][:, c0:c1], KW[0:64, kt * 128:(kt + 1) * 128], QN[0:64, r, tg * 512 + c0:tg * 512 + c1], True, False,
                           [("KW", kt // 4), ("QNq", r, tg)], [("ps", pi)])
                        dd = kt - 4 * tg
                        mm(psf[pi][:, c0:c1], ident_b[:, :], btab[:, r, 640 + (qlo - dd) * 128:640 + (qhi - dd + 1) * 128], False, True,
                           ["ident_b", ("btab", g)], [("ps", pi)])

                    def B_(c0=c0, c1=c1, st=st):
                        pi = st["pi"]
                        p_ = pit[0] % 4
                        pit[0] += 1
                        st["pt"] = p_
                        op("act", "activation", [("ps", pi)], [("PT", p_)], out=PT[p_][:, c0:c1], in_=psf[pi][:, c0:c1], func=AF.Exp)

                    def C_(r=r, kt=kt, qlo=qlo, qhi=qhi, st=st, grp=grp, last=(d == ds_[-1])):
                        if grp["ob"] is None:
                            grp["ob"] = 4 + obr[0] % 3
                            obr[0] += 1
                            grp["first"] = True
                        ob = grp["ob"]
                        p_ = st["pt"]
                        for qb in range(qlo, qhi + 1):
                            mm(psf[ob][:, qb * 65:(qb + 1) * 65], PT[p_][:, qb * 128:(qb + 1) * 128], V1w[:, kt, :], grp["first"], False,
                               [("PT", p_), ("V1w", kt), "V1w1"], [("ps", ob)], skip=True)
                            grp["first"] = False
                        if last:
                            evac_o(tg, r, 2, ob, qmin)
                    tiles.append((A_, B_, C_))
            return tiles

        def finalize(tg):
            onsa = onsa2[oslot[tg]]
            par = oslot[tg]
            allo = [("onsa", par, qb, r) for qb in range(4) for r in range(4)]
            op("pool", "tensor_copy", allo, ["onsb"], out=onsb[:, :, :], in_=onsa[:, :, :])
            for qb in range(4):
                for cc in range(2):
                    tr(psb[:, (qb * 2 + cc) * 128:(qb * 2 + cc + 1) * 128], onsb[:, qb, cc * 128:(cc + 1) * 128], ident_b[:, :],
                       ["onsb", "ident_b"], [PSB])
            pv = psb[:, :].rearrange("p (q c t) -> p q c t", q=4, c=2)
            for cc in range(2):
                op("act", "activation", [PSB], [("ostage", cc)], out=ostage[:, cc, :].rearrange("p (q t) -> p q t", q=4), in_=pv[:, :, cc, :], func=AF.Copy)
                dma("sp", oT_v[:, 2 * g + cc, tg * 512:(tg + 1) * 512], ostage[:, cc, :], [("ostage", cc)],
                    [("oscr", 2 * g + cc, tg * 4 + j) for j in range(4)])

        for gi_, (tg_, _q) in enumerate(QGROUPS):
            oslot[tg_] = gi_ % 3
        prefetch_ctab(*QGROUPS[0])
        hnext = load_hbuf(0, hit[0]); hit[0] += 1
        for tg in range(NTG):
            hbq, khb = hnext
            if tg + 1 < NTG:
                hnext = load_hbuf(tg + 1, hit[0]); hit[0] += 1
            cols = slice(tg * 512, (tg + 1) * 512)
            for r in range(4 if tg >= QGROUPS[0][0] else 0):
                pi = next_ps(0, 6)
                for k in range(8):
                    mm(psf[pi][0:64, :], Wg[:, k, r * 64:(r + 1) * 64], hbq[:, k, :], k == 0, k == 7, ["Wg", khb], [("ps", pi)])
                op("act", "activation", [("ps", pi)], [("QNq", r, tg)], out=QN[0:64, r, cols], in_=psf[pi][0:64, :], func=AF.Copy, scale=0.125)
            for which, dst, key in ((0, KE, "KEk"), (1, KW, "KW")):
                pi = next_ps(0, 6)
                for k in range(8):
                    mm(psf[pi][0:64, :], Wg[:, k, 256 + which * 64:320 + which * 64], hbq[:, k, :], k == 0, k == 7, ["Wg", khb], [("ps", pi)])
                op("dve", "tensor_copy", [("ps", pi)], [(key, tg)], out=dst[0:64, cols], in_=psf[pi][0:64, :])
            for b in range(4):
                tile_i = tg * 4 + b
                pi = next_ps(0, 6)
                for k in range(8):
                    mm(psf[pi][:, 0:140], hbq[:, k, b * 128:(b + 1) * 128], Wg[:, k, 384:524], k == 0, k == 7, ["Wg", khb], [("ps", pi)])
                op("dve", "tensor_copy", [("ps", pi)], [("V1s", tile_i), ("psx", pi)], out=V1s[:, tile_i, 0:64], in_=psf[pi][:, 0:64])
                op("dve", "tensor_copy", [("ps", pi)], [("V1w", tile_i), ("psx", pi)], out=V1w[:, tile_i, 0:64], in_=psf[pi][:, 64:128])
                op("act", "activation", [("ps", pi)], [("gsig", tile_i), ("psx", pi)], out=gsig[:, tile_i, :], in_=psf[pi][:, 128:140], func=AF.Sigmoid)
            for cc in range(2 if tg >= QGROUPS[0][0] else 0):
                ch = 2 * g + cc
                kz = ("zbuf", cc)
                pc = next_ps(0, 6)
                for k in range(8):
                    mm(psf[pc][:, :], Wg[:, k, 524 + cc * 128:652 + cc * 128], hbq[:, k, :], k == 0, k == 7, ["Wg", khb], [("ps", pc)])
                op("act", "activation", [("ps", pc)], ["tmpc"], out=tmpc[:, :], in_=psf[pc][:, :], func=AF.Copy)
                px = next_ps(0, 6)
                for k in range(8):
                    mm(psf[px][:, :], Wg[:, k, 780 + cc * 128:908 + cc * 128], hbq[:, k, :], k == 0, k == 7, ["Wg", khb], [("ps", px)])
                op("dve", "tensor_tensor", [("ps", px), "tmpc", kz], [kz], out=zbuf[:, cc, 2:514], in0=tmpc[:, :], in1=psf[px][:, :], op=ALU.mult)
                op("dve", "tensor_scalar", [kz, "cwm"], ["cacc"], out=cacc[:, :], in0=zbuf[:, cc, 2:514], scalar1=cwm[:, ch, 2:3], scalar2=None, op0=ALU.mult)
                op("dve", "scalar_tensor_tensor", [kz, "cwm", "cacc"], ["cacc"], out=cacc[:, :], in0=zbuf[:, cc, 1:513], scalar=cwm[:, ch, 1:2],
                   in1=cacc[:, :], op0=ALU.mult, op1=ALU.add)
                op("dve", "scalar_tensor_tensor", [kz, "cwm", "cacc"], ["cacc"], out=cacc[:, :], in0=zbuf[:, cc, 0:512], scalar=cwm[:, ch, 0:1],
                   in1=cacc[:, :], op0=ALU.mult, op1=ALU.add)
                op("dve", "tensor_copy", [kz], [kz], out=zbuf[:, cc, 0:2], in_=zbuf[:, cc, 512:514])
                pb = next_ps(0, 6)
                for k in range(8):
                    mm(psf[pb][:, :], Wg[:, k, 1036 + cc * 128:1164 + cc * 128], hbq[:, k, :], k == 0, k == 7, ["Wg", khb], [("ps", pb)])
                cvo = convo[cit[0] % 2]
                kcv = ("convo", cit[0] % 2)
                cit[0] += 1
                op("dve", "tensor_tensor", [("ps", pb), "cacc"], [kcv], out=cvo[:, :], in0=cacc[:, :], in1=psf[pb][:, :], op=ALU.mult)
                dma("sp", oT_v[:, 4 + ch, tg * 512:(tg + 1) * 512], cvo[:, :], [kcv],
                    [("oscr", 4 + ch, tg * 4 + j) for j in range(4)])
            if tg == QGROUPS[0][0]:
                run_pipeline(comp_tiles(*QGROUPS[0]))
                prefetch_ctab(*QGROUPS[1])

        def late_weights():
            dma("pool", Wg[:, :, :], w_g[1].rearrange("(k p) n -> p k n", p=128), [], ["Wg"])
            for k2 in range(2):
                dma("pool", wout[:, 4 * k2:4 * k2 + 4, :], w_out_d.rearrange("(k p) n -> p k n", p=128)[:, 4 * k2:4 * k2 + 4, :], [], [("wout", k2)])
        topk_pe(*QGROUPS[0])
        for gi_, (tg, qmin) in enumerate(QGROUPS):
            tl = []
            hooks = {}
            if gi_ + 1 < len(QGROUPS):
                tl += comp_tiles(*QGROUPS[gi_ + 1])
            ncomp = len(tl)
            tl += sel_tiles(tg, qmin)
            tl += win_tiles(tg, qmin)
            if gi_ > 0:
                ptg = QGROUPS[gi_ - 1][0]
                hooks.setdefault(min(ncomp + 10, len(tl) - 1), []).append(lambda ptg=ptg: finalize(ptg))
            if gi_ + 1 < len(QGROUPS):
                nxt = QGROUPS[gi_ + 1]
                hooks.setdefault(ncomp + (len(tl) - ncomp) // 2, []).append(lambda nxt=nxt: topk_pe(*nxt))
            if g == 0 and gi_ == len(QGROUPS) - 2:
                hooks.setdefault(ncomp + (len(tl) - ncomp) // 3, []).append(late_weights)
            if gi_ + 2 < len(QGROUPS):
                nn2 = QGROUPS[gi_ + 2]
                hooks.setdefault(ncomp + (len(tl) - ncomp) // 3, []).append(lambda nn2=nn2: prefetch_ctab(*nn2))
            run_pipeline(tl, hooks)
        finalize(QGROUPS[-1][0])
    A.peak = max(getattr(A, "peak", 0), A.off)
    if A.off > WOUT_OFF and len(S.ops) > 3000 and len(S.ops) < 16000:
        raise RuntimeError("attention scope overlaps wout")
    S.barrier()
    A.release(mark_main)

    h2T = A("h2T", [128, 8, OWN_T * 128], BF16)
    mark_post = A.mark()
    osel = [A(f"osel{i}", [128, 8, 128], BF16) for i in range(2)]
    ybufs = [A(f"ybuf{i}", [128, D], F32) for i in range(4)]
    xo = [A(f"xo{i}", [128, D], F32) for i in range(5)]
    x1b = [A(f"x1b{i}", [128, D], F32) for i in range(5)]
    sq2s = [A(f"sq2{i}", [128, D], F32) for i in range(2)]
    hb2 = [A(f"hb2{i}", [128, D], BF16) for i in range(2)]
    g1 = A("g1", [128, D], F32)
    g2 = A("g2", [128, D], F32)
    rs1 = A("rs1", [128, 2 * OWN_T], F32)
    dma("sp", g1[:, :], gam[1], [], ["g1"])
    dma("sp", g2[:, :], gam[2], [], ["g2"])
    mp_ps = {}

    def mp0(i):
        s_ = osel[i % 2]
        dma("sp", s_[:, :, :], oT_v[:, :, (15 + i) * 128:(16 + i) * 128], [], [("osel", i % 2)])
        dma("sp", xo[i % 5][:, :], x_full[(15 + i) * 128:(16 + i) * 128, :], [], [("xo", i % 5)])
        for hf in range(2):
            pi = next_ps(0, 6)
            mp_ps[(i, hf)] = pi
            for k in range(8):
                mm(psf[pi][:, :], s_[:, k, :], wout[:, k, hf * 512:(hf + 1) * 512], k == 0, k == 7, [("osel", i % 2), ("wout", k // 4)], [("ps", pi)])

    def mp1(i):
        ybuf = ybufs[i % 4]
        for hf in range(2):
            pi = mp_ps[(i, hf)]
            op("act", "activation", [("ps", pi)], [("ybuf", i % 4, hf)], out=ybuf[:, hf * 512:(hf + 1) * 512], in_=psf[pi][:, :], func=AF.Copy)
        ky = [("ybuf", i % 4, 0), ("ybuf", i % 4, 1)]
        op("act", "activation", ky, [("sq2", 0), ("rs1", 2 * i)], out=sq2s[0][:, :], in_=ybuf[:, :], func=AF.Square, accum_out=rs1[:, 2 * i:2 * i + 1])

    def mp2(i):
        op("dve", "reduce_sum", [("sq2", 0)], [("rs1", 2 * i)], out=rs1[:, 2 * i:2 * i + 1], in_=sq2s[0][:, :], axis=AX.X)

    def mp3(i):
        ka = ("rs1", 2 * i)
        op("act", "activation", [ka, "eps_t"], [ka], out=rs1[:, 2 * i:2 * i + 1], in_=rs1[:, 2 * i:2 * i + 1], func=AF.Ln, bias=eps_t[:, 0:1], scale=1.0 / D)
        op("act", "activation", [ka], [ka], out=rs1[:, 2 * i:2 * i + 1], in_=rs1[:, 2 * i:2 * i + 1], func=AF.Exp, scale=-0.5)

    def mp4(i):
        ybuf = ybufs[i % 4]
        ky = [("ybuf", i % 4, 0), ("ybuf", i % 4, 1)]
        ka = ("rs1", 2 * i)
        x1 = x1b[i % 5]
        kx1 = ("x1b", i % 5)
        op("dve", "scalar_tensor_tensor", ky + [ka, "g1"], ky, out=ybuf[:, :], in0=ybuf[:, :], scalar=rs1[:, 2 * i:2 * i + 1], in1=g1[:, :], op0=ALU.mult, op1=ALU.mult)
        op("dve", "tensor_tensor", ky + [("xo", i % 5)], [kx1], out=x1[:, :], in0=ybuf[:, :], in1=xo[i % 5][:, :], op=ALU.add)
        dma("sp", x1_scr[i * 128:(i + 1) * 128, :], x1[:, :], [kx1], [("x1s", i)])

    def mp5(i):
        op("act", "activation", [("x1b", i % 5)], [("sq2", 1), ("rs1", 2 * i + 1)], out=sq2s[1][:, :], in_=x1b[i % 5][:, :], func=AF.Square,
           accum_out=rs1[:, 2 * i + 1:2 * i + 2])

    def mp6(i):
        op("dve", "reduce_sum", [("sq2", 1)], [("rs1", 2 * i + 1)], out=rs1[:, 2 * i + 1:2 * i + 2], in_=sq2s[1][:, :], axis=AX.X)

    def mp7(i):
        kb_ = ("rs1", 2 * i + 1)
        op("act", "activation", [kb_, "eps_t"], [kb_], out=rs1[:, 2 * i + 1:2 * i + 2], in_=rs1[:, 2 * i + 1:2 * i + 2], func=AF.Ln, bias=eps_t[:, 0:1], scale=1.0 / D)
        op("act", "activation", [kb_], [kb_], out=rs1[:, 2 * i + 1:2 * i + 2], in_=rs1[:, 2 * i + 1:2 * i + 2], func=AF.Exp, scale=-0.5)

    def mp8(i):
        h2 = hb2[i % 2]
        kh2 = ("hb2", i % 2)
        op("dve", "scalar_tensor_tensor", [("x1b", i % 5), ("rs1", 2 * i + 1), "g2"], [kh2], out=h2[:, :], in0=x1b[i % 5][:, :],
           scalar=rs1[:, 2 * i + 1:2 * i + 2], in1=g2[:, :], op0=ALU.mult, op1=ALU.mult)
        for k in range(8):
            tr(psb[:, k * 128:(k + 1) * 128], h2[:, k * 128:(k + 1) * 128], ident_b[:, :], [kh2, "ident_b"], [PSB])

    def mp9(i):
        op("act", "activation", [PSB], [("h2T", i)], out=h2T[:, :, i * 128:(i + 1) * 128], in_=psb[:, :].rearrange("p (k t) -> p k t", k=8), func=AF.Copy)

    run_stages(OWN_T, [mp0, mp1, mp3, mp4, mp5, mp7, mp8, mp9])
    A.peak = max(getattr(A, "peak", 0), A.off)
    if A.off > WOUT_OFF:
        raise RuntimeError("post scope overlaps wout")
    S.barrier()
    A.release(mark_post)

    aT = A("aT", [128, NFC, 1024], BF16)
    wdn = A("wdn", [128, NFC, D], BF16)
    wub = [A(f"wub{i}", [128, 8, 256], BF16) for i in range(3)]
    cvb = [A(f"cvb{i}", [128, 512], F32) for i in range(6)]
    Gb = [A(f"Gb{i}", [128, 514], F32) for i in range(4)]
    git = [0]
    ggbs = [A(f"ggb{i}", [128, 512], F32) for i in range(3)]
    carry = A("carry", [128, 2 * NFC, 2], F32)
    cwf = A("cwf", [128, 2 * NFC, 3], F32)
    y2bs = [A(f"y2b{i}", [128, D], F32) for i in range(4)]
    x1r = [A(f"x1r{i}", [128, D], F32) for i in range(2)]
    sq3 = A("sq3", [128, D], F32)
    outb = [A(f"outb{i}", [128, D], F32) for i in range(2)]
    g3 = A("g3", [128, D], F32)
    rs3 = A("rs3", [128, 16], F32)
    dma("sp", cwf[:, :, :], cwf_d, [], ["cwf"])
    dma("sp", g3[:, :], gam[3], [], ["g3"])
    h2all = [("h2T", i) for i in range(OWN_T)]
    pairs = [(half, j) for half in range(2) for j in range(NFC)]

    def load_wub(pidx):
        half, j = pairs[pidx]
        wb = wub[pidx % 3]
        kwb = ("wub", pidx % 3)
        dma("pool", wb[:, :, :], w_up_d[j], [], [kwb])

    load_wub(0)
    load_wub(1)
    wdn_v = w_dn_d.rearrange("(j p) n -> p j n", p=128)
    wdn_loads = [(lambda j0=j0: dma("pool", wdn[:, j0:j0 + 2, :], wdn_v[:, j0:j0 + 2, :], [], [("wdn", j0 // 2)]))
                 for j0 in range(0, NFC, 2)]
    fit = [0]
    dn_ps = {}

    def dn0(ot):
        tb = ot % 8
        for hf in range(2):
            pi = next_ps(0, 6)
            dn_ps[(ot, hf)] = pi
            for j in range(NFC):
                mm(psf[pi][:, :], aT[:, j, tb * 128:(tb + 1) * 128], wdn[:, j, hf * 512:(hf + 1) * 512], j == 0, j == NFC - 1,
                   [("aT", j, tb // 4), ("wdn", j // 2)], [("ps", pi)])

    def dn1(ot):
        y2b = y2bs[ot % 4]
        for hf in range(2):
            pi = dn_ps[(ot, hf)]
            op("act", "activation", [("ps", pi)], [("y2b", ot % 4, hf)], out=y2b[:, hf * 512:(hf + 1) * 512], in_=psf[pi][:, :], func=AF.Copy)
        ky = [("y2b", ot % 4, 0), ("y2b", ot % 4, 1)]
        op("act", "activation", ky, ["sq3", ("rs3", ot)], out=sq3[:, :], in_=y2b[:, :], func=AF.Square, accum_out=rs3[:, ot:ot + 1])

    def dn2(ot):
        op("dve", "reduce_sum", ["sq3"], [("rs3", ot)], out=rs3[:, ot:ot + 1], in_=sq3[:, :], axis=AX.X)

    def dn3(ot):
        kr = ("rs3", ot)
        dma("sp", x1r[ot % 2][:, :], x1_scr[(1 + ot) * 128:(2 + ot) * 128, :], [("x1s", 1 + ot)], [("x1r", ot % 2)])
        op("act", "activation", [kr, "eps_t"], [kr], out=rs3[:, ot:ot + 1], in_=rs3[:, ot:ot + 1], func=AF.Ln, bias=eps_t[:, 0:1], scale=1.0 / D)
        op("act", "activation", [kr], [kr], out=rs3[:, ot:ot + 1], in_=rs3[:, ot:ot + 1], func=AF.Exp, scale=-0.5)

    def dn4(ot):
        y2b = y2bs[ot % 4]
        ky = [("y2b", ot % 4, 0), ("y2b", ot % 4, 1)]
        kr = ("rs3", ot)
        op("dve", "scalar_tensor_tensor", ky + [kr, "g3"], ky, out=y2b[:, :], in0=y2b[:, :], scalar=rs3[:, ot:ot + 1], in1=g3[:, :], op0=ALU.mult, op1=ALU.mult)
        ob_ = outb[ot % 2]
        op("dve", "tensor_tensor", ky + [("x1r", ot % 2)], [("outb", ot % 2)], out=ob_[:, :], in0=y2b[:, :], in1=x1r[ot % 2][:, :], op=ALU.add)
        dma("sp", out_d[ot * 128:(ot + 1) * 128, :], ob_[:, :], [("outb", ot % 2)], [("out", ot)])

    for pidx, (half, j) in enumerate(pairs):
        wb = wub[pidx % 3]
        kwb = ("wub", pidx % 3)
        if pidx + 2 < len(pairs):
            load_wub(pidx + 2)
        if half == 0 and pidx % 2 == 0 and wdn_loads:
            wdn_loads.pop(0)()
        if half == 0:
            for which in range(2):
                chn = j + NFC * which
                pi = next_ps(0, 6)
                for k in range(8):
                    mm(psf[pi][:, 0:2], wb[:, k, which * 128:(which + 1) * 128], h2T[:, k, 126:128], k == 0, k == 7, [kwb] + h2all, [("ps", pi)])
                op("dve", "tensor_copy", [("ps", pi)], [("carry", chn)], out=carry[:, chn, :], in_=psf[pi][:, 0:2])
        for t in range(2):
            tg = 2 * half + t
            tcols = slice(128 + tg * 512, 128 + (tg + 1) * 512)
            par = fit[0] % 3
            fit[0] += 1
            for which in range(2):
                chn = j + NFC * which
                pi = next_ps(0, 6)
                kp = ("ps", pi)
                for k in range(8):
                    mm(psf[pi][:, :], wb[:, k, which * 128:(which + 1) * 128], h2T[:, k, tcols], k == 0, k == 7, [kwb] + h2all, [kp])
                cv = cvb[which * 3 + par]
                kcv = ("cvb", which * 3 + par)
                kc = ("carry", chn)
                gi = which * 2 + (git[0] % 2)
                G = Gb[gi]
                kG = ("Gb", gi)
                op("act", "activation", [kc], [(kG, "h")], out=G[:, 0:2], in_=carry[:, chn, :], func=AF.Copy)
                op("act", "activation", [kp], [(kG, "m")], out=G[:, 2:514], in_=psf[pi][:, :], func=AF.Copy)
                op("act", "activation", [kp, "cwf"], [kcv], out=cv[:, :], in_=psf[pi][:, :], func=AF.Copy, scale=cwf[:, chn, 2:3])
                op("dve", "tensor_copy", [(kG, "m")], [kc], out=carry[:, chn, :], in_=G[:, 512:514])
                op("dve", "scalar_tensor_tensor", [(kG, "h"), (kG, "m"), "cwf", kcv], [kcv], out=cv[:, :], in0=G[:, 1:513], scalar=cwf[:, chn, 1:2],
                   in1=cv[:, :], op0=ALU.mult, op1=ALU.add)
                op("dve", "scalar_tensor_tensor", [(kG, "h"), (kG, "m"), "cwf", kcv], [kcv], out=cv[:, :], in0=G[:, 0:512], scalar=cwf[:, chn, 0:1],
                   in1=cv[:, :], op0=ALU.mult, op1=ALU.add)
            git[0] += 1
            ggb = ggbs[par]
            op("act", "activation", [("cvb", 0 * 3 + par)], [("ggb", par)], out=ggb[:, :], in_=cvb[0 * 3 + par][:, :], func=AF.Gelu_apprx_tanh)
            op("dve", "tensor_tensor", [("ggb", par), ("cvb", 1 * 3 + par)], [("aT", j, t)], out=aT[:, j, t * 512:(t + 1) * 512], in0=ggb[:, :],
               in1=cvb[1 * 3 + par][:, :], op=ALU.mult)
        if j == NFC - 1:
            run_stages(8, [lambda tb, half=half: dn0(half * 8 + tb), lambda tb, half=half: dn1(half * 8 + tb),
                           lambda tb, half=half: dn3(half * 8 + tb),
                           lambda tb, half=half: dn4(half * 8 + tb)])
    A.peak = max(getattr(A, "peak", 0), A.off)
    S.barrier()
    n_ops = S.emit(nc)
    return nc, n_ops


def _t5_bucket(d):
    d = np.asarray(d)
    max_exact = 16
    df = np.maximum(d, 1).astype(np.float32)
    large = max_exact + (np.log(df / np.float32(max_exact)) / np.float32(math.log(128 / max_exact)) * np.float32(32 - max_exact)).astype(np.int32)
    return np.where(d < max_exact, d, np.minimum(large, 31)).astype(np.int64)


def _prep_common(inp):
    f = np.float32
    w_in = np.asarray(inp["w_in"][0], f)
    rel = np.asarray(inp["rel_bias"], f)
    c = {}
    c["gam"] = np.ascontiguousarray(np.stack([
        np.broadcast_to(np.asarray(inp[k][0], f)[None, :], (128, D))
        for k in ("norm_mix_pre", "norm_mix_post", "norm_ffn_pre", "norm_ffn_post")]))
    kc0, vc0 = 512, 640
    c["w_kvc"] = np.ascontiguousarray(np.concatenate(
        [w_in[:, kc0:kc0 + 64], w_in[:, vc0:vc0 + 64], w_in[:, kc0 + 64:kc0 + 128], w_in[:, vc0 + 64:vc0 + 128]], axis=1))
    wg = []
    for g in range(2):
        gcols = np.array([1280 + g * 12 + r * 3 + b for r in range(4) for b in range(3)])
        parts = [w_in[:, g * 256:(g + 1) * 256],
                 w_in[:, 768 + g * 64:768 + (g + 1) * 64],
                 w_in[:, 1024 + g * 64:1024 + (g + 1) * 64],
                 w_in[:, 896 + g * 64:896 + (g + 1) * 64],
                 w_in[:, 1152 + g * 64:1152 + (g + 1) * 64],
                 w_in[:, gcols],
                 w_in[:, 1816 + g * 256:1816 + (g + 1) * 256],
                 w_in[:, 2328 + g * 256:2328 + (g + 1) * 256],
                 w_in[:, 1304 + g * 256:1304 + (g + 1) * 256]]
        wg.append(np.concatenate(parts, axis=1))
    c["w_g"] = np.ascontiguousarray(np.stack(wg))
    w1k = np.asarray(inp["w_cmp_k1"][0], f).reshape(32, 64, 256).transpose(1, 0, 2)
    w1v = np.asarray(inp["w_cmp_v1"][0], f).reshape(32, 64, 256).transpose(1, 0, 2)
    c["w1kv"] = np.ascontiguousarray(np.concatenate([w1k, w1v], axis=0))
    c["pekv"] = np.ascontiguousarray(np.concatenate([np.asarray(inp["pe_cmp_k"][0], f).T, np.asarray(inp["pe_cmp_v"][0], f).T], axis=0))
    w2k = np.asarray(inp["w_cmp_k2"][0], f).reshape(2, 128, 64).transpose(1, 0, 2)
    w2v = np.asarray(inp["w_cmp_v2"][0], f).reshape(2, 128, 64).transpose(1, 0, 2)
    c["w2kv"] = np.ascontiguousarray(np.concatenate([w2k, w2v], axis=1))
    bias_d = rel[:, _t5_bucket(np.arange(0, 8192))]
    s_i = np.arange(128)[:, None]
    t_i = np.arange(128)[None, :]
    bt = np.empty((8, 128, 4, 128), f)
    d0 = t_i - s_i
    bt[:, :, 0, :] = np.where(d0 >= 0, bias_d[:, np.maximum(d0, 0)], f(NEG))
    bt[:, :, 1, :] = bias_d[:, 128 + d0]
    bt[:, :, 2, :] = rel[:, 31][:, None, None]
    bt[:, :, 3, :] = np.where((512 + d0) < 512, rel[:, 31][:, None, None], f(NEG))
    bt10 = bt[:, :, [0, 1, 2, 2, 2, 0, 1, 2, 2, 3], :]
    c["btab"] = np.ascontiguousarray(bt10.transpose(1, 0, 2, 3).reshape(128, 8, 1280))
    c["c31"] = np.ascontiguousarray(np.broadcast_to(rel[:, 31][None, :], (128, 8)))
    c["_bias_d"] = bias_d
    nn = np.arange(256)[:, None]
    jj = np.arange(64)[None, :]
    ov = np.clip(np.minimum(16 * nn + 32, 64 * jj + 64) - np.maximum(16 * nn, 64 * jj), 0, None).astype(f) / 32.0
    ov[255] = 0
    ovm = np.zeros((256, 65), f)
    ovm[:, :64] = ov
    ovm[:255, 64] = 1.0
    c["ovm1"] = np.ascontiguousarray(ovm.reshape(2, 128, 65).transpose(1, 0, 2))
    c["emat"] = np.ascontiguousarray((np.arange(S_LEN)[None, :] // 64 == np.arange(64)[:, None]).astype(f))
    c["ident"] = np.eye(128, dtype=f)
    c["cwm"] = np.ascontiguousarray(np.asarray(inp["conv_mix_w"][0], f).T.reshape(4, 128, 3).transpose(1, 0, 2))
    c["w_out"] = np.ascontiguousarray(np.asarray(inp["w_out"][0], f))
    wu = np.asarray(inp["w_ffn_up"][0], f)
    wu2 = np.stack([wu[:, :D_FF].reshape(8, 128, NFC, 128), wu[:, D_FF:].reshape(8, 128, NFC, 128)], axis=3)
    c["w_up"] = np.ascontiguousarray(wu2.transpose(2, 1, 0, 3, 4).reshape(NFC, 128, 8, 256))
    c["cwf"] = np.ascontiguousarray(np.asarray(inp["ffn_conv_w"][0], f).T.reshape(2 * NFC, 128, 3).transpose(1, 0, 2))
    c["w_dn"] = np.ascontiguousarray(np.asarray(inp["w_ffn_down"][0], f))
    return c


_NC_CACHE = {}


def _frame_tables(bias_d, shift):
    f = np.float32
    tl = np.arange(S_LEN)
    valid = (tl >= shift).astype(f)
    out = {"valid": np.ascontiguousarray(valid.reshape(NT, 128).T)}
    jj = np.arange(64)[None, :]
    cur = (tl // 64)[:, None]
    j0 = shift // 64
    okj = (jj >= j0) & (jj <= cur) & (tl[:, None] >= shift)
    forced = (jj == j0) | (jj == cur) | (jj == cur - 1)
    bsel = np.where(okj, np.where(forced, f(1e4), f(0)), f(NEGB)).astype(f)
    out["bsel"] = np.ascontiguousarray(bsel.reshape(NTG, 4, 128, 64).transpose(0, 2, 1, 3))
    n_i = np.arange(256)[:, None]
    dc = tl[None, :] - 16 * n_i - 31
    okn = (dc >= 0) & (n_i < 255) & (n_i >= shift // 16)
    ct = np.where(okn[None], bias_d[:, np.clip(dc, 0, 8191)], f(NEG)).astype(f)
    out["ctab"] = np.ascontiguousarray(ct.reshape(8, 2, 128, S_LEN))
    return out


def _in_maps(inputs):
    common = _prep_common(inputs)
    bias_d = common.pop("_bias_d")
    frames = [_frame_tables(bias_d, 2048), _frame_tables(bias_d, 0)]
    x = np.asarray(inputs["x"], np.float32)
    maps = []
    for core in range(8):
        b, cpos = core // 2, core % 2
        m = dict(common)
        m.update(frames[cpos])
        if cpos == 0:
            xf = np.zeros((S_LEN, D), np.float32)
            xf[2048:] = x[b, 0:2048]
        else:
            xf = np.ascontiguousarray(x[b])
        m["x_full"] = xf
        maps.append(m)
    return maps


def kernel(**inputs):
    if "nc" not in _NC_CACHE:
        _NC_CACHE["nc"] = build_nc(False)[0]
    nc = _NC_CACHE["nc"]
    maps = _in_maps(inputs)
    res = run_bass_kernel_spmd(nc, maps, core_ids=list(range(8)))
    out = np.empty((4, S_LEN, D), np.float32)
    for core in range(8):
        b, cpos = core // 2, core % 2
        out[b, 2048 * cpos:2048 * (cpos + 1)] = res.results[core]["out_own"]
    return out
```

```python
import contextlib
import math
import numpy as np
import concourse.bass as bass
import concourse.mybir as mybir
from concourse.bass_utils import run_bass_kernel_spmd

F32 = mybir.dt.float32
BF16 = mybir.dt.bfloat16
AF = mybir.ActivationFunctionType
ALU = mybir.AluOpType
AX = mybir.AxisListType

ENGS = ("pe", "act", "dve", "pool", "sp")

S_LEN = 4096
D = 1024
NT = 32
NTG = 8
NEG = -30000.0
NEGB = -1.0e9
D_FF = 2816
NFC = 22
OWN_T = 17
SCR_W = S_LEN
WOUT_OFF = 212608
QGROUPS = ((3, 3), (4, 0), (5, 0), (6, 0), (7, 0))


class Op:
    __slots__ = ("eng", "fn", "deps", "dma", "idx", "need_inc", "semval", "sem", "strong", "eidx")


class Sched:
    def __init__(self):
        self.ops = []
        self.lw = {}
        self.rd = {}
        self.last_eng = {}
        self.dma_since_barrier = []
        self.ecount = {}

    def add(self, eng, fn, r=(), w=(), dma=False):
        op = Op()
        op.eng = eng
        op.fn = fn
        op.dma = dma
        op.idx = len(self.ops)
        op.need_inc = dma
        op.sem = None
        op.semval = 0
        deps = set()
        for k in r:
            if k in self.lw:
                deps.add(self.lw[k])
        op.strong = set(deps)
        op.eidx = self.ecount.get(eng, 0)
        self.ecount[eng] = op.eidx + 1
        for k in w:
            if k in self.lw:
                deps.add(self.lw[k])
            deps.update(self.rd.get(k, ()))
        for k in r:
            self.rd.setdefault(k, set()).add(op.idx)
        for k in w:
            self.lw[k] = op.idx
            self.rd[k] = set()
        op.deps = deps
        self.ops.append(op)
        if dma:
            self.dma_since_barrier.append(op.idx)
        else:
            self.last_eng[eng] = op.idx
        return op.idx

    def barrier(self):
        deps = set(self.last_eng.values()) | set(self.dma_since_barrier)
        self.dma_since_barrier = []
        for eng in ENGS:
            op = Op()
            op.eng = eng
            op.fn = None
            op.dma = False
            op.idx = len(self.ops)
            op.need_inc = False
            op.sem = None
            op.semval = 0
            op.deps = set(deps)
            op.strong = set(deps)
            op.eidx = -1
            self.ops.append(op)
        self.lw = {}
        self.rd = {}

    @staticmethod
    def _weak_skip(op, Dp, d):
        return ((not op.dma) and (not Dp.dma) and op.fn is not None and Dp.eng == op.eng
                and d not in op.strong and op.eidx - Dp.eidx >= 2)

    def emit(self, nc, n_dma_sems=64):
        ops = self.ops
        es = contextlib.ExitStack()
        with es:
            dma_sems = [es.enter_context(nc.semaphore(f"dq{i}")) for i in range(n_dma_sems)]
            eng_sems = {e: es.enter_context(nc.semaphore(f"es_{e}")) for e in ENGS}
            dma_last = [None] * n_dma_sems
            dma_cnt = [0] * n_dma_sems
            di = 0
            for op in ops:
                if op.dma:
                    j = di % n_dma_sems
                    di += 1
                    if dma_last[j] is not None:
                        op.deps.add(dma_last[j])
                    dma_last[j] = op.idx
                    dma_cnt[j] += 16
                    op.sem = ("d", j)
                    op.semval = dma_cnt[j]
            for op in ops:
                for d in op.deps:
                    Dp = ops[d]
                    if Dp.dma:
                        continue
                    if Dp.eng == op.eng and Dp.eng == "pe" and not op.dma:
                        continue
                    if self._weak_skip(op, Dp, d):
                        continue
                    Dp.need_inc = True
            cnt = {}
            for op in ops:
                if (not op.dma) and op.need_inc:
                    cnt[op.eng] = cnt.get(op.eng, 0) + 1
                    op.sem = ("e", op.eng)
                    op.semval = cnt[op.eng]

            def semh(s):
                return dma_sems[s[1]] if s[0] == "d" else eng_sems[s[1]]

            def make(eng_name):
                def body(e):
                    waited = {}
                    for op in ops:
                        if op.eng != eng_name:
                            continue
                        need = {}
                        for d in op.deps:
                            Dp = ops[d]
                            if (not Dp.dma) and Dp.eng == eng_name and eng_name == "pe" and not op.dma:
                                continue
                            if self._weak_skip(op, Dp, d):
                                continue
                            if Dp.sem is None:
                                continue
                            if need.get(Dp.sem, 0) < Dp.semval:
                                need[Dp.sem] = Dp.semval
                        for s, v in need.items():
                            if waited.get(s, 0) < v:
                                e.wait_ge(semh(s), v)
                                waited[s] = v
                        if op.fn is None:
                            continue
                        ins = op.fn(e)
                        if op.need_inc:
                            ins.then_inc(semh(op.sem), 16 if op.dma else 1)
                return body

            with nc.Block() as block:
                block.tensor(make("pe"))
                block.scalar(make("act"))
                block.vector(make("dve"))
                block.gpsimd(make("pool"))
                block.sync(make("sp"))
        return len(ops)


class SbAlloc:
    def __init__(self, nc, limit=229000):
        self.nc = nc
        self.off = 16384
        self.limit = limit
        self.n = 0

    def __call__(self, name, shape, dtype):
        nb = 4 if dtype == F32 else 2
        sz = int(np.prod(shape[1:])) * nb
        sz = (sz + 63) // 64 * 64
        if self.off + sz > self.limit:
            raise RuntimeError(f"SBUF overflow allocating {name}: {self.off}+{sz}")
        self.n += 1
        t = self.nc.alloc_sbuf_tensor_at(f"{name}_{self.n}", list(shape), dtype, offset=self.off)
        self.off += sz
        return t

    def mark(self):
        return self.off

    def release(self, m):
        self.off = m


def build_nc(debug=False):
    nc = bass.Bass("TRN2", target_bir_lowering=False)
    okind = "ExternalOutput" if debug else "Internal"

    def din(name, shape, dt=F32):
        return nc.dram_tensor(name, list(shape), dt, kind="ExternalInput").ap()

    x_full = din("x_full", [S_LEN, D])
    valid_d = din("valid", [128, NT])
    gam = din("gam", [4, 128, D])
    w_kvc = din("w_kvc", [D, 256])
    w_g = din("w_g", [2, D, 1292])
    w1kv_d = din("w1kv", [128, 32, 256])
    pekv_d = din("pekv", [128, 32])
    w2kv_d = din("w2kv", [128, 4, 64])
    btab_d = din("btab", [128, 8, 1280])
    c31_d = din("c31", [128, 8])
    ctab_d = din("ctab", [8, 2, 128, S_LEN])
    ovm_d = din("ovm1", [128, 2, 65])
    bsel_d = din("bsel", [NTG, 128, 4, 64])
    emat_d = din("emat", [64, S_LEN])
    ident_d = din("ident", [128, 128])
    cwm_d = din("cwm", [128, 4, 3])
    w_out_d = din("w_out", [D, D])
    w_up_d = din("w_up", [NFC, 128, 8, 256])
    cwf_d = din("cwf", [128, 2 * NFC, 3])
    w_dn_d = din("w_dn", [D_FF, D])
    out_d = nc.dram_tensor("out_own", [2048, D], F32, kind="ExternalOutput").ap()

    hT_scr = nc.dram_tensor("hT_scr", [D, S_LEN], BF16, kind=okind).ap()
    oT_scr = nc.dram_tensor("oT_scr", [D, SCR_W], BF16, kind=okind).ap()
    x1_scr = nc.dram_tensor("x1_scr", [OWN_T * 128, D], F32, kind=okind).ap()

    A = SbAlloc(nc)
    S = Sched()

    def op(eng, meth, r, w, *args, **kw):
        S.add(eng, lambda e: getattr(e, meth)(*args, **kw), r=r, w=w)

    def dma(eng, out, in_, r, w):
        S.add(eng, lambda e: e.dma_start(out=out, in_=in_), r=r, w=w, dma=True)

    def mm(out, lhsT, rhs, start, stop, r, w, skip=False):
        S.add("pe", lambda e: e.matmul(out, lhsT, rhs, start=start, stop=stop, skip_group_check=skip), r=r, w=w)

    def tr(out, in_, ident, r, w):
        S.add("pe", lambda e: e.transpose(out=out, in_=in_, identity=ident), r=r, w=w)

    def run_stages(n, stages):
        ns = len(stages)
        for step in range(n + ns - 1):
            for si in range(ns - 1, -1, -1):
                i = step - si
                if 0 <= i < n:
                    stages[si](i)

    psall = nc.alloc_psum_tensor("psall", [128, 7 * 512], F32)
    psf = [psall[:, i * 512:(i + 1) * 512] for i in range(7)]
    psb = nc.alloc_psum_tensor("psb", [128, 1024], BF16)
    PSB = ("ps", 7)
    psrot = [0]

    def next_ps(lo, hi):
        i = lo + psrot[0] % (hi - lo)
        psrot[0] += 1
        return i

    ident_b = A("ident_b", [128, 128], BF16)
    dma("pool", ident_b[:, :], ident_d, [], ["ident_b"])
    gbuf = A("gbuf", [128, D], F32)
    small = A("small", [128, 64], F32)
    eps_t = A("eps_t", [128, 1], F32)
    op("dve", "memset", [], ["eps_t"], eps_t[:, :], 1e-6)

    def rms_rstd(src, key_src, dst, key_dst, sq, key_sq):
        op("act", "activation", [key_src], [key_sq], out=sq, in_=src, func=AF.Square)
        op("dve", "reduce_sum", [key_sq], [key_dst], out=dst, in_=sq, axis=AX.X)
        op("act", "activation", [key_dst, "eps_t"], [key_dst], out=dst, in_=dst, func=AF.Ln, bias=eps_t[:, 0:1], scale=1.0 / D)
        op("act", "activation", [key_dst], [key_dst], out=dst, in_=dst, func=AF.Exp, scale=-0.5)

    mark_main = A.mark()

    hi = 150016
    Wc = nc.alloc_sbuf_tensor_at("Wc_hi", [128, 8, 256], BF16, offset=hi)
    w1kv = nc.alloc_sbuf_tensor_at("w1kv_hi", [128, 32, 256], BF16, offset=hi + 4096)
    pekv = nc.alloc_sbuf_tensor_at("pekv_hi", [128, 32], BF16, offset=hi + 4096 + 16384)
    w2kv = nc.alloc_sbuf_tensor_at("w2kv_hi", [128, 4, 64], BF16, offset=hi + 4096 + 16384 + 64)
    dma("pool", Wc[:, :, :], w_kvc.rearrange("(k p) n -> p k n", p=128), [], ["Wc"])
    for q4 in range(4):
        dma("pool", w1kv[:, q4 * 8:(q4 + 1) * 8, :], w1kv_d[:, q4 * 8:(q4 + 1) * 8, :], [], [("w1kv", q4)])
    dma("pool", pekv[:, :], pekv_d, [], ["pekv"])
    dma("pool", w2kv[:, :, :], w2kv_d, [], ["w2kv"])

    xin = [A(f"xin{i}", [128, D], F32) for i in range(8)]
    sqbs = [A(f"sqb{i}", [128, D], F32) for i in range(2)]
    hb = [A(f"hb{i}", [128, D], BF16) for i in range(2)]
    hstage = [A(f"hst{i}", [128, 8, 512], BF16) for i in range(2)]
    rs0 = A("rs0", [128, NT], F32)
    dma("sp", gbuf[:, :], gam[0], [], ["gbuf"])
    hT_v = hT_scr.rearrange("(k p) t -> p k t", p=128)

    def p0_a(i):
        dma("sp", xin[i % 8][:, :], x_full[i * 128:(i + 1) * 128, :], [], [("xin", i % 8)])

    def p0_b(i):
        op("act", "activation", [("xin", i % 8)], [("sqb", i % 2), ("rs0", i)], out=sqbs[i % 2][:, :], in_=xin[i % 8][:, :], func=AF.Square,
           accum_out=rs0[:, i:i + 1])

    def p0_c(i):
        op("dve", "reduce_sum", [("sqb", i % 2)], [("rs0", i)], out=rs0[:, i:i + 1], in_=sqbs[i % 2][:, :], axis=AX.X)

    def p0_d(i):
        kr = ("rs0", i)
        op("act", "activation", [kr, "eps_t"], [kr], out=rs0[:, i:i + 1], in_=rs0[:, i:i + 1], func=AF.Ln, bias=eps_t[:, 0:1], scale=1.0 / D)
        op("act", "activation", [kr], [kr], out=rs0[:, i:i + 1], in_=rs0[:, i:i + 1], func=AF.Exp, scale=-0.5)

    def p0_e(i):
        hbi = hb[i % 2]
        kh = ("hb", i % 2)
        op("dve", "scalar_tensor_tensor", [("xin", i % 8), ("rs0", i), "gbuf"], [kh], out=hbi[:, :], in0=xin[i % 8][:, :],
           scalar=rs0[:, i:i + 1], in1=gbuf[:, :], op0=ALU.mult, op1=ALU.mult)
        for k in range(8):
            tr(psb[:, k * 128:(k + 1) * 128], hbi[:, k * 128:(k + 1) * 128], ident_b[:, :], [kh, "ident_b"], [PSB])

    def p0_f(i):
        tg = i // 4
        hs = hstage[tg % 2]
        ks = ("hst", tg % 2, i % 4)
        hdst = hs[:, :, (i % 4) * 128:(i % 4 + 1) * 128]
        hsrc = psb[:, :].rearrange("p (k t) -> p k t", k=8)
        if i % 2 == 0:
            op("act", "activation", [PSB], [ks], out=hdst, in_=hsrc, func=AF.Copy)
        else:
            op("dve", "tensor_copy", [PSB], [ks], out=hdst, in_=hsrc)
        if i % 4 == 3:
            dma("sp", hT_v[:, :, tg * 512:(tg + 1) * 512], hs[:, :, :], [("hst", tg % 2, j) for j in range(4)], [("hT", tg)])

    run_stages(NT, [p0_a, lambda i: None, lambda i: None, p0_b, p0_d, p0_e, p0_f])
    A.peak = max(getattr(A, "peak", 0), A.off)
    if A.off > WOUT_OFF and len(S.ops) > 3000 and len(S.ops) < 16000:
        raise RuntimeError("attention scope overlaps wout")
    S.barrier()
    A.release(mark_main)

    hbuf = [A(f"hbuf{i}", [128, 8, 512], BF16) for i in range(2)]
    KCT = [A(f"KCT{g}", [64, 256], BF16) for g in range(2)]
    CV = [A(f"CV{g}", [128, 2, 128], BF16) for g in range(2)]
    cbias = A("cbias", [128, 4], F32)

    def load_hbuf(tg, it):
        hbq = hbuf[it % 2]
        dma("sp", hbq[:, :, :], hT_v[:, :, tg * 512:(tg + 1) * 512], [("hT", tg)], [("hbuf", it % 2)])
        return hbq, ("hbuf", it % 2)

    hit = [0]
    mark_1a = A.mark()
    KVC = [A(f"KVC{g}", [128, 16, 256], BF16) for g in range(2)]
    HID = A("HID", [128, 4, 256], BF16)
    for g in range(2):
        op("pool", "memset", [], [("CV", g)], CV[g][:, :, :], 0.0)
        op("pool", "memset", [], [("KCT", g)], KCT[g][:, :], 0.0)
    for g in range(2):
        dma("pool", CV[g][:, :, 0:64], ovm_d[:, :, 0:64], [("CV", g)], [("CV", g)])
    for tg in range(NTG):
        hbq, khb = load_hbuf(tg, hit[0]); hit[0] += 1
        for g in range(2):
            pi = next_ps(0, 6)
            for k in range(8):
                mm(psf[pi][:, :], Wc[:, k, g * 128:(g + 1) * 128], hbq[:, k, :], k == 0, k == 7, ["Wc", khb], [("ps", pi)])
            op("act", "activation", [("ps", pi)], [("KVC", g, tg)], out=KVC[g][:, :, tg * 32:(tg + 1) * 32], in_=psf[pi][:, :].rearrange("p (m f) -> p f m", f=16), func=AF.Copy)
    w1keys = [("w1kv", q4) for q4 in range(4)]
    for g in range(2):
        kvall = [("KVC", g, tg) for tg in range(NTG)]
        pi = next_ps(0, 6)
        for kvi in range(2):
            rows = slice(kvi * 64, (kvi + 1) * 64)
            for hc in range(2):
                col = kvi * 2 + hc
                for l in range(32):
                    mm(psf[pi][:, col:col + 1], w1kv[rows, l, hc * 128:(hc + 1) * 128], pekv[rows, l:l + 1],
                       l == 0 and col == 0, l == 31, w1keys + ["pekv"], [("ps", pi)], skip=True)
        op("dve", "tensor_copy", [("ps", pi)], ["cbias"], out=cbias[:, :], in_=psf[pi][:, 0:4])
        for kvi in range(2):
            rows = slice(kvi * 64, (kvi + 1) * 64)
            for hc in range(2):
                col = kvi * 2 + hc
                pi = next_ps(0, 6)
                for l in range(32):
                    mm(psf[pi][:, 0:255], w1kv[rows, l, hc * 128:(hc + 1) * 128], KVC[g][rows, l % 16, l // 16:l // 16 + 255],
                       l == 0, l == 31, w1keys + kvall, [("ps", pi)])
                op("act", "activation", [("ps", pi), "cbias"], [("HID", col)], out=HID[:, col, 0:255], in_=psf[pi][:, 0:255],
                   func=AF.Gelu_apprx_tanh, bias=cbias[:, col:col + 1], scale=1.0)
        pi = next_ps(0, 6)
        for hc in range(2):
            mm(psf[pi][0:64, 0:255], w2kv[:, hc, :], HID[:, hc, 0:255], hc == 0, hc == 1, ["w2kv", ("HID", hc)], [("ps", pi)])
        op("act", "activation", [("ps", pi)], [("KCT", g)], out=KCT[g][:, 0:255], in_=psf[pi][0:64, 0:255], func=AF.Copy)
        for nt in range(2):
            rows_n = 128 if nt == 0 else 127
            pi = next_ps(0, 6)
            for hc in range(2):
                mm(psf[pi][0:rows_n, 0:64], HID[:, 2 + hc, nt * 128:nt * 128 + rows_n], w2kv[:, 2 + hc, :], hc == 0, hc == 1,
                   ["w2kv", ("HID", 2 + hc)], [("ps", pi)])
            op("act", "activation", [("ps", pi)], [("CV", g)], out=CV[g][0:rows_n, nt, 64:128], in_=psf[pi][0:rows_n, 0:64], func=AF.Copy)
    A.peak = max(getattr(A, "peak", 0), A.off)
    S.barrier()
    A.release(mark_1a)

    Wg = A("Wg", [128, 8, 1292], BF16)
    QN = A("QN", [128, 4, S_LEN], BF16)
    KE = A("KE", [128, S_LEN], BF16)
    KW = A("KW", [64, S_LEN], BF16)
    V1s = A("V1s", [128, NT, 65], BF16)
    V1w = A("V1w", [128, NT, 65], BF16)
    btab2 = [A(f"btab{i}", [128, 4, 1280], BF16) for i in range(2)]
    wout = nc.alloc_sbuf_tensor_at("wout_hi", [128, 8, D], BF16, offset=WOUT_OFF)
    c31 = A("c31", [128, 8], F32)
    gsig = A("gsig", [128, NT, 12], F32)
    bsel = [A(f"bsel{i}", [128, 4, 64], F32) for i in range(2)]
    ctb = [A(f"ctb{i}", [128, 512], BF16) for i in range(16)]
    EC = [A(f"EC{i}", [128, 2, 512], BF16) for i in range(3)]
    PT = [A(f"PT{i}", [128, 512], BF16) for i in range(4)]
    PT2 = [A(f"PTW{i}", [128, 1024], BF16) for i in range(3)]
    p2it = [0]
    pbit = [0]
    onsa2 = [A(f"onsa{i}", [128, 4, 256], F32) for i in range(3)]
    onsb = A("onsb", [128, 4, 256], BF16)
    ostage = A("ostage", [128, 2, 512], BF16)
    impacc = A("impacc", [128, 4, 64], F32)
    sc1 = A("sc1", [128, 64], F32)
    sc2 = A("sc2", [128, 64], F32)
    m8a = A("m8a", [128, 8], F32)
    m8b = A("m8b", [128, 8], F32)
    nsb4 = A("nsb4", [128, 4, 128], BF16)
    zbuf = A("zbuf", [128, 2, 514], F32)
    tmpc = A("tmpc", [128, 512], F32)
    cacc = A("cacc", [128, 512], F32)
    convo = [A(f"convo{i}", [128, 512], BF16) for i in range(2)]
    cwm = A("cwm", [128, 4, 3], F32)
    zero_b = A("zero_b", [128, 128], BF16)
    nsT = A("nsT", [128, 512], BF16)

    dma("sp", c31[:, :], c31_d, [], ["c31"])
    dma("sp", cwm[:, :, :], cwm_d, [], ["cwm"])
    dma("pool", KE[64:128, :], emat_d, [], ["KE_E"])
    vld = A("vld", [128, NT], F32)
    dma("sp", vld[:, :], valid_d, [], ["vld"])
    op("pool", "tensor_copy", ["vld"], ["V1s1"], out=V1s[:, :, 64], in_=vld[:, :])
    op("pool", "tensor_copy", ["vld"], ["V1w1"], out=V1w[:, :, 64], in_=vld[:, :])
    for i2 in range(3):
        op("pool", "memset", [], [("onsa", i2, qb, r) for qb in range(4) for r in range(4)], onsa2[i2][:, :, :], 0.0)
    op("pool", "memset", [], ["nsb0"], nsb4[:, :, 0:64], 0.0)
    op("pool", "memset", [], ["zero_b"], zero_b[:, :], 0.0)
    oT_v = oT_scr.rearrange("(k p) t -> p k t", p=128)

    cit = [0]
    pit = [0]
    sm = [0]
    eit = [0]
    obr = [0]

    def smcol():
        c = sm[0] % 64
        sm[0] += 1
        return c

    for g in range(2):
        btab = btab2[g]
        if g == 0:
            dma("pool", Wg[:, :, :], w_g[0].rearrange("(k p) n -> p k n", p=128), [], ["Wg"])
            dma("pool", btab2[0][:, :, :], btab_d[:, 0:4, :], [], [("btab", 0)])
            dma("pool", btab2[1][:, :, :], btab_d[:, 4:8, :], [], [("btab", 1)])
        op("dve", "memset", [], [("zbuf", 0), ("zbuf", 1)], zbuf[:, :, 0:2], 0.0)
        def sm4():
            c = (sm[0] % 16) * 4
            sm[0] += 1
            return c, ("sm4", c)

        oslot = {}

        def run_pipeline(tiles, hooks=None, SK=2):
            n = len(tiles)
            for i in range(n + SK):
                if i < n:
                    tiles[i][0]()
                    tiles[i][1]()
                if i - SK >= 0 and tiles[i - SK][2] is not None:
                    tiles[i - SK][2]()
                if hooks and i in hooks:
                    for hk in hooks[i]:
                        hk()

        gidx = {t_: i_ for i_, (t_, _q2) in enumerate(QGROUPS)}

        def prefetch_ctab(tg, qmin):
            cq_ = qmin * 128
            cols_ = slice(tg * 512 + cq_, (tg + 1) * 512)
            for r_ in range(4):
                for nt_ in ([0] if tg < 4 else [0, 1]):
                    ci_ = (gidx[tg] % 2) * 8 + r_ * 2 + nt_
                    dma("pool", ctb[ci_][:, cq_:512], ctab_d[4 * g + r_, nt_, :, cols_], [], [("ctb", ci_)])

        def comp_tiles(tg, qmin):
            cols = slice(tg * 512 + qmin * 128, (tg + 1) * 512)
            cq = qmin * 128
            onsa = onsa2[oslot[tg]]
            par = oslot[tg]
            bs = bsel[tg % 2]
            kbs = ("bsel", tg % 2)
            dma("sp", bs[:, :, :], bsel_d[tg], [], [kbs])
            nts = [0] if tg < 4 else [0, 1]
            tiles = []
            for r in range(4):
                h = 4 * g + r
                ei = eit[0] % 3
                eit[0] += 1
                ec = EC[ei]
                for nt in nts:
                    st = {}

                    def A_(r=r, h=h, nt=nt, st=st):
                        ci = (gidx[tg] % 2) * 8 + r * 2 + nt
                        pi = next_ps(0, 4)
                        st["pi"] = pi
                        mm(psf[pi][:, cq:512], KCT[g][0:64, nt * 128:(nt + 1) * 128], QN[0:64, r, cols], True, False,
                           [("KCT", g), ("QNq", r, tg)], [("ps", pi)])
                        mm(psf[pi][:, cq:512], ident_b[:, :], ctb[ci][:, cq:512], False, True, ["ident_b", ("ctb", ci)], [("ps", pi)])

                    def B_(nt=nt, st=st, ec=ec, ei=ei):
                        pi = st["pi"]
                        op("act", "activation", [("ps", pi)], [("EC", ei, nt)], out=ec[:, nt, cq:512], in_=psf[pi][:, cq:512], func=AF.Exp)

                    C_ = None
                    if nt == nts[-1]:
                        def C_(r=r, ec=ec, ei=ei):
                            ob = 4 + obr[0] % 3
                            obr[0] += 1
                            kob = ("ps", ob)
                            firstmm = True
                            for qb in range(qmin, 4):
                                for nt2 in nts:
                                    mm(psf[ob][:, qb * 128:(qb + 1) * 128], ec[:, nt2, qb * 128:(qb + 1) * 128], CV[g][:, nt2, :],
                                       firstmm, False, [("EC", ei, nt2), ("CV", g)], [kob], skip=True)
                                    firstmm = False
                            pv4 = psf[ob][:, :].rearrange("p (q c) -> p q c", q=4)
                            c0, k0 = sm4()
                            c1, k1 = sm4()
                            op("dve", "reduce_sum", [kob], [k0], out=small[:, c0:c0 + 4], in_=pv4[:, :, 0:64], axis=AX.X)
                            op("dve", "tensor_scalar", [k0], [k0], out=small[:, c0:c0 + 4], in0=small[:, c0:c0 + 4], scalar1=1e-30, scalar2=None, op0=ALU.max)
                            op("dve", "reciprocal", [k0], [k0], out=small[:, c0:c0 + 4], in_=small[:, c0:c0 + 4])
                            op("dve", "tensor_tensor", [k0] + [("gsig", 4 * tg + j) for j in range(4)], [k1], out=small[:, c1:c1 + 4],
                               in0=small[:, c0:c0 + 4], in1=gsig[:, 4 * tg:4 * tg + 4, r * 3], op=ALU.mult)
                            for qb in range(qmin, 4):
                                if r == 0:
                                    op("dve", "tensor_scalar", [kob, k0], [("imp", qb)], out=impacc[:, qb, :], in0=pv4[:, qb, 0:64],
                                       scalar1=small[:, c0 + qb:c0 + qb + 1], scalar2=None, op0=ALU.mult)
                                else:
                                    op("dve", "scalar_tensor_tensor", [kob, k0, ("imp", qb)], [("imp", qb)], out=impacc[:, qb, :],
                                       in0=pv4[:, qb, 0:64], scalar=small[:, c0 + qb:c0 + qb + 1], in1=impacc[:, qb, :], op0=ALU.mult, op1=ALU.add)
                                op("dve", "tensor_scalar", [kob, k1], [("onsa", par, qb, r)], out=onsa[:, qb, r * 64:(r + 1) * 64],
                                   in0=pv4[:, qb, 64:128], scalar1=small[:, c1 + qb:c1 + qb + 1], scalar2=None, op0=ALU.mult)
                            if r == 3:
                                topk_dve(tg, qmin)
                    tiles.append((A_, B_, C_))
            return tiles

        def topk_dve(tg, qmin):
            bs = bsel[tg % 2]
            kbs = ("bsel", tg % 2)
            for qb in range(qmin, 4):
                op("dve", "tensor_tensor", [("imp", qb), kbs], ["sc1"], out=sc1[:, :], in0=impacc[:, qb, :], in1=bs[:, qb, :], op=ALU.add)
                op("dve", "max", ["sc1"], ["m8a"], out=m8a[:, :], in_=sc1[:, :])
                op("dve", "match_replace", ["sc1", "m8a"], ["sc2"], out=sc2[:, :], in_to_replace=m8a[:, :], in_values=sc1[:, :], imm_value=-1e30)
                op("dve", "max", ["sc2"], ["m8b"], out=m8b[:, :], in_=sc2[:, :])
                op("dve", "tensor_scalar", ["sc1", "m8b", "nsb0"], [("nsb", qb)], out=nsb4[:, qb, 64:128], in0=sc1[:, :], scalar1=m8b[:, 7:8], scalar2=NEG,
                   op0=ALU.is_lt, op1=ALU.mult)

        def topk_pe(tg, qmin):
            for qb in range(qmin, 4):
                tr(psb[:, qb * 128:(qb + 1) * 128], nsb4[:, qb, :], ident_b[:, :], [("nsb", qb), "nsb0", "ident_b"], [PSB])
            cq = qmin * 128
            op("dve", "tensor_copy", [PSB], ["nsT"], out=nsT[64:128, cq:512], in_=psb[64:128, cq:512])
            for r in range(4):
                op("pool", "tensor_copy", ["nsT"], [("QNm", r, tg * 4 + qb) for qb in range(qmin, 4)],
                   out=QN[64:128, r, tg * 512 + cq:(tg + 1) * 512], in_=nsT[64:128, cq:512])

        def evac_o(tg, r, br, ob, qmin=0):
            onsa = onsa2[oslot[tg]]
            par = oslot[tg]
            kob = ("ps", ob)
            c0, k0 = sm4()
            op("dve", "tensor_scalar", [kob], [k0], out=small[:, c0:c0 + 4], in0=psf[ob][:, 64:260:65], scalar1=1e-30, scalar2=None, op0=ALU.max)
            op("dve", "reciprocal", [k0], [k0], out=small[:, c0:c0 + 4], in_=small[:, c0:c0 + 4])
            op("dve", "tensor_tensor", [k0] + [("gsig", 4 * tg + j) for j in range(4)], [k0], out=small[:, c0:c0 + 4],
               in0=small[:, c0:c0 + 4], in1=gsig[:, 4 * tg:4 * tg + 4, r * 3 + br], op=ALU.mult)
            for qb in range(qmin, 4):
                op("dve", "scalar_tensor_tensor", [kob, k0, ("onsa", par, qb, r)], [("onsa", par, qb, r)], out=onsa[:, qb, r * 64:(r + 1) * 64],
                   in0=psf[ob][:, qb * 65:qb * 65 + 64], scalar=small[:, c0 + qb:c0 + qb + 1], in1=onsa[:, qb, r * 64:(r + 1) * 64],
                   op0=ALU.mult, op1=ALU.add)

        def sel_tiles(tg, qmin):
            tiles = []
            for r in range(4):
                h = 4 * g + r
                grp = {"ob": None}
                nkt = 4 * tg + 4
                npair = (4 * tg - 1) // 2 if qmin == 0 else 0
                for pj in range(npair):
                    st = {}
                    kt0 = 2 * pj

                    def A_(r=r, kt0=kt0, st=st):
                        pb = pbit[0] % 2
                        pbit[0] += 1
                        st["pb"] = pb
                        for u in range(2):
                            kt = kt0 + u
                            mm(psf[2 * pb + u][:, :], KE[:, kt * 128:(kt + 1) * 128], QN[:, r, tg * 512:(tg + 1) * 512], True, True,
                               [("KEk", kt // 4), "KE_E", ("QNq", r, tg)] + [("QNm", r, tg * 4 + j) for j in range(4)], [("ps", 2 * pb + u)])

                    def B_(h=h, st=st):
                        pb = st["pb"]
                        x_ = p2it[0] % 3
                        p2it[0] += 1
                        st["pt2"] = x_
                        op("act", "activation", [("ps", 2 * pb), ("ps", 2 * pb + 1), "c31"], [("PT2", x_)], out=PT2[x_][:, :],
                           in_=psall[:, 2 * pb * 512:2 * pb * 512 + 1024], func=AF.Exp, bias=c31[:, h:h + 1], scale=1.0)

                    def C_(r=r, kt0=kt0, st=st, grp=grp):
                        if grp["ob"] is None:
                            grp["ob"] = 4 + obr[0] % 3
                            obr[0] += 1
                            grp["first"] = True
                        ob = grp["ob"]
                        x_ = st["pt2"]
                        for u in range(2):
                            kt = kt0 + u
                            for qb in range(4):
                                mm(psf[ob][:, qb * 65:(qb + 1) * 65], PT2[x_][:, u * 512 + qb * 128:u * 512 + (qb + 1) * 128], V1s[:, kt, :],
                                   grp["first"], False, [("PT2", x_), ("V1s", kt), "V1s1"], [("ps", ob)], skip=True)
                                grp["first"] = False
                    tiles.append((A_, B_, C_))
                for kt in range(2 * npair, nkt):
                    st = {}
                    d = kt - 4 * tg
                    q0 = max(qmin, max(0, d))
                    if q0 > 3:
                        continue
                    c0 = 128 * q0
                    near = d >= -1

                    def A_(r=r, kt=kt, d=d, q0=q0, c0=c0, near=near, st=st):
                        pi = next_ps(0, 4)
                        st["pi"] = pi
                        mm(psf[pi][:, c0:512], KE[:, kt * 128:(kt + 1) * 128], QN[:, r, tg * 512 + c0:(tg + 1) * 512], True, not near,
                           [("KEk", kt // 4), "KE_E", ("QNq", r, tg)] + [("QNm", r, tg * 4 + j) for j in range(q0, 4)], [("ps", pi)])
                        if near:
                            mm(psf[pi][:, c0:512], ident_b[:, :], btab[:, r, (q0 - d) * 128:(4 - d) * 128], False, True,
                               ["ident_b", ("btab", g)], [("ps", pi)])

                    def B_(h=h, c0=c0, near=near, st=st):
                        pi = st["pi"]
                        p_ = pit[0] % 4
                        pit[0] += 1
                        st["pt"] = p_
                        if near:
                            op("act", "activation", [("ps", pi)], [("PT", p_)], out=PT[p_][:, c0:512], in_=psf[pi][:, c0:512], func=AF.Exp)
                        else:
                            op("act", "activation", [("ps", pi), "c31"], [("PT", p_)], out=PT[p_][:, c0:512], in_=psf[pi][:, c0:512], func=AF.Exp,
                               bias=c31[:, h:h + 1], scale=1.0)

                    def C_(r=r, kt=kt, q0=q0, st=st, grp=grp, last=(kt == nkt - 1)):
                        if grp["ob"] is None:
                            grp["ob"] = 4 + obr[0] % 3
                            obr[0] += 1
                            grp["first"] = True
                        ob = grp["ob"]
                        p_ = st["pt"]
                        for qb in range(q0, 4):
                            mm(psf[ob][:, qb * 65:(qb + 1) * 65], PT[p_][:, qb * 128:(qb + 1) * 128], V1s[:, kt, :], grp["first"], False,
                               [("PT", p_), ("V1s", kt), "V1s1"], [("ps", ob)], skip=True)
                            grp["first"] = False
                        if last:
                            evac_o(tg, r, 1, ob, qmin)
                    tiles.append((A_, B_, C_))
            return tiles

        def win_tiles(tg, qmin):
            tiles = []
            for r in range(4):
                grp = {"ob": None}
                ds_ = [d for d in range(-4, 4) if 4 * tg + d >= 0 and max(qmin, max(0, d)) <= min(3, d + 4)]
                for d in ds_:
                    st = {}
                    kt = 4 * tg + d
                    qlo = max(qmin, max(0, d))
                    qhi = min(3, d + 4)
                    c0 = 128 * qlo
                    c1 = 128 * (qhi + 1)

                    def A_(r=r, kt=kt, qlo=qlo, qhi=qhi, c0=c0, c1=c1, st=st):
                        pi = next_ps(0, 4)
                        st["pi"] = pi
                        mm(psf[pi][:, c0:c1], KW[0:64, kt * 128:(kt + 1) * 128], QN[0:64, r, tg * 512 + c0:tg * 512 + c1], True, False,
                           [("KW", kt // 4), ("QNq", r, tg)], [("ps", pi)])
                        dd = kt - 4 * tg
                        mm(psf[pi][:, c0:c1], ident_b[:, :], btab[:, r, 640 + (qlo - dd) * 128:640 + (qhi - dd + 1) * 128], False, True,
                           ["ident_b", ("btab", g)], [("ps", pi)])

                    def B_(c0=c0, c1=c1, st=st):
                        pi = st["pi"]
                        p_ = pit[0] % 4
                        pit[0] += 1
                        st["pt"] = p_
                        op("act", "activation", [("ps", pi)], [("PT", p_)], out=PT[p_][:, c0:c1], in_=psf[pi][:, c0:c1], func=AF.Exp)

                    def C_(r=r, kt=kt, qlo=qlo, qhi=qhi, st=st, grp=grp, last=(d == ds_[-1])):
                        if grp["ob"] is None:
                            grp["ob"] = 4 + obr[0] % 3
                            obr[0] += 1
                            grp["first"] = True
                        ob = grp["ob"]
                        p_ = st["pt"]
                        for qb in range(qlo, qhi + 1):
                            mm(psf[ob][:, qb * 65:(qb + 1) * 65], PT[p_][:, qb * 128:(qb + 1) * 128], V1w[:, kt, :], grp["first"], False,
                               [("PT", p_), ("V1w", kt), "V1w1"], [("ps", ob)], skip=True)
                            grp["first"] = False
                        if last:
                            evac_o(tg, r, 2, ob, qmin)
                    tiles.append((A_, B_, C_))
            return tiles

        def finalize(tg):
            onsa = onsa2[oslot[tg]]
            par = oslot[tg]
            allo = [("onsa", par, qb, r) for qb in range(4) for r in range(4)]
            op("pool", "tensor_copy", allo, ["onsb"], out=onsb[:, :, :], in_=onsa[:, :, :])
            for qb in range(4):
                for cc in range(2):
                    tr(psb[:, (qb * 2 + cc) * 128:(qb * 2 + cc + 1) * 128], onsb[:, qb, cc * 128:(cc + 1) * 128], ident_b[:, :],
                       ["onsb", "ident_b"], [PSB])
            pv = psb[:, :].rearrange("p (q c t) -> p q c t", q=4, c=2)
            for cc in range(2):
                op("act", "activation", [PSB], [("ostage", cc)], out=ostage[:, cc, :].rearrange("p (q t) -> p q t", q=4), in_=pv[:, :, cc, :], func=AF.Copy)
                dma("sp", oT_v[:, 2 * g + cc, tg * 512:(tg + 1) * 512], ostage[:, cc, :], [("ostage", cc)],
                    [("oscr", 2 * g + cc, tg * 4 + j) for j in range(4)])

        for gi_, (tg_, _q) in enumerate(QGROUPS):
            oslot[tg_] = gi_ % 3
        prefetch_ctab(*QGROUPS[0])
        hnext = load_hbuf(0, hit[0]); hit[0] += 1
        for tg in range(NTG):
            hbq, khb = hnext
            if tg + 1 < NTG:
                hnext = load_hbuf(tg + 1, hit[0]); hit[0] += 1
            cols = slice(tg * 512, (tg + 1) * 512)
            for r in range(4 if tg >= QGROUPS[0][0] else 0):
                pi = next_ps(0, 6)
                for k in range(8):
                    mm(psf[pi][0:64, :], Wg[:, k, r * 64:(r + 1) * 64], hbq[:, k, :], k == 0, k == 7, ["Wg", khb], [("ps", pi)])
                op("act", "activation", [("ps", pi)], [("QNq", r, tg)], out=QN[0:64, r, cols], in_=psf[pi][0:64, :], func=AF.Copy, scale=0.125)
            for which, dst, key in ((0, KE, "KEk"), (1, KW, "KW")):
                pi = next_ps(0, 6)
                for k in range(8):
                    mm(psf[pi][0:64, :], Wg[:, k, 256 + which * 64:320 + which * 64], hbq[:, k, :], k == 0, k == 7, ["Wg", khb], [("ps", pi)])
                op("dve", "tensor_copy", [("ps", pi)], [(key, tg)], out=dst[0:64, cols], in_=psf[pi][0:64, :])
            for b in range(4):
                tile_i = tg * 4 + b
                pi = next_ps(0, 6)
                for k in range(8):
                    mm(psf[pi][:, 0:140], hbq[:, k, b * 128:(b + 1) * 128], Wg[:, k, 384:524], k == 0, k == 7, ["Wg", khb], [("ps", pi)])
                op("dve", "tensor_copy", [("ps", pi)], [("V1s", tile_i), ("psx", pi)], out=V1s[:, tile_i, 0:64], in_=psf[pi][:, 0:64])
                op("dve", "tensor_copy", [("ps", pi)], [("V1w", tile_i), ("psx", pi)], out=V1w[:, tile_i, 0:64], in_=psf[pi][:, 64:128])
                op("act", "activation", [("ps", pi)], [("gsig", tile_i), ("psx", pi)], out=gsig[:, tile_i, :], in_=psf[pi][:, 128:140], func=AF.Sigmoid)
            for cc in range(2 if tg >= QGROUPS[0][0] else 0):
                ch = 2 * g + cc
                kz = ("zbuf", cc)
                pc = next_ps(0, 6)
                for k in range(8):
                    mm(psf[pc][:, :], Wg[:, k, 524 + cc * 128:652 + cc * 128], hbq[:, k, :], k == 0, k == 7, ["Wg", khb], [("ps", pc)])
                op("act", "activation", [("ps", pc)], ["tmpc"], out=tmpc[:, :], in_=psf[pc][:, :], func=AF.Copy)
                px = next_ps(0, 6)
                for k in range(8):
                    mm(psf[px][:, :], Wg[:, k, 780 + cc * 128:908 + cc * 128], hbq[:, k, :], k == 0, k == 7, ["Wg", khb], [("ps", px)])
                op("dve", "tensor_tensor", [("ps", px), "tmpc", kz], [kz], out=zbuf[:, cc, 2:514], in0=tmpc[:, :], in1=psf[px][:, :], op=ALU.mult)
                op("dve", "tensor_scalar", [kz, "cwm"], ["cacc"], out=cacc[:, :], in0=zbuf[:, cc, 2:514], scalar1=cwm[:, ch, 2:3], scalar2=None, op0=ALU.mult)
                op("dve", "scalar_tensor_tensor", [kz, "cwm", "cacc"], ["cacc"], out=cacc[:, :], in0=zbuf[:, cc, 1:513], scalar=cwm[:, ch, 1:2],
                   in1=cacc[:, :], op0=ALU.mult, op1=ALU.add)
                op("dve", "scalar_tensor_tensor", [kz, "cwm", "cacc"], ["cacc"], out=cacc[:, :], in0=zbuf[:, cc, 0:512], scalar=cwm[:, ch, 0:1],
                   in1=cacc[:, :], op0=ALU.mult, op1=ALU.add)
                op("dve", "tensor_copy", [kz], [kz], out=zbuf[:, cc, 0:2], in_=zbuf[:, cc, 512:514])
                pb = next_ps(0, 6)
                for k in range(8):
                    mm(psf[pb][:, :], Wg[:, k, 1036 + cc * 128:1164 + cc * 128], hbq[:, k, :], k == 0, k == 7, ["Wg", khb], [("ps", pb)])
                cvo = convo[cit[0] % 2]
                kcv = ("convo", cit[0] % 2)
                cit[0] += 1
                op("dve", "tensor_tensor", [("ps", pb), "cacc"], [kcv], out=cvo[:, :], in0=cacc[:, :], in1=psf[pb][:, :], op=ALU.mult)
                dma("sp", oT_v[:, 4 + ch, tg * 512:(tg + 1) * 512], cvo[:, :], [kcv],
                    [("oscr", 4 + ch, tg * 4 + j) for j in range(4)])
            if tg == QGROUPS[0][0]:
                run_pipeline(comp_tiles(*QGROUPS[0]))
                prefetch_ctab(*QGROUPS[1])

        def late_weights():
            dma("pool", Wg[:, :, :], w_g[1].rearrange("(k p) n -> p k n", p=128), [], ["Wg"])
            for k2 in range(2):
                dma("pool", wout[:, 4 * k2:4 * k2 + 4, :], w_out_d.rearrange("(k p) n -> p k n", p=128)[:, 4 * k2:4 * k2 + 4, :], [], [("wout", k2)])
        topk_pe(*QGROUPS[0])
        for gi_, (tg, qmin) in enumerate(QGROUPS):
            tl = []
            hooks = {}
            if gi_ + 1 < len(QGROUPS):
                tl += comp_tiles(*QGROUPS[gi_ + 1])
            ncomp = len(tl)
            tl += sel_tiles(tg, qmin)
            tl += win_tiles(tg, qmin)
            if gi_ > 0:
                ptg = QGROUPS[gi_ - 1][0]
                hooks.setdefault(min(ncomp + 10, len(tl) - 1), []).append(lambda ptg=ptg: finalize(ptg))
            if gi_ + 1 < len(QGROUPS):
                nxt = QGROUPS[gi_ + 1]
                hooks.setdefault(ncomp + (len(tl) - ncomp) // 2, []).append(lambda nxt=nxt: topk_pe(*nxt))
            if g == 0 and gi_ == len(QGROUPS) - 2:
                hooks.setdefault(ncomp + (len(tl) - ncomp) // 3, []).append(late_weights)
            if gi_ + 2 < len(QGROUPS):
                nn2 = QGROUPS[gi_ + 2]
                hooks.setdefault(ncomp + (len(tl) - ncomp) // 3, []).append(lambda nn2=nn2: prefetch_ctab(*nn2))
            run_pipeline(tl, hooks)
        finalize(QGROUPS[-1][0])
    A.peak = max(getattr(A, "peak", 0), A.off)
    if A.off > WOUT_OFF and len(S.ops) > 3000 and len(S.ops) < 16000:
        raise RuntimeError("attention scope overlaps wout")
    S.barrier()
    A.release(mark_main)

    h2T = A("h2T", [128, 8, OWN_T * 128], BF16)
    mark_post = A.mark()
    osel = [A(f"osel{i}", [128, 8, 128], BF16) for i in range(2)]
    ybufs = [A(f"ybuf{i}", [128, D], F32) for i in range(4)]
    xo = [A(f"xo{i}", [128, D], F32) for i in range(5)]
    x1b = [A(f"x1b{i}", [128, D], F32) for i in range(5)]
    sq2s = [A(f"sq2{i}", [128, D], F32) for i in range(2)]
    hb2 = [A(f"hb2{i}", [128, D], BF16) for i in range(2)]
    g1 = A("g1", [128, D], F32)
    g2 = A("g2", [128, D], F32)
    rs1 = A("rs1", [128, 2 * OWN_T], F32)
    dma("sp", g1[:, :], gam[1], [], ["g1"])
    dma("sp", g2[:, :], gam[2], [], ["g2"])
    mp_ps = {}

    def mp0(i):
        s_ = osel[i % 2]
        dma("sp", s_[:, :, :], oT_v[:, :, (15 + i) * 128:(16 + i) * 128], [], [("osel", i % 2)])
        dma("sp", xo[i % 5][:, :], x_full[(15 + i) * 128:(16 + i) * 128, :], [], [("xo", i % 5)])
        for hf in range(2):
            pi = next_ps(0, 6)
            mp_ps[(i, hf)] = pi
            for k in range(8):
                mm(psf[pi][:, :], s_[:, k, :], wout[:, k, hf * 512:(hf + 1) * 512], k == 0, k == 7, [("osel", i % 2), ("wout", k // 4)], [("ps", pi)])

    def mp1(i):
        ybuf = ybufs[i % 4]
        for hf in range(2):
            pi = mp_ps[(i, hf)]
            op("act", "activation", [("ps", pi)], [("ybuf", i % 4, hf)], out=ybuf[:, hf * 512:(hf + 1) * 512], in_=psf[pi][:, :], func=AF.Copy)
        ky = [("ybuf", i % 4, 0), ("ybuf", i % 4, 1)]
        op("act", "activation", ky, [("sq2", 0), ("rs1", 2 * i)], out=sq2s[0][:, :], in_=ybuf[:, :], func=AF.Square, accum_out=rs1[:, 2 * i:2 * i + 1])

    def mp2(i):
        op("dve", "reduce_sum", [("sq2", 0)], [("rs1", 2 * i)], out=rs1[:, 2 * i:2 * i + 1], in_=sq2s[0][:, :], axis=AX.X)

    def mp3(i):
        ka = ("rs1", 2 * i)
        op("act", "activation", [ka, "eps_t"], [ka], out=rs1[:, 2 * i:2 * i + 1], in_=rs1[:, 2 * i:2 * i + 1], func=AF.Ln, bias=eps_t[:, 0:1], scale=1.0 / D)
        op("act", "activation", [ka], [ka], out=rs1[:, 2 * i:2 * i + 1], in_=rs1[:, 2 * i:2 * i + 1], func=AF.Exp, scale=-0.5)

    def mp4(i):
        ybuf = ybufs[i % 4]
        ky = [("ybuf", i % 4, 0), ("ybuf", i % 4, 1)]
        ka = ("rs1", 2 * i)
        x1 = x1b[i % 5]
        kx1 = ("x1b", i % 5)
        op("dve", "scalar_tensor_tensor", ky + [ka, "g1"], ky, out=ybuf[:, :], in0=ybuf[:, :], scalar=rs1[:, 2 * i:2 * i + 1], in1=g1[:, :], op0=ALU.mult, op1=ALU.mult)
        op("dve", "tensor_tensor", ky + [("xo", i % 5)], [kx1], out=x1[:, :], in0=ybuf[:, :], in1=xo[i % 5][:, :], op=ALU.add)
        dma("sp", x1_scr[i * 128:(i + 1) * 128, :], x1[:, :], [kx1], [("x1s", i)])

    def mp5(i):
        op("act", "activation", [("x1b", i % 5)], [("sq2", 1), ("rs1", 2 * i + 1)], out=sq2s[1][:, :], in_=x1b[i % 5][:, :], func=AF.Square,
           accum_out=rs1[:, 2 * i + 1:2 * i + 2])

    def mp6(i):
        op("dve", "reduce_sum", [("sq2", 1)], [("rs1", 2 * i + 1)], out=rs1[:, 2 * i + 1:2 * i + 2], in_=sq2s[1][:, :], axis=AX.X)

    def mp7(i):
        kb_ = ("rs1", 2 * i + 1)
        op("act", "activation", [kb_, "eps_t"], [kb_], out=rs1[:, 2 * i + 1:2 * i + 2], in_=rs1[:, 2 * i + 1:2 * i + 2], func=AF.Ln, bias=eps_t[:, 0:1], scale=1.0 / D)
        op("act", "activation", [kb_], [kb_], out=rs1[:, 2 * i + 1:2 * i + 2], in_=rs1[:, 2 * i + 1:2 * i + 2], func=AF.Exp, scale=-0.5)

    def mp8(i):
        h2 = hb2[i % 2]
        kh2 = ("hb2", i % 2)
        op("dve", "scalar_tensor_tensor", [("x1b", i % 5), ("rs1", 2 * i + 1), "g2"], [kh2], out=h2[:, :], in0=x1b[i % 5][:, :],
           scalar=rs1[:, 2 * i + 1:2 * i + 2], in1=g2[:, :], op0=ALU.mult, op1=ALU.mult)
        for k in range(8):
            tr(psb[:, k * 128:(k + 1) * 128], h2[:, k * 128:(k + 1) * 128], ident_b[:, :], [kh2, "ident_b"], [PSB])

    def mp9(i):
        op("act", "activation", [PSB], [("h2T", i)], out=h2T[:, :, i * 128:(i + 1) * 128], in_=psb[:, :].rearrange("p (k t) -> p k t", k=8), func=AF.Copy)

    run_stages(OWN_T, [mp0, mp1, mp3, mp4, mp5, mp7, mp8, mp9])
    A.peak = max(getattr(A, "peak", 0), A.off)
    if A.off > WOUT_OFF:
        raise RuntimeError("post scope overlaps wout")
    S.barrier()
    A.release(mark_post)

    aT = A("aT", [128, NFC, 1024], BF16)
    wdn = A("wdn", [128, NFC, D], BF16)
    wub = [A(f"wub{i}", [128, 8, 256], BF16) for i in range(3)]
    cvb = [A(f"cvb{i}", [128, 512], F32) for i in range(6)]
    Gb = [A(f"Gb{i}", [128, 514], F32) for i in range(4)]
    git = [0]
    ggbs = [A(f"ggb{i}", [128, 512], F32) for i in range(3)]
    carry = A("carry", [128, 2 * NFC, 2], F32)
    cwf = A("cwf", [128, 2 * NFC, 3], F32)
    y2bs = [A(f"y2b{i}", [128, D], F32) for i in range(4)]
    x1r = [A(f"x1r{i}", [128, D], F32) for i in range(2)]
    sq3 = A("sq3", [128, D], F32)
    outb = [A(f"outb{i}", [128, D], F32) for i in range(2)]
    g3 = A("g3", [128, D], F32)
    rs3 = A("rs3", [128, 16], F32)
    dma("sp", cwf[:, :, :], cwf_d, [], ["cwf"])
    dma("sp", g3[:, :], gam[3], [], ["g3"])
    h2all = [("h2T", i) for i in range(OWN_T)]
    pairs = [(half, j) for half in range(2) for j in range(NFC)]

    def load_wub(pidx):
        half, j = pairs[pidx]
        wb = wub[pidx % 3]
        kwb = ("wub", pidx % 3)
        dma("pool", wb[:, :, :], w_up_d[j], [], [kwb])

    load_wub(0)
    load_wub(1)
    wdn_v = w_dn_d.rearrange("(j p) n -> p j n", p=128)
    wdn_loads = [(lambda j0=j0: dma("pool", wdn[:, j0:j0 + 2, :], wdn_v[:, j0:j0 + 2, :], [], [("wdn", j0 // 2)]))
                 for j0 in range(0, NFC, 2)]
    fit = [0]
    dn_ps = {}
    ffn_pend = []

    def dn0(ot):
        tb = ot % 8
        for hf in range(2):
            pi = next_ps(0, 6)
            dn_ps[(ot, hf)] = pi
            for j in range(NFC):
                mm(psf[pi][:, :], aT[:, j, tb * 128:(tb + 1) * 128], wdn[:, j, hf * 512:(hf + 1) * 512], j == 0, j == NFC - 1,
                   [("aT", j, tb // 4), ("wdn", j // 2)], [("ps", pi)])

    def dn1(ot):
        y2b = y2bs[ot % 4]
        for hf in range(2):
            pi = dn_ps[(ot, hf)]
            op("act", "activation", [("ps", pi)], [("y2b", ot % 4, hf)], out=y2b[:, hf * 512:(hf + 1) * 512], in_=psf[pi][:, :], func=AF.Copy)
        ky = [("y2b", ot % 4, 0), ("y2b", ot % 4, 1)]
        op("act", "activation", ky, ["sq3", ("rs3", ot)], out=sq3[:, :], in_=y2b[:, :], func=AF.Square, accum_out=rs3[:, ot:ot + 1])

    def dn2(ot):
        op("dve", "reduce_sum", ["sq3"], [("rs3", ot)], out=rs3[:, ot:ot + 1], in_=sq3[:, :], axis=AX.X)

    def dn3(ot):
        kr = ("rs3", ot)
        dma("sp", x1r[ot % 2][:, :], x1_scr[(1 + ot) * 128:(2 + ot) * 128, :], [("x1s", 1 + ot)], [("x1r", ot % 2)])
        op("act", "activation", [kr, "eps_t"], [kr], out=rs3[:, ot:ot + 1], in_=rs3[:, ot:ot + 1], func=AF.Ln, bias=eps_t[:, 0:1], scale=1.0 / D)
        op("act", "activation", [kr], [kr], out=rs3[:, ot:ot + 1], in_=rs3[:, ot:ot + 1], func=AF.Exp, scale=-0.5)

    def dn4(ot):
        y2b = y2bs[ot % 4]
        ky = [("y2b", ot % 4, 0), ("y2b", ot % 4, 1)]
        kr = ("rs3", ot)
        op("dve", "scalar_tensor_tensor", ky + [kr, "g3"], ky, out=y2b[:, :], in0=y2b[:, :], scalar=rs3[:, ot:ot + 1], in1=g3[:, :], op0=ALU.mult, op1=ALU.mult)
        ob_ = outb[ot % 2]
        op("dve", "tensor_tensor", ky + [("x1r", ot % 2)], [("outb", ot % 2)], out=ob_[:, :], in0=y2b[:, :], in1=x1r[ot % 2][:, :], op=ALU.add)
        dma("sp", out_d[ot * 128:(ot + 1) * 128, :], ob_[:, :], [("outb", ot % 2)], [("out", ot)])

    for pidx, (half, j) in enumerate(pairs):
        wb = wub[pidx % 3]
        kwb = ("wub", pidx % 3)
        if pidx + 2 < len(pairs):
            load_wub(pidx + 2)
        if half == 0 and pidx % 2 == 0 and wdn_loads:
            wdn_loads.pop(0)()
        if half == 0:
            for which in range(2):
                chn = j + NFC * which
                pi = next_ps(0, 6)
                for k in range(8):
                    mm(psf[pi][:, 0:2], wb[:, k, which * 128:(which + 1) * 128], h2T[:, k, 126:128], k == 0, k == 7, [kwb] + h2all, [("ps", pi)])
                op("dve", "tensor_copy", [("ps", pi)], [("carry", chn)], out=carry[:, chn, :], in_=psf[pi][:, 0:2])
        for t in range(2):
            tg = 2 * half + t
            tcols = slice(128 + tg * 512, 128 + (tg + 1) * 512)
            par = fit[0] % 3
            fit[0] += 1
            for which in range(2):
                chn = j + NFC * which
                pi = next_ps(0, 6)
                kp = ("ps", pi)
                for k in range(8):
                    mm(psf[pi][:, :], wb[:, k, which * 128:(which + 1) * 128], h2T[:, k, tcols], k == 0, k == 7, [kwb] + h2all, [kp])
                cv = cvb[which * 3 + par]
                kcv = ("cvb", which * 3 + par)
                kc = ("carry", chn)
                gi = which * 2 + (git[0] % 2)
                G = Gb[gi]
                kG = ("Gb", gi)
                op("act", "activation", [kc], [(kG, "h")], out=G[:, 0:2], in_=carry[:, chn, :], func=AF.Copy)
                op("act", "activation", [kp], [(kG, "m")], out=G[:, 2:514], in_=psf[pi][:, :], func=AF.Copy)
                op("act", "activation", [kp, "cwf"], [kcv], out=cv[:, :], in_=psf[pi][:, :], func=AF.Copy, scale=cwf[:, chn, 2:3])
                op("dve", "tensor_copy", [(kG, "m")], [kc], out=carry[:, chn, :], in_=G[:, 512:514])
                op("dve", "scalar_tensor_tensor", [(kG, "h"), (kG, "m"), "cwf", kcv], [kcv], out=cv[:, :], in0=G[:, 1:513], scalar=cwf[:, chn, 1:2],
                   in1=cv[:, :], op0=ALU.mult, op1=ALU.add)
                op("dve", "scalar_tensor_tensor", [(kG, "h"), (kG, "m"), "cwf", kcv], [kcv], out=cv[:, :], in0=G[:, 0:512], scalar=cwf[:, chn, 0:1],
                   in1=cv[:, :], op0=ALU.mult, op1=ALU.add)
            git[0] += 1
            def _gm(par=par, j=j, t=t):
                ggb = ggbs[par]
                op("act", "activation", [("cvb", 0 * 3 + par)], [("ggb", par)], out=ggb[:, :], in_=cvb[0 * 3 + par][:, :], func=AF.Gelu_apprx_tanh)
                op("dve", "tensor_tensor", [("ggb", par), ("cvb", 1 * 3 + par)], [("aT", j, t)], out=aT[:, j, t * 512:(t + 1) * 512], in0=ggb[:, :],
                   in1=cvb[1 * 3 + par][:, :], op=ALU.mult)
            if ffn_pend:
                ffn_pend.pop()()
            ffn_pend.append(_gm)
        if j == NFC - 1:
            if ffn_pend:
                ffn_pend.pop()()
            run_stages(8, [lambda tb, half=half: dn0(half * 8 + tb), lambda tb, half=half: dn1(half * 8 + tb),
                           lambda tb, half=half: dn3(half * 8 + tb),
                           lambda tb, half=half: dn4(half * 8 + tb)])
    A.peak = max(getattr(A, "peak", 0), A.off)
    S.barrier()
    n_ops = S.emit(nc)
    return nc, n_ops


def _t5_bucket(d):
    d = np.asarray(d)
    max_exact = 16
    df = np.maximum(d, 1).astype(np.float32)
    large = max_exact + (np.log(df / np.float32(max_exact)) / np.float32(math.log(128 / max_exact)) * np.float32(32 - max_exact)).astype(np.int32)
    return np.where(d < max_exact, d, np.minimum(large, 31)).astype(np.int64)


def _prep_common(inp):
    f = np.float32
    w_in = np.asarray(inp["w_in"][0], f)
    rel = np.asarray(inp["rel_bias"], f)
    c = {}
    c["gam"] = np.ascontiguousarray(np.stack([
        np.broadcast_to(np.asarray(inp[k][0], f)[None, :], (128, D))
        for k in ("norm_mix_pre", "norm_mix_post", "norm_ffn_pre", "norm_ffn_post")]))
    kc0, vc0 = 512, 640
    c["w_kvc"] = np.ascontiguousarray(np.concatenate(
        [w_in[:, kc0:kc0 + 64], w_in[:, vc0:vc0 + 64], w_in[:, kc0 + 64:kc0 + 128], w_in[:, vc0 + 64:vc0 + 128]], axis=1))
    wg = []
    for g in range(2):
        gcols = np.array([1280 + g * 12 + r * 3 + b for r in range(4) for b in range(3)])
        parts = [w_in[:, g * 256:(g + 1) * 256],
                 w_in[:, 768 + g * 64:768 + (g + 1) * 64],
                 w_in[:, 1024 + g * 64:1024 + (g + 1) * 64],
                 w_in[:, 896 + g * 64:896 + (g + 1) * 64],
                 w_in[:, 1152 + g * 64:1152 + (g + 1) * 64],
                 w_in[:, gcols],
                 w_in[:, 1816 + g * 256:1816 + (g + 1) * 256],
                 w_in[:, 2328 + g * 256:2328 + (g + 1) * 256],
                 w_in[:, 1304 + g * 256:1304 + (g + 1) * 256]]
        wg.append(np.concatenate(parts, axis=1))
    c["w_g"] = np.ascontiguousarray(np.stack(wg))
    w1k = np.asarray(inp["w_cmp_k1"][0], f).reshape(32, 64, 256).transpose(1, 0, 2)
    w1v = np.asarray(inp["w_cmp_v1"][0], f).reshape(32, 64, 256).transpose(1, 0, 2)
    c["w1kv"] = np.ascontiguousarray(np.concatenate([w1k, w1v], axis=0))
    c["pekv"] = np.ascontiguousarray(np.concatenate([np.asarray(inp["pe_cmp_k"][0], f).T, np.asarray(inp["pe_cmp_v"][0], f).T], axis=0))
    w2k = np.asarray(inp["w_cmp_k2"][0], f).reshape(2, 128, 64).transpose(1, 0, 2)
    w2v = np.asarray(inp["w_cmp_v2"][0], f).reshape(2, 128, 64).transpose(1, 0, 2)
    c["w2kv"] = np.ascontiguousarray(np.concatenate([w2k, w2v], axis=1))
    bias_d = rel[:, _t5_bucket(np.arange(0, 8192))]
    s_i = np.arange(128)[:, None]
    t_i = np.arange(128)[None, :]
    bt = np.empty((8, 128, 4, 128), f)
    d0 = t_i - s_i
    bt[:, :, 0, :] = np.where(d0 >= 0, bias_d[:, np.maximum(d0, 0)], f(NEG))
    bt[:, :, 1, :] = bias_d[:, 128 + d0]
    bt[:, :, 2, :] = rel[:, 31][:, None, None]
    bt[:, :, 3, :] = np.where((512 + d0) < 512, rel[:, 31][:, None, None], f(NEG))
    bt10 = bt[:, :, [0, 1, 2, 2, 2, 0, 1, 2, 2, 3], :]
    c["btab"] = np.ascontiguousarray(bt10.transpose(1, 0, 2, 3).reshape(128, 8, 1280))
    c["c31"] = np.ascontiguousarray(np.broadcast_to(rel[:, 31][None, :], (128, 8)))
    c["_bias_d"] = bias_d
    nn = np.arange(256)[:, None]
    jj = np.arange(64)[None, :]
    ov = np.clip(np.minimum(16 * nn + 32, 64 * jj + 64) - np.maximum(16 * nn, 64 * jj), 0, None).astype(f) / 32.0
    ov[255] = 0
    ovm = np.zeros((256, 65), f)
    ovm[:, :64] = ov
    ovm[:255, 64] = 1.0
    c["ovm1"] = np.ascontiguousarray(ovm.reshape(2, 128, 65).transpose(1, 0, 2))
    c["emat"] = np.ascontiguousarray((np.arange(S_LEN)[None, :] // 64 == np.arange(64)[:, None]).astype(f))
    c["ident"] = np.eye(128, dtype=f)
    c["cwm"] = np.ascontiguousarray(np.asarray(inp["conv_mix_w"][0], f).T.reshape(4, 128, 3).transpose(1, 0, 2))
    c["w_out"] = np.ascontiguousarray(np.asarray(inp["w_out"][0], f))
    wu = np.asarray(inp["w_ffn_up"][0], f)
    wu2 = np.stack([wu[:, :D_FF].reshape(8, 128, NFC, 128), wu[:, D_FF:].reshape(8, 128, NFC, 128)], axis=3)
    c["w_up"] = np.ascontiguousarray(wu2.transpose(2, 1, 0, 3, 4).reshape(NFC, 128, 8, 256))
    c["cwf"] = np.ascontiguousarray(np.asarray(inp["ffn_conv_w"][0], f).T.reshape(2 * NFC, 128, 3).transpose(1, 0, 2))
    c["w_dn"] = np.ascontiguousarray(np.asarray(inp["w_ffn_down"][0], f))
    return c


_NC_CACHE = {}


def _frame_tables(bias_d, shift):
    f = np.float32
    tl = np.arange(S_LEN)
    valid = (tl >= shift).astype(f)
    out = {"valid": np.ascontiguousarray(valid.reshape(NT, 128).T)}
    jj = np.arange(64)[None, :]
    cur = (tl // 64)[:, None]
    j0 = shift // 64
    okj = (jj >= j0) & (jj <= cur) & (tl[:, None] >= shift)
    forced = (jj == j0) | (jj == cur) | (jj == cur - 1)
    bsel = np.where(okj, np.where(forced, f(1e4), f(0)), f(NEGB)).astype(f)
    out["bsel"] = np.ascontiguousarray(bsel.reshape(NTG, 4, 128, 64).transpose(0, 2, 1, 3))
    n_i = np.arange(256)[:, None]
    dc = tl[None, :] - 16 * n_i - 31
    okn = (dc >= 0) & (n_i < 255) & (n_i >= shift // 16)
    ct = np.where(okn[None], bias_d[:, np.clip(dc, 0, 8191)], f(NEG)).astype(f)
    out["ctab"] = np.ascontiguousarray(ct.reshape(8, 2, 128, S_LEN))
    return out


def _in_maps(inputs):
    common = _prep_common(inputs)
    bias_d = common.pop("_bias_d")
    frames = [_frame_tables(bias_d, 2048), _frame_tables(bias_d, 0)]
    x = np.asarray(inputs["x"], np.float32)
    maps = []
    for core in range(8):
        b, cpos = core // 2, core % 2
        m = dict(common)
        m.update(frames[cpos])
        if cpos == 0:
            xf = np.zeros((S_LEN, D), np.float32)
            xf[2048:] = x[b, 0:2048]
        else:
            xf = np.ascontiguousarray(x[b])
        m["x_full"] = xf
        maps.append(m)
    return maps


def kernel(**inputs):
    if "nc" not in _NC_CACHE:
        _NC_CACHE["nc"] = build_nc(False)[0]
    nc = _NC_CACHE["nc"]
    maps = _in_maps(inputs)
    res = run_bass_kernel_spmd(nc, maps, core_ids=list(range(8)))
    out = np.empty((4, S_LEN, D), np.float32)
    for core in range(8):
        b, cpos = core // 2, core % 2
        out[b, 2048 * cpos:2048 * (cpos + 1)] = res.results[core]["out_own"]
    return out
```
